# Optimizing a Trainium2 kernel written in Bass

```python
import math
import jax, jax.numpy as jnp
from jax import lax
import numpy as np

D_MODEL = 1024
BATCH = 4
SEQ = 8192
DEPTH = 1
DEC_BATCH = 2
DEC_SEQ = 8192
PAST_LEN = 128

GRID_W = 64
HEAD_DIM = 64
HQ_A = 8
HKV_A = 2
GROUP_A = HQ_A // HKV_A
AX_DIM = HEAD_DIM // 2
AX_THETA = 10000.0
A_Q = HQ_A * HEAD_DIM
A_KV = HKV_A * HEAD_DIM
BR_A = A_Q
H_B = 4
B_QK = H_B * 2 * HEAD_DIM
B_V = H_B * 2 * HEAD_DIM
BR_B = B_V
ROT_DIM = HEAD_DIM // 4
ROPE_THETA = 500000.0
D_IN = A_Q + 2 * A_KV + BR_A + 2 * B_QK + B_V + BR_B + 2 * D_MODEL
Q_BLOCK = 128
EPS = 1e-6

kernel_name = "hybrid_gqa_axial_diffattn_encoder"


def rms_norm(x, g):
    x32 = x.astype(jnp.float32)
    y = x32 * lax.rsqrt(jnp.mean(x32 * x32, axis=-1, keepdims=True) + EPS)
    return (y * g.astype(jnp.float32)).astype(x.dtype)


def rope_tables(pos, dim, theta):
    inv = theta ** (-jnp.arange(0, dim, 2, dtype=jnp.float32) / dim)
    ang = pos.astype(jnp.float32)[:, None] * inv[None, :]
    ang = jnp.concatenate([ang, ang], axis=-1)
    return jnp.cos(ang), jnp.sin(ang)


def apply_rope(x, cos, sin):
    shape = (1, x.shape[1]) + (1,) * (x.ndim - 3) + (x.shape[-1],)
    cos = cos.reshape(shape).astype(x.dtype)
    sin = sin.reshape(shape).astype(x.dtype)
    half = x.shape[-1] // 2
    x1, x2 = x[..., :half], x[..., half:]
    rot = jnp.concatenate([-x2, x1], axis=-1)
    return x * cos + rot * sin


def gqa_attention(q, k, v):
    B, S = q.shape[0], q.shape[1]
    nb = S // Q_BLOCK
    scale = 1.0 / math.sqrt(HEAD_DIM)
    qb = q.reshape(B, nb, Q_BLOCK, HKV_A, GROUP_A, HEAD_DIM).transpose(1, 0, 2, 3, 4, 5)

    def one(qblk):
        s = jnp.einsum('bqgrd,bkgd->bgrqk', qblk, k).astype(jnp.float32) * scale
        p = jax.nn.softmax(s, axis=-1).astype(v.dtype)
        return jnp.einsum('bgrqk,bkgd->bqgrd', p, v)

    o = lax.map(one, qb)
    return o.transpose(1, 0, 2, 3, 4, 5).reshape(B, S, HQ_A * HEAD_DIM)


def diff_attention(q, k, v, lam):
    B, S = q.shape[0], q.shape[1]
    nb = S // Q_BLOCK
    scale = 1.0 / math.sqrt(HEAD_DIM)
    qb = q.reshape(B, nb, Q_BLOCK, H_B, 2, HEAD_DIM).transpose(1, 0, 2, 3, 4, 5)

    def one(qblk):
        s = jnp.einsum('bqhmd,bkhmd->bhmqk', qblk, k).astype(jnp.float32) * scale
        p = jax.nn.softmax(s, axis=-1)
        pd = (p[:, :, 0] - lam * p[:, :, 1]).astype(v.dtype)
        return jnp.einsum('bhqk,bkhe->bqhe', pd, v)

    o = lax.map(one, qb)
    return o.transpose(1, 0, 2, 3, 4).reshape(B, S, H_B, 2 * HEAD_DIM)


def encoder_layer(x, c, layer_idx, w_ada, b_ada, norm_g, w_in, qn_a, kn_a, qn_b, kn_b,
                  lam_q1, lam_k1, lam_q2, lam_k2, subln_g, w_proj_a, w_proj_b, w_out):
    B, S, _ = x.shape
    rows = S // GRID_W
    mod = jnp.einsum('bd,de->be', jax.nn.silu(c), w_ada) + b_ada
    shift, scale, gate = jnp.split(mod, 3, axis=-1)
    h = rms_norm(x, norm_g) * (1.0 + scale[:, None, :]) + shift[:, None, :]

    proj = jnp.einsum('bsd,de->bse', h, w_in)
    o1 = A_Q
    o2 = o1 + A_KV
    o3 = o2 + A_KV
    o4 = o3 + BR_A
    o5 = o4 + B_QK
    o6 = o5 + B_QK
    o7 = o6 + B_V
    o8 = o7 + BR_B
    o9 = o8 + D_MODEL
    qa, ka, va, za, qb, kb, vb, zb, ga, gb = jnp.split(proj, [o1, o2, o3, o4, o5, o6, o7, o8, o9], axis=-1)

    qa = rms_norm(qa.reshape(B, S, HQ_A, HEAD_DIM), qn_a)
    ka = rms_norm(ka.reshape(B, S, HKV_A, HEAD_DIM), kn_a)
    va = va.reshape(B, S, HKV_A, HEAD_DIM)
    row = jnp.repeat(jnp.arange(rows), GRID_W)
    col = jnp.tile(jnp.arange(GRID_W), rows)
    cos_r, sin_r = rope_tables(row, AX_DIM, AX_THETA)
    cos_c, sin_c = rope_tables(col, AX_DIM, AX_THETA)
    qa = jnp.concatenate([apply_rope(qa[..., :AX_DIM], cos_r, sin_r),
                          apply_rope(qa[..., AX_DIM:], cos_c, sin_c)], axis=-1)
    ka = jnp.concatenate([apply_rope(ka[..., :AX_DIM], cos_r, sin_r),
                          apply_rope(ka[..., AX_DIM:], cos_c, sin_c)], axis=-1)
    oa = gqa_attention(qa, ka, va) * jax.nn.silu(za)

    qb = rms_norm(qb.reshape(B, S, H_B, 2, HEAD_DIM), qn_b)
    kb = rms_norm(kb.reshape(B, S, H_B, 2, HEAD_DIM), kn_b)
    vb = vb.reshape(B, S, H_B, 2 * HEAD_DIM)
    cos_p, sin_p = rope_tables(jnp.arange(S), ROT_DIM, ROPE_THETA)
    qb = jnp.concatenate([apply_rope(qb[..., :ROT_DIM], cos_p, sin_p), qb[..., ROT_DIM:]], axis=-1)
    kb = jnp.concatenate([apply_rope(kb[..., :ROT_DIM], cos_p, sin_p), kb[..., ROT_DIM:]], axis=-1)
    lam_init = 0.8 - 0.6 * math.exp(-0.3 * layer_idx)
    lam = (jnp.exp(jnp.sum(lam_q1.astype(jnp.float32) * lam_k1.astype(jnp.float32)))
           - jnp.exp(jnp.sum(lam_q2.astype(jnp.float32) * lam_k2.astype(jnp.float32)))
           + lam_init)
    ob = diff_attention(qb, kb, vb, lam)
    ob = rms_norm(ob, subln_g) * (1.0 - lam_init)
    ob = ob.reshape(B, S, BR_B) * jax.nn.silu(zb)

    pa = jnp.einsum('bse,ed->bsd', oa, w_proj_a)
    pb = jnp.einsum('bse,ed->bsd', ob, w_proj_b)
    merged = jax.nn.sigmoid(ga) * pa + jax.nn.sigmoid(gb) * pb
    out = jnp.einsum('bsd,de->bse', merged, w_out)
    return x + gate[:, None, :] * out


def setup_inputs(seed: int = 0) -> dict:
    key = jax.random.key(seed)
    ks = jax.random.split(key, 20)
    f32 = jnp.float32
    D = D_MODEL

    def nrm(k, shape, s):
        return jax.random.normal(k, shape, f32) * s

    return {
        "x_prompt": nrm(ks[0], (BATCH, SEQ, D), 1.0),
        "x_sample": nrm(ks[1], (DEC_BATCH, DEC_SEQ, D), 1.0),
        "c_prompt": nrm(ks[2], (BATCH, D), 1.0),
        "c_sample": nrm(ks[3], (DEC_BATCH, D), 1.0),
        "w_ada": nrm(ks[4], (DEPTH, D, 3 * D), 0.5 * D ** -0.5),
        "b_ada": nrm(ks[5], (DEPTH, 3 * D), 0.01),
        "norm_g": 1.0 + nrm(ks[6], (DEPTH, D), 0.02),
        "w_in": nrm(ks[7], (DEPTH, D, D_IN), D ** -0.5),
        "qn_a": 1.0 + nrm(ks[8], (DEPTH, HEAD_DIM), 0.02),
        "kn_a": 1.0 + nrm(ks[9], (DEPTH, HEAD_DIM), 0.02),
        "qn_b": 1.0 + nrm(ks[10], (DEPTH, HEAD_DIM), 0.02),
        "kn_b": 1.0 + nrm(ks[11], (DEPTH, HEAD_DIM), 0.02),
        "lam_q1": nrm(ks[12], (DEPTH, HEAD_DIM), 0.1),
        "lam_k1": nrm(ks[13], (DEPTH, HEAD_DIM), 0.1),
        "lam_q2": nrm(ks[14], (DEPTH, HEAD_DIM), 0.1),
        "lam_k2": nrm(ks[15], (DEPTH, HEAD_DIM), 0.1),
        "subln_g": 1.0 + nrm(ks[16], (DEPTH, 2 * HEAD_DIM), 0.02),
        "w_proj_a": nrm(ks[17], (DEPTH, BR_A, D), BR_A ** -0.5),
        "w_proj_b": nrm(ks[18], (DEPTH, BR_B, D), BR_B ** -0.5),
        "w_out": nrm(ks[19], (DEPTH, D, D), D ** -0.5),
    }


def reference(x_prompt, x_sample, c_prompt, c_sample, w_ada, b_ada, norm_g, w_in,
              qn_a, kn_a, qn_b, kn_b, lam_q1, lam_k1, lam_q2, lam_k2, subln_g,
              w_proj_a, w_proj_b, w_out):
    yp = x_prompt
    ys = x_sample
    for i in range(DEPTH):
        args = (w_ada[i], b_ada[i], norm_g[i], w_in[i], qn_a[i], kn_a[i], qn_b[i], kn_b[i],
                lam_q1[i], lam_k1[i], lam_q2[i], lam_k2[i], subln_g[i],
                w_proj_a[i], w_proj_b[i], w_out[i])
        yp = encoder_layer(yp, c_prompt, i, *args)
        ys = encoder_layer(ys, c_sample, i, *args)
    y_prompt = yp
    y_sample = ys
    return (y_prompt, y_sample)
```

```python
import math
import os
from contextlib import ExitStack

import numpy as np
import concourse.bass as bass
import concourse.mybir as mybir
from concourse.bass_utils import run_bass_kernel_spmd

F32 = mybir.dt.float32
BF16 = mybir.dt.bfloat16
I32 = mybir.dt.int32
AF = mybir.ActivationFunctionType
ALU = mybir.AluOpType
AX = mybir.AxisListType

D = 1024
S = 8192
NQ = 6144
NCORES = 8
EPS = 1e-6
LAM_INIT = 0.8 - 0.6 * math.exp(-0.3 * 0)
NT_K = S // 128
NT_Q = NQ // 128
NTAB = NT_K + NT_Q
TWO_PI = 2.0 * math.pi
CW1 = 6.28125
CW2 = TWO_PI - CW1


class Buf:
    __slots__ = ("w", "r", "const", "multi", "ws")

    def __init__(self, const=False, multi=False):
        self.w = None
        self.r = []
        self.const = const
        self.multi = multi
        self.ws = {}


class Prog:
    ENGS = ("sync", "scalar", "vector", "gpsimd", "tensor")

    def __init__(self, nc, es):
        self.nc = nc
        self.es = es
        self.ops = []
        self.sem = {e: es.enter_context(nc.semaphore("s_" + e)) for e in self.ENGS}
        self.dsems = []
        self.last = {e: None for e in self.ENGS}

    def dsem(self, name):
        s = self.es.enter_context(self.nc.semaphore(name))
        d = {"sem": s, "count": 0, "last": None}
        self.dsems.append(d)
        return d

    def raw(self, eng, fn, deps, dsem=None):
        o = {"eng": eng, "fn": fn, "deps": [d for d in deps if d is not None], "sig": False,
             "dma": None, "val": None}
        if dsem is not None:
            dsem["count"] += 16
            o["dma"] = (dsem, dsem["count"])
            dsem["last"] = o
        self.ops.append(o)
        self.last[eng] = o
        return o

    def do(self, eng, fn, reads=(), writes=(), dsem=None, extra=()):
        deps = list(extra)
        for b in reads:
            if b.multi:
                deps.extend(b.ws.values())
            elif b.w is not None:
                deps.append(b.w)
        for b in writes:
            if b.multi:
                pass
            else:
                if b.w is not None and not (b.w["dma"] is None and b.w["eng"] == eng and dsem is None):
                    deps.append(b.w)
                deps.extend(b.r)
        o = self.raw(eng, fn, deps, dsem)
        for b in reads:
            if not b.const:
                b.r.append(o)
        for b in writes:
            if b.multi:
                b.ws[id(dsem)] = o
            else:
                b.w = o
                b.r = []
        return o

    def barrier(self):
        deps = [o for o in self.last.values() if o is not None]
        deps += [d["last"] for d in self.dsems if d["last"] is not None]
        for e in self.ENGS:
            self.raw(e, lambda eng: eng.nop(), deps)

    def emit(self):
        nc = self.nc
        for o in self.ops:
            for d in o["deps"]:
                if d["dma"] is None and not (d["eng"] == "tensor" and o["eng"] == "tensor"):
                    d["sig"] = True
        cnt = {e: 0 for e in self.ENGS}
        for o in self.ops:
            if o["dma"] is None and o["sig"]:
                cnt[o["eng"]] += 1
                o["val"] = cnt[o["eng"]]
        by_eng = {e: [o for o in self.ops if o["eng"] == e] for e in self.ENGS}
        sem = self.sem
        with nc.Block() as block:
            def mk(e):
                def body(eng):
                    waited = {}
                    for o in by_eng[e]:
                        need = {}
                        for d in o["deps"]:
                            if d["dma"] is not None:
                                key, s, v = id(d["dma"][0]), d["dma"][0]["sem"], d["dma"][1]
                            else:
                                if d["eng"] == "tensor" and e == "tensor":
                                    continue
                                key, s, v = d["eng"], sem[d["eng"]], d["val"]
                            if v > need.get(key, (None, 0))[1]:
                                need[key] = (s, v)
                        for key, (s, v) in need.items():
                            if waited.get(key, 0) >= v:
                                continue
                            eng.wait_ge(s, v)
                            waited[key] = v
                        ins = o["fn"](eng)
                        if o["dma"] is not None:
                            ins.then_inc(o["dma"][0]["sem"], 16)
                        elif o["sig"]:
                            ins.then_inc(sem[e], 1)
                return body
            for e in self.ENGS:
                if by_eng[e]:
                    getattr(block, e)(mk(e))


def build_program(debug=False, stop=99):
    nc = bass.Bass("TRN2", target_bir_lowering=False)

    def din(name, shape, dt=F32):
        return nc.dram_tensor(name, list(shape), dt, kind="ExternalInput").ap()

    xa_d = din("xa", [S, D])
    xb_d = din("xb", [S, D])
    xq_d = din("xq", [NQ, D])
    cT_d = din("cT", [128, 8, 2])
    wada_d = din("w_ada", [D, 3 * D])
    badaT_d = din("b_adaT", [128, 24])
    brow_d = din("b_row", [1, 3 * D])
    gb_d = din("gb", [128, D])
    gq_d = din("gq", [128, 16, 64])
    gk_d = din("gk", [128, 10, 64])
    lamv_d = din("lamv", [128, 4, 64])
    sg_d = din("sg", [128, 1])
    pos_d = din("pos", [128, NTAB])
    jidx_d = din("jidx", [128, 24])
    ident_d = din("ident", [128, 128])
    sel_d = din("selc", [128, 384])
    wkv_d = din("w_kv", [D, 1280])
    wq_d = din("w_q", [D, 1024])
    wz_d = din("w_z", [D, 1024])
    wg_d = din("w_g", [D, 2048])
    wpa_d = din("w_pa", [512, D])
    wpb_d = din("w_pb", [512, D])
    wout_d = din("w_out", [D, D])
    y_d = nc.dram_tensor("y", [NQ, D], F32, kind="ExternalOutput").ap()
    skind = "ExternalOutput" if debug else "Internal"
    KT_d = [nc.dram_tensor("KT%d" % s, [640, S], BF16, kind=skind).ap() for s in range(2)]
    VS_d = [nc.dram_tensor("VS%d" % s, [S, 640], BF16, kind=skind).ap() for s in range(2)]
    QT_d = nc.dram_tensor("QT", [1024, NQ], BF16, kind=skind).ap()
    OT_d = nc.dram_tensor("OT", [1024, NQ], BF16, kind=skind).ap()
    x_srcs = [xa_d, xb_d]

    with ExitStack() as es:
        P = Prog(nc, es)

        def sbt(scope, name, shape, dt):
            return scope.enter_context(nc.sbuf_tensor("sb_" + name, list(shape), dt))

        def pst(scope, name, shape, dt):
            return scope.enter_context(nc.psum_tensor("ps_" + name, list(shape), dt))

        dbg_sem = P.dsem("dbg") if debug else None

        def dbg(name, ap, shape, bufs, dt=F32):
            if not debug:
                return
            o = nc.dram_tensor("dbg_" + name, list(shape), dt, kind="ExternalOutput").ap()
            P.do("sync", lambda e: e.dma_start(out=o, in_=ap), reads=bufs, dsem=dbg_sem)

        KT_b = [Buf(multi=True) for _ in range(2)]
        VS_b = [Buf(multi=True) for _ in range(2)]
        QT_b = Buf(multi=True)
        OT_b = Buf(multi=True)
        Y_b = Buf(multi=True)

        ident = sbt(es, "ident", [128, 128], BF16)
        ones_b = sbt(es, "ones_b", [128, 128], BF16)
        ones_f = sbt(es, "ones_f", [128, 128], F32)
        gbt = sbt(es, "gbt", [128, D], F32)
        modT = sbt(es, "modT", [128, 24, 2], F32)
        sc1 = sbt(es, "sc1", [128, 8, 2], F32)
        gate_bc = sbt(es, "gate_bc", [128, 2, D], F32)
        neglam = sbt(es, "neglam", [128, 1], F32)
        sg8 = sbt(es, "sg8", [128, 1], F32)
        epsc = sbt(es, "epsc", [128, 1], F32)
        sel = sbt(es, "sel", [128, 384], F32)
        sel_b = sbt(es, "sel_b", [128, 384], BF16)
        rstd_own = sbt(es, "rstd_own", [128, NT_Q], F32)
        rownB = Buf()
        cB = Buf(const=True)

        def load_weight(ctx, dst, dst_buf, w_ap, nchunk, ncols, engines=("vector", "gpsimd", "scalar"), defer=None):
            for c in range(nchunk):
                for c0 in range(0, ncols, 1024):
                    if defer is not None:
                        defer.append(lambda c=c, c0=c0: load_weight_piece(ctx, dst, dst_buf, w_ap, ncols, engines, c, c0))
                    else:
                        load_weight_piece(ctx, dst, dst_buf, w_ap, ncols, engines, c, c0)

        def load_weight_piece(ctx, dst, dst_buf, w_ap, ncols, engines, c, c0):
            if True:
                if True:
                    cw = min(1024, ncols - c0)
                    i = ctx["wn"]
                    ctx["wn"] += 1
                    st, stb, std = ctx["wst"][i % len(ctx["wst"])]
                    P.do("sync", lambda e, st=st, c=c, c0=c0, cw=cw: e.dma_start(out=st[:, 0:cw], in_=w_ap[c * 128:(c + 1) * 128, c0:c0 + cw]),
                         writes=[stb], dsem=std)
                    eng = engines[i % len(engines)]
                    if eng == "scalar":
                        P.do(eng, lambda e, st=st, c=c, c0=c0, cw=cw: e.activation(out=dst[:, c, c0:c0 + cw], in_=st[:, 0:cw], func=AF.Copy),
                             reads=[stb], writes=[dst_buf])
                    else:
                        P.do(eng, lambda e, st=st, c=c, c0=c0, cw=cw: e.tensor_copy(out=dst[:, c, c0:c0 + cw], in_=st[:, 0:cw]),
                             reads=[stb], writes=[dst_buf])

        es1 = ExitStack()
        wkv = sbt(es1, "wkv", [128, 8, 1280], BF16)
        wq = sbt(es1, "wq", [128, 8, 1024], BF16)
        wkvB, wqB = Buf(), Buf()
        ctx1 = {"n": 0, "wn": 0}
        ctx1["wst"] = [(sbt(es1, "wst%d" % i, [128, 1024], F32), Buf(), P.dsem("dwst%d" % i)) for i in range(3)]

        with ExitStack() as s0:
            wa = sbt(s0, "wa", [128, 8, 3 * D], F32)
            idf = sbt(s0, "idf", [128, 128], F32)
            cT = sbt(s0, "cT", [128, 8, 2], F32)
            ec = sbt(s0, "ec", [128, 8, 2], F32)
            scT = sbt(s0, "scT", [128, 8, 2], F32)
            grep = [(sbt(s0, "grep%d" % i, [128, 128], F32), Buf()) for i in range(2)]
            badaT = sbt(s0, "badaT", [128, 24], F32)
            brow = sbt(s0, "brow", [1, 3 * D], F32)
            lamv = sbt(s0, "lamv", [128, 4, 64], F32)
            lprod = sbt(s0, "lprod", [128, 2, 64], F32)
            lsum = sbt(s0, "lsum", [128, 2], F32)
            lexp = sbt(s0, "lexp", [128, 2], F32)
            sgt = sbt(s0, "sgt", [128, 1], F32)
            mps = pst(s0, "mps", [128, 24, 2], F32)
            gps = pst(s0, "gps", [128, 2, 512], F32)
            b0 = {k: Buf() for k in ("wa", "idf", "cT", "ec", "scT", "screp", "badaT", "brow", "lamv",
                                     "lprod", "lsum", "lexp", "sgt", "mps", "gps0", "gps1",
                                     "ident", "ones", "gbt", "modT", "sc1", "gate", "neglam", "sg8", "epsc")}
            ds0 = [P.dsem("ds0_%d" % i) for i in range(4)]
            for c in range(8):
                P.do("sync", lambda e, c=c: e.dma_start(out=wa[:, c, :], in_=wada_d[c * 128:(c + 1) * 128, :]),
                     writes=[], dsem=ds0[0])
            b0["wa"].w = ds0[0]["last"]
            P.do("sync", lambda e: e.dma_start(out=idf[:], in_=ident_d), writes=[b0["idf"]], dsem=ds0[1])
            P.do("sync", lambda e: e.dma_start(out=cT[:], in_=cT_d), writes=[b0["cT"]], dsem=ds0[1])
            P.do("sync", lambda e: e.dma_start(out=badaT[:], in_=badaT_d), writes=[b0["badaT"]], dsem=ds0[1])
            P.do("sync", lambda e: e.dma_start(out=brow[:], in_=brow_d), writes=[b0["brow"]], dsem=ds0[1])
            P.do("sync", lambda e: e.dma_start(out=lamv[:], in_=lamv_d), writes=[b0["lamv"]], dsem=ds0[1])
            P.do("sync", lambda e: e.dma_start(out=sgt[:], in_=sg_d), writes=[b0["sgt"]], dsem=ds0[1])
            P.do("sync", lambda e: e.dma_start(out=gbt[:], in_=gb_d), writes=[b0["gbt"]], dsem=ds0[1])
            P.do("sync", lambda e: e.dma_start(out=sel[:], in_=sel_d), writes=[b0["gbt"]], dsem=ds0[1])
            small_last = ds0[1]["last"]
            for k in ("idf", "cT", "badaT", "brow", "lamv", "sgt", "gbt"):
                b0[k].w = small_last
            load_weight(ctx1, wkv, wkvB, wkv_d, 8, 1280, engines=("gpsimd",))
            load_weight(ctx1, wq, wqB, wq_d, 8, 1024, engines=("gpsimd",))
            P.do("vector", lambda e: e.tensor_copy(out=ident[:], in_=idf[:]), reads=[b0["idf"]], writes=[b0["ident"]])
            P.do("vector", lambda e: e.tensor_copy(out=sel_b[:], in_=sel[:]), reads=[b0["gbt"]], writes=[b0["ident"]])
            P.do("gpsimd", lambda e: e.memset(ones_b[:], 1.0), writes=[b0["ones"]])
            P.do("gpsimd", lambda e: e.memset(ones_f[:], 1.0), writes=[b0["ones"]])
            P.do("gpsimd", lambda e: e.memset(epsc[:], EPS), writes=[b0["epsc"]])
            if stop <= -3:
                P.barrier(); P.emit(); return nc
            P.do("scalar", lambda e: e.activation(out=ec[:], in_=cT[:], func=AF.Exp, scale=-1.0),
                 reads=[b0["cT"]], writes=[b0["ec"]])
            P.do("vector", lambda e: e.tensor_scalar(out=ec[:], in0=ec[:], scalar1=1.0, scalar2=None, op0=ALU.add),
                 reads=[b0["ec"]], writes=[b0["ec"]])
            P.do("vector", lambda e: e.reciprocal(out=ec[:], in_=ec[:]), reads=[b0["ec"]], writes=[b0["ec"]])
            P.do("vector", lambda e: e.tensor_tensor(out=scT[:], in0=cT[:], in1=ec[:], op=ALU.mult),
                 reads=[b0["ec"], b0["cT"]], writes=[b0["scT"]])
            if stop <= -2:
                P.barrier(); P.emit(); return nc
            for j in range(24):
                for c in range(8):
                    P.do("tensor", lambda e, j=j, c=c: e.matmul(mps[:, j, :], lhsT=wa[:, c, j * 128:(j + 1) * 128],
                                                              rhs=scT[:, c, :], start=(c == 0), stop=(c == 7)),
                         reads=[b0["wa"], b0["scT"]], writes=[b0["mps"]])
            P.do("vector", lambda e: e.tensor_tensor(out=modT[:], in0=mps[:],
                                                     in1=badaT[:].unsqueeze(2).to_broadcast([128, 24, 2]), op=ALU.add),
                 reads=[b0["mps"], b0["badaT"]], writes=[b0["modT"]])
            P.do("vector", lambda e: e.tensor_scalar(out=sc1[:], in0=modT[:, 8:16, :], scalar1=1.0, scalar2=None, op0=ALU.add),
                 reads=[b0["modT"]], writes=[b0["sc1"]])
            if stop <= -1:
                P.barrier(); P.emit(); return nc
            for s in range(2):
                for hb in range(2):
                    gk_ = "gps%d" % hb
                    for jj in range(4):
                        j = hb * 4 + jj
                        rp_, rpb_ = grep[j % 2]
                        P.do("vector", lambda e, rp_=rp_, j=j, s=s: e.tensor_copy(out=rp_[:], in_=modT[:, 16 + j, s:s + 1].to_broadcast([128, 128])),
                             reads=[b0["modT"]], writes=[rpb_])
                        P.do("tensor", lambda e, rp_=rp_, hb=hb, jj=jj: e.transpose(out=gps[:, hb, jj * 128:(jj + 1) * 128], in_=rp_[:], identity=idf[:]),
                             reads=[rpb_, b0["idf"]], writes=[b0[gk_]])
                    P.do("vector", lambda e, s=s, hb=hb: e.tensor_copy(out=gate_bc[:, s, hb * 512:(hb + 1) * 512], in_=gps[:, hb, :]),
                         reads=[b0[gk_]], writes=[b0["gate"]])
            if stop <= -0.5:
                P.barrier(); P.emit(); return nc
            P.do("vector", lambda e: e.tensor_tensor(out=lprod[:, 0, :], in0=lamv[:, 0, :], in1=lamv[:, 1, :], op=ALU.mult),
                 reads=[b0["lamv"]], writes=[b0["lprod"]])
            P.do("vector", lambda e: e.tensor_tensor(out=lprod[:, 1, :], in0=lamv[:, 2, :], in1=lamv[:, 3, :], op=ALU.mult),
                 reads=[b0["lamv"], b0["lprod"]], writes=[b0["lprod"]])
            P.do("vector", lambda e: e.tensor_reduce(out=lsum[:], in_=lprod[:], axis=AX.X, op=ALU.add),
                 reads=[b0["lprod"]], writes=[b0["lsum"]])
            P.do("scalar", lambda e: e.activation(out=lexp[:], in_=lsum[:], func=AF.Exp),
                 reads=[b0["lsum"]], writes=[b0["lexp"]])
            P.do("vector", lambda e: e.scalar_tensor_tensor(out=neglam[:], in0=lexp[:, 1:2], scalar=-LAM_INIT, in1=lexp[:, 0:1],
                                                            op0=ALU.add, op1=ALU.subtract),
                 reads=[b0["lexp"]], writes=[b0["neglam"]])
            P.do("vector", lambda e: e.tensor_scalar(out=sg8[:], in0=sgt[:], scalar1=1.0 - LAM_INIT, scalar2=None, op0=ALU.mult),
                 reads=[b0["sgt"]], writes=[b0["sg8"]])
            dbg("modT", modT[:], [128, 24, 2], [b0["modT"]])
            dbg("sc1", sc1[:], [128, 8, 2], [b0["sc1"]])
            dbg("gate", gate_bc[:], [128, 2, D], [b0["gate"]])
            pass
            P.barrier()
        if stop <= 0:
            P.emit()
            return nc

        def front_end(sc, ctx, x_ap, slot_idx, mslot, hT_dst, hT_buf, xt_keep=None):
            i = ctx["n"]
            ctx["n"] += 1
            xs = ctx["xt"][i % len(ctx["xt"])] if xt_keep is None else xt_keep
            xt, xb_, xd = xs
            P.do("sync", lambda e: e.dma_start(out=xt[:], in_=x_ap), writes=[xb_], dsem=xd)
            k = i % 2
            FE = int(os.environ.get("PH1_FE", "99"))
            ss, ssb = ctx["ss"][k]
            if FE < 1:
                return
            P.do("scalar", lambda e: e.activation(out=ctx["junk"][:], in_=xt[:], func=AF.Square, accum_out=ss[:]),
                 reads=[xb_], writes=[ctx["junkb"], ssb])
            if FE < 2:
                return
            P.do("scalar", lambda e: e.activation(out=ss[:], in_=ss[:], func=AF.Ln, scale=1.0 / D, bias=epsc[:, 0:1]),
                 reads=[ssb, cB], writes=[ssb])
            P.do("scalar", lambda e: e.activation(out=ss[:], in_=ss[:], func=AF.Exp, scale=-0.5), reads=[ssb], writes=[ssb])
            if FE < 3:
                return
            xn, xnb = ctx["xn"][k]
            P.do("vector", lambda e: e.scalar_tensor_tensor(out=xn[:], in0=xt[:], scalar=ss[:, 0:1], in1=gbt[:],
                                                            op0=ALU.mult, op1=ALU.mult),
                 reads=[xb_, ssb, cB], writes=[xnb])
            if FE < 4:
                return
            pT, pTb = ctx["psT"][k]
            for c in range(8):
                P.do("tensor", lambda e, c=c: e.transpose(out=pT[:, c, :], in_=xn[:, c * 128:(c + 1) * 128], identity=ident[:]),
                     reads=[xnb, cB], writes=[pTb])
            if FE < 5:
                return
            EV = os.environ.get("PH1_EV", "act")
            for c in range(8):
                if (c % 2 == 0 and EV == "both") or EV == "act":
                    P.do("scalar", lambda e, c=c: e.activation(out=hT_dst(c), in_=pT[:, c, :], func=AF.Identity,
                                                               scale=sc1[:, c, mslot:mslot + 1], bias=modT[:, c, mslot:mslot + 1]),
                         reads=[pTb, cB], writes=[hT_buf[0]])
                else:
                    P.do("vector", lambda e, c=c: e.tensor_scalar(out=hT_dst(c), in0=pT[:, c, :],
                                                                  scalar1=sc1[:, c, mslot:mslot + 1], scalar2=modT[:, c, mslot:mslot + 1],
                                                                  op0=ALU.mult, op1=ALU.add),
                         reads=[pTb, cB], writes=[hT_buf[1]])

        with ExitStack() as s1:
            sinT = sbt(s1, "sinT", [128, NTAB, 40], F32)
            cosT = sbt(s1, "cosT", [128, NTAB, 40], F32)
            gq = sbt(s1, "gq", [128, 16, 64], F32)
            gk = sbt(s1, "gk", [128, 10, 64], F32)
            tabB = Buf()
            with ExitStack() as sr:
                pos = sbt(sr, "pos", [128, NTAB], F32)
                jidx = sbt(sr, "jidx", [128, 24], F32)
                inv = sbt(sr, "inv", [128, 24], F32)
                posi = sbt(sr, "posi", [128, NTAB], I32)
                rowf = sbt(sr, "rowf", [128, NTAB], F32)
                colf = sbt(sr, "colf", [128, NTAB], F32)
                ang = sbt(sr, "ang", [128, NTAB, 40], F32)
                tq = sbt(sr, "tq", [128, NTAB, 40], F32)
                ki = sbt(sr, "ki", [128, NTAB, 40], I32)
                kf = sbt(sr, "kf", [128, NTAB, 40], F32)
                yy = sbt(sr, "yy", [128, NTAB, 40], F32)
                cc = tq
                rb = Buf()
                dsr = P.dsem("dsr")
                P.do("sync", lambda e: e.dma_start(out=pos[:], in_=pos_d), writes=[rb], dsem=dsr)
                P.do("sync", lambda e: e.dma_start(out=inv[:], in_=jidx_d), writes=[rb], dsem=dsr)
                P.do("sync", lambda e: e.dma_start(out=gq[:], in_=gq_d), writes=[rb], dsem=dsr)
                P.do("sync", lambda e: e.dma_start(out=gk[:], in_=gk_d), writes=[rb], dsem=dsr)
                V = lambda fn: P.do("vector", fn, reads=[rb], writes=[rb])
                A = lambda fn: P.do("scalar", fn, reads=[rb], writes=[rb])
                V(lambda e: e.tensor_scalar(out=posi[:], in0=pos[:], scalar1=1.0 / 64.0, scalar2=None, op0=ALU.mult))
                V(lambda e: e.tensor_copy(out=rowf[:], in_=posi[:]))
                V(lambda e: e.scalar_tensor_tensor(out=colf[:], in0=rowf[:], scalar=-64.0, in1=pos[:], op0=ALU.mult, op1=ALU.add))
                V(lambda e: e.tensor_scalar(out=tq[:, :, 0], in0=colf[:], scalar1=0.0, scalar2=None, op0=ALU.is_lt))
                V(lambda e: e.tensor_tensor(out=rowf[:], in0=rowf[:], in1=tq[:, :, 0], op=ALU.subtract))
                V(lambda e: e.scalar_tensor_tensor(out=colf[:], in0=tq[:, :, 0], scalar=64.0, in1=colf[:], op0=ALU.mult, op1=ALU.add))
                V(lambda e: e.tensor_tensor(out=ang[:, :, 0:16], in0=rowf[:].unsqueeze(2).to_broadcast([128, NTAB, 16]),
                                            in1=inv[:, 0:16].unsqueeze(1).to_broadcast([128, NTAB, 16]), op=ALU.mult))
                V(lambda e: e.tensor_tensor(out=ang[:, :, 16:32], in0=colf[:].unsqueeze(2).to_broadcast([128, NTAB, 16]),
                                            in1=inv[:, 0:16].unsqueeze(1).to_broadcast([128, NTAB, 16]), op=ALU.mult))
                V(lambda e: e.tensor_tensor(out=ang[:, :, 32:40], in0=pos[:].unsqueeze(2).to_broadcast([128, NTAB, 8]),
                                            in1=inv[:, 16:24].unsqueeze(1).to_broadcast([128, NTAB, 8]), op=ALU.mult))
                for (dst, shift) in ((sinT, 0.0), (cosT, 0.25)):
                    V(lambda e, shift=shift: e.tensor_scalar(out=tq[:], in0=ang[:], scalar1=1.0 / TWO_PI, scalar2=shift,
                                                             op0=ALU.mult, op1=ALU.add))
                    V(lambda e: e.tensor_copy(out=ki[:], in_=tq[:]))
                    V(lambda e: e.tensor_copy(out=kf[:], in_=ki[:]))
                    V(lambda e: e.scalar_tensor_tensor(out=yy[:], in0=kf[:], scalar=-CW1, in1=ang[:], op0=ALU.mult, op1=ALU.add))
                    V(lambda e: e.scalar_tensor_tensor(out=yy[:], in0=kf[:], scalar=-CW2, in1=yy[:], op0=ALU.mult, op1=ALU.add))
                    if shift != 0.0:
                        V(lambda e: e.tensor_scalar(out=yy[:], in0=yy[:], scalar1=0.5 * math.pi, scalar2=None, op0=ALU.add))
                    V(lambda e: e.tensor_scalar(out=cc[:], in0=yy[:], scalar1=math.pi, scalar2=-TWO_PI, op0=ALU.is_gt, op1=ALU.mult))
                    V(lambda e: e.tensor_tensor(out=yy[:], in0=yy[:], in1=cc[:], op=ALU.add))
                    V(lambda e: e.tensor_scalar(out=cc[:], in0=yy[:], scalar1=-math.pi, scalar2=TWO_PI, op0=ALU.is_lt, op1=ALU.mult))
                    V(lambda e: e.tensor_tensor(out=yy[:], in0=yy[:], in1=cc[:], op=ALU.add))
                    V(lambda e: e.tensor_scalar(out=yy[:], in0=yy[:], scalar1=3.1415925, scalar2=-3.1415925, op0=ALU.min, op1=ALU.max))
                    A(lambda e, dst=dst: e.activation(out=dst[:], in_=yy[:], func=AF.Sin))
                dbg("sinT", sinT[:], [128, NTAB, 40], [rb])
                dbg("cosT", cosT[:], [128, NTAB, 40], [rb])
                P.barrier()
            if stop <= 0.5:
                P.emit()
                return nc

            NF = 4
            NX = 6
            xts = [(sbt(s1, "xt%d" % i, [128, D], F32), Buf(), P.dsem("dxt%d" % i)) for i in range(NX)]
            sss = [(sbt(s1, "ss%d" % i, [128, 1], F32), Buf()) for i in range(NF)]
            junk = sbt(s1, "junk", [128, D], BF16)
            junkb = Buf()
            xns = [(sbt(s1, "xn%d" % i, [128, D], BF16), Buf()) for i in range(NF)]
            psT = (pst(s1, "psT", [128, 8, 128], BF16), Buf())
            hTs = [(sbt(s1, "hT%d" % i, [128, 8, 128], BF16), Buf()) for i in range(NF)]
            pps = [(pst(s1, "pp%d" % i, [128, 1536], F32), Buf()) for i in range(2)]
            pkT = (pst(s1, "pkT", [128, 8, 128], BF16), Buf())
            sqs = [(sbt(s1, "sq%d" % i, [128, 1024], F32), Buf()) for i in range(2)]
            ssk = [(sbt(s1, "ssk%d" % i, [128, 16], F32), Buf()) for i in range(NF)]
            xg = [(sbt(s1, "xg%d" % i, [128, 1024], F32), Buf()) for i in range(NF)]
            rts = [[(sbt(s1, "rt%d_%d" % (k, i), [128, 256], F32), Buf()) for i in range(4)] for k in range(2)]
            ybf = [(sbt(s1, "ybf%d" % i, [128, 1024], BF16), Buf(), Buf()) for i in range(NF)]
            vst = [(sbt(s1, "vst%d" % i, [128, 640], BF16), Buf(), P.dsem("dvst%d" % i)) for i in range(2)]
            kst = [(sbt(s1, "kst%d" % i, [128, 8, 512], BF16), Buf(), P.dsem("dkst%d" % i)) for i in range(2)]

            jobs = []
            for s in range(2):
                for t in range(NT_K):
                    jobs.append(("kv", x_srcs[s][t * 128:(t + 1) * 128, :], t, s, s, t))
            for t in range(NT_Q):
                jobs.append(("q", xq_d[t * 128:(t + 1) * 128, :], NT_K + t, 0 if t < 16 else 1, None, t))
            if os.environ.get("PH1_JOBS"):
                jobs = jobs[:int(os.environ["PH1_JOBS"])]
            n_jobs = len(jobs)

            def st_load(ji):
                xt, xb_, xd = xts[ji % NX]
                x_ap = jobs[ji][1]
                P.do("sync", lambda e: e.dma_start(out=xt[:], in_=x_ap), writes=[xb_], dsem=xd)

            def st_stats(ji):
                kind, x_ap, tab, mslot, sslot, t = jobs[ji]
                xt, xb_, xd = xts[ji % NX]
                ss, ssb = sss[ji % NF]
                P.do("scalar", lambda e: e.activation(out=junk[:], in_=xt[:], func=AF.Square, accum_out=ss[:]),
                     reads=[xb_], writes=[junkb, ssb])
                P.do("scalar", lambda e: e.activation(out=ss[:], in_=ss[:], func=AF.Ln, scale=1.0 / D, bias=epsc[:, 0:1]),
                     reads=[ssb, cB], writes=[ssb])
                P.do("scalar", lambda e: e.activation(out=ss[:], in_=ss[:], func=AF.Exp, scale=-0.5), reads=[ssb], writes=[ssb])
                if kind == "q":
                    P.do("scalar", lambda e: e.activation(out=rstd_own[:, t:t + 1], in_=ss[:], func=AF.Copy), reads=[ssb], writes=[rownB])
                xn, xnb = xns[ji % NF]
                P.do("vector", lambda e: e.scalar_tensor_tensor(out=xn[:], in0=xt[:], scalar=ss[:, 0:1], in1=gbt[:],
                                                                op0=ALU.mult, op1=ALU.mult),
                     reads=[xb_, ssb, cB], writes=[xnb])

            def st_trans(ji):
                kind, x_ap, tab, mslot, sslot, t = jobs[ji]
                xn, xnb = xns[ji % NF]
                pT, pTb = psT
                for c in range(8):
                    P.do("tensor", lambda e, c=c: e.transpose(out=pT[:, c, :], in_=xn[:, c * 128:(c + 1) * 128], identity=ident[:]),
                         reads=[xnb, cB], writes=[pTb])
                hT, hTb = hTs[ji % NF]
                for c in range(8):
                    if ji % 2 == 0:
                        P.do("scalar", lambda e, c=c: e.activation(out=hT[:, c, :], in_=pT[:, c, :], func=AF.Identity,
                                                                   scale=sc1[:, c, mslot:mslot + 1], bias=modT[:, c, mslot:mslot + 1]),
                             reads=[pTb, cB], writes=[hTb])
                    else:
                        P.do("vector", lambda e, c=c: e.tensor_scalar(out=hT[:, c, :], in0=pT[:, c, :],
                                                                      scalar1=sc1[:, c, mslot:mslot + 1], scalar2=modT[:, c, mslot:mslot + 1],
                                                                      op0=ALU.mult, op1=ALU.add),
                             reads=[pTb, cB], writes=[hTb])

            def st_proj(ji):
                kind = jobs[ji][0]
                hT, hTb = hTs[ji % NF]
                pp, ppB = pps[ji % 2]
                w, wB, ncol = (wkv, wkvB, 1280) if kind == "kv" else (wq, wqB, 1024)
                for c0 in range(0, ncol, 512):
                    cw = min(512, ncol - c0)
                    for c in range(8):
                        P.do("tensor", lambda e, c=c, c0=c0, cw=cw: e.matmul(
                            pp[:, c0:c0 + cw], lhsT=hT[:, c, :], rhs=w[:, c, c0:c0 + cw], start=(c == 0), stop=(c == 7)),
                            reads=[hTb, wB], writes=[ppB])

            def st_b1(ji, part):
                kind, x_ap, tab, mslot, sslot, t = jobs[ji]
                pp, ppB = pps[ji % 2]
                if kind == "kv":
                    nh, nf, gain, Arange, Brange = 10, 640, gk, (8, 2), (0, 8)
                else:
                    nh, nf, gain, Arange, Brange = 16, 1024, gq, (0, 8), (8, 8)
                sq, sqB = sqs[ji % 2]
                sk, skb = ssk[ji % NF]
                xgt, xgb = xg[ji % NF]
                X = xgt[:, 0:nf].rearrange("p (h d) -> p h d", d=64)
                yt, yb, yb2 = ybf[ji % NF]
                Y = yt[:, 0:nf].rearrange("p (h d) -> p h d", d=64)
                if part == "act1":
                    if kind == "kv":
                        vs, vsb, vsd = vst[t % 2]
                        P.do("scalar", lambda e: e.activation(out=vs[:], in_=pp[:, 640:1280], func=AF.Copy),
                             reads=[ppB], writes=[vsb])
                        P.do("sync", lambda e: e.dma_start(out=VS_d[sslot][t * 128:(t + 1) * 128, :], in_=vs[:]),
                             reads=[vsb], writes=[VS_b[sslot]], dsem=vsd)
                    P.do("scalar", lambda e: e.activation(out=sq[:, 0:nf], in_=pp[:, 0:nf], func=AF.Square),
                         reads=[ppB], writes=[sqB])
                    return
                if part == "dve1":
                    P.do("vector", lambda e: e.tensor_reduce(out=sk[:, 0:nh], in_=sq[:, 0:nf].rearrange("p (h d) -> p h d", d=64),
                                                             axis=AX.X, op=ALU.add),
                         reads=[sqB], writes=[skb])
                    return
                if part == "act2":
                    P.do("scalar", lambda e: e.activation(out=sk[:, 0:nh], in_=sk[:, 0:nh], func=AF.Ln, scale=1.0 / 64.0, bias=epsc[:, 0:1]),
                         reads=[skb, cB], writes=[skb])
                    P.do("scalar", lambda e: e.activation(out=sk[:, 0:nh], in_=sk[:, 0:nh], func=AF.Exp, scale=-0.5),
                         reads=[skb], writes=[skb])
                    return
                if part == "dve2":
                    P.do("vector", lambda e: e.tensor_tensor(out=X, in0=pp[:, 0:nf].rearrange("p (h d) -> p h d", d=64),
                                                             in1=sk[:, 0:nh].unsqueeze(2).to_broadcast([128, nh, 64]), op=ALU.mult),
                         reads=[ppB, skb], writes=[xgb])
                    P.do("vector", lambda e: e.tensor_tensor(out=X, in0=X, in1=gain[:], op=ALU.mult),
                         reads=[xgb, tabB], writes=[xgb])
                    return
                if part == "cast":
                    P.do("scalar", lambda e: e.activation(out=Y[:, Brange[0]:Brange[0] + 8, 16:64], in_=X[:, Brange[0]:Brange[0] + 8, 16:64],
                                                          func=AF.Copy),
                         reads=[xgb], writes=[yb2])
                    return
                assert part == "rope"
                rt = rts[ji % 2]
                for (a_mode, (H0, H), ti) in ((True, Arange, 0), (False, Brange, 2)):
                    (t0, tb0), (t1, tb1) = rt[ti], rt[ti + 1]
                    if a_mode:
                        Xv = X[:, H0:H0 + H, :].rearrange("p h (a f j) -> p h a f j", a=2, f=2)
                        Yv = Y[:, H0:H0 + H, :].rearrange("p h (a f j) -> p h a f j", a=2, f=2)
                        X1, X2 = Xv[:, :, :, 0, :], Xv[:, :, :, 1, :]
                        Y1, Y2 = Yv[:, :, :, 0, :], Yv[:, :, :, 1, :]
                        shp = [128, H, 2, 16]
                        cs = cosT[:, tab, 0:32].rearrange("p (a j) -> p a j", a=2).unsqueeze(1).to_broadcast(shp)
                        sn = sinT[:, tab, 0:32].rearrange("p (a j) -> p a j", a=2).unsqueeze(1).to_broadcast(shp)
                        n = H * 32
                        T0 = t0[:, 0:n].rearrange("p (h a j) -> p h a j", h=H, a=2)
                        T1 = t1[:, 0:n].rearrange("p (h a j) -> p h a j", h=H, a=2)
                    else:
                        X1, X2 = X[:, H0:H0 + H, 0:8], X[:, H0:H0 + H, 8:16]
                        Y1, Y2 = Y[:, H0:H0 + H, 0:8], Y[:, H0:H0 + H, 8:16]
                        shp = [128, H, 8]
                        cs = cosT[:, tab, 32:40].unsqueeze(1).to_broadcast(shp)
                        sn = sinT[:, tab, 32:40].unsqueeze(1).to_broadcast(shp)
                        n = H * 8
                        T0 = t0[:, 0:n].rearrange("p (h j) -> p h j", h=H)
                        T1 = t1[:, 0:n].rearrange("p (h j) -> p h j", h=H)
                    G = lambda fn, rd, wr: P.do("gpsimd", fn, reads=rd, writes=wr)
                    G(lambda e, T0=T0, X1=X1, cs=cs: e.tensor_tensor(out=T0, in0=X1, in1=cs, op=ALU.mult), [xgb, tabB], [tb0])
                    G(lambda e, T1=T1, X2=X2, sn=sn: e.tensor_tensor(out=T1, in0=X2, in1=sn, op=ALU.mult), [xgb, tabB], [tb1])
                    G(lambda e, T0=T0, T1=T1, Y1=Y1: e.tensor_tensor(out=Y1, in0=T0, in1=T1, op=ALU.subtract), [tb0, tb1], [yb])
                    G(lambda e, T0=T0, X2=X2, cs=cs: e.tensor_tensor(out=T0, in0=X2, in1=cs, op=ALU.mult), [xgb, tabB, tb0], [tb0])
                    G(lambda e, T1=T1, X1=X1, sn=sn: e.tensor_tensor(out=T1, in0=X1, in1=sn, op=ALU.mult), [xgb, tabB, tb1], [tb1])
                    G(lambda e, T0=T0, T1=T1, Y2=Y2: e.tensor_tensor(out=Y2, in0=T0, in1=T1, op=ALU.add), [tb0, tb1], [yb])

            def st_b2(ji):
                kind, x_ap, tab, mslot, sslot, t = jobs[ji]
                nf = 640 if kind == "kv" else 1024
                nch = nf // 128
                yt, yb, yb2 = ybf[ji % NF]
                pk, pkb = pkT
                for c in range(nch):
                    P.do("tensor", lambda e, c=c: e.transpose(out=pk[:, c, :], in_=yt[:, c * 128:(c + 1) * 128], identity=ident[:]),
                         reads=[yb, yb2, cB], writes=[pkb])
                g = t // 4
                ks, ksb, ksd = kst[g % 2]
                P.do("vector", lambda e: e.tensor_copy(out=ks[:, 0:nch, (t % 4) * 128:(t % 4 + 1) * 128], in_=pk[:, 0:nch, :]),
                     reads=[pkb], writes=[ksb])
                if t % 4 == 3:
                    if kind == "kv":
                        dst, dB = KT_d[sslot].rearrange("(c p) t -> p c t", p=128)[:, :, g * 512:(g + 1) * 512], KT_b[sslot]
                    else:
                        dst, dB = QT_d.rearrange("(c p) t -> p c t", p=128)[:, :, g * 512:(g + 1) * 512], QT_b
                    P.do("sync", lambda e: e.dma_start(out=dst, in_=ks[:, 0:nch, :]), reads=[ksb], writes=[dB], dsem=ksd)

            LA = 5
            ok = lambda j: 0 <= j < n_jobs
            for ji in range(min(LA, n_jobs)):
                st_load(ji)
            for it in range(-3, n_jobs + 3):
                if ok(it + LA) and it + LA >= LA:
                    st_load(it + LA)
                if ok(it - 1):
                    st_b1(it - 1, "cast")
                if ok(it):
                    st_b1(it, "act1")
                    st_b1(it, "dve1")
                if ok(it + 3):
                    st_stats(it + 3)
                if ok(it):
                    st_b1(it, "act2")
                    st_b1(it, "dve2")
                if ok(it + 2):
                    st_trans(it + 2)
                if ok(it + 1):
                    st_proj(it + 1)
                if ok(it):
                    st_b1(it, "rope")
                if ok(it - 2):
                    st_b2(it - 2)
            P.barrier()
        es1.close()
        if stop <= 1:
            P.emit()
            return nc

        es3 = ExitStack()
        wz = sbt(es3, "wz", [128, 8, 1024], BF16)
        wg = sbt(es3, "wg", [128, 8, 2048], BF16)
        wpa = sbt(es3, "wpa", [128, 4, 1024], BF16)
        wpb = sbt(es3, "wpb", [128, 4, 1024], BF16)
        wo = sbt(es3, "wo", [128, 8, 1024], BF16)
        wzB, wgB, wpaB, wpbB, woB = Buf(), Buf(), Buf(), Buf(), Buf()
        with ExitStack() as s2:
            NBQ = 2048
            ctx3 = {"n": 0, "wn": 0}
            ctx3["wst"] = [(sbt(s2, "w3st%d" % i, [128, 1024], F32), Buf(), P.dsem("dw3st%d" % i)) for i in range(2)]
            ub = []
            for i in range(2):
                u = {"K": sbt(s2, "uK%d" % i, [128, S], BF16), "V": sbt(s2, "uV%d" % i, [128, NT_K, 128], BF16),
                     "Q": sbt(s2, "uQ%d" % i, [128, NBQ], BF16),
                     "Kb": [Buf() for _ in range(4)], "Vb": [Buf() for _ in range(4)], "Qb": Buf(),
                     "Kd": [P.dsem("dK%d_%d" % (i, j)) for j in range(4)],
                     "Vd": [P.dsem("dV%d_%d" % (i, j)) for j in range(4)], "Qd": P.dsem("dQ%d" % i)}
                ub.append(u)
            Sps = [(pst(s2, "Sps%d" % i, [128, 1024], F32), Buf()) for i in range(2)]
            acc = [(pst(s2, "acc%d" % i, [128, 512], F32), Buf()) for i in range(4)]
            Pt = [(sbt(s2, "Pt%d" % i, [128, 1024], BF16), Buf()) for i in range(4)]
            rinv = [(sbt(s2, "rinv%d" % i, [128, 512], F32), Buf()) for i in range(2)]
            tt = [(sbt(s2, "tt%d" % i, [128, 512], F32), Buf()) for i in range(2)]
            ob_ = (sbt(s2, "ob", [128, 512], F32), Buf())
            ocp = [(sbt(s2, "ocp%d" % i, [128, 512], F32), Buf()) for i in range(2)]
            smc = (sbt(s2, "smc", [128, 512], F32), Buf())
            smcA = smc
            rinvA = rinv[1]
            hl = [(sbt(s2, "hl%d" % i, [128, 512], BF16), Buf()) for i in range(2)]

            def split_hl(src, srcb):
                (h_, hb_), (l_, lb_) = hl
                P.do("vector", lambda e: e.tensor_copy(out=h_[:], in_=src[:]), reads=[srcb], writes=[hb_])
                P.do("vector", lambda e: e.tensor_tensor(out=l_[:], in0=src[:], in1=h_[:], op=ALU.subtract),
                     reads=[srcb, hb_], writes=[lb_])

            def mm_hl(out_ap, out_b, lhsT_ap):
                (h_, hb_), (l_, lb_) = hl
                P.do("tensor", lambda e: e.matmul(out_ap, lhsT=lhsT_ap, rhs=h_[:], start=True, stop=False),
                     reads=[hb_, cB], writes=[out_b])
                P.do("tensor", lambda e: e.matmul(out_ap, lhsT=lhsT_ap, rhs=l_[:], start=False, stop=True),
                     reads=[lb_, cB], writes=[out_b])
            sqo = (sbt(s2, "sqo", [128, 512], F32), Buf())
            rs = (sbt(s2, "rs", [128, 512], F32), Buf())
            ost = [(sbt(s2, "ost%d" % i, [128, 512], BF16), Buf(), P.dsem("dost%d" % i)) for i in range(2)]

            units = []
            for qi in range(3):
                slot = 0 if qi == 0 else 1
                for u in range(4):
                    units.append((qi, slot, "A", u))
                for h in range(4):
                    units.append((qi, slot, "B", h))

            def load_unit(n):
                qi, slot, kind, idx = units[n]
                u = ub[n % 2]
                kchunk = 4 if kind == "A" else idx
                qchunk = idx if kind == "A" else 4 + idx
                f0 = 0 if kind == "A" else 128 + idx * 128
                P.do("sync", lambda e: e.dma_start(out=u["Q"][:], in_=QT_d[qchunk * 128:(qchunk + 1) * 128, qi * NBQ:(qi + 1) * NBQ]),
                     reads=[QT_b], writes=[u["Qb"]], dsem=u["Qd"])
                for j in range(4):
                    P.do("sync", lambda e, j=j: e.dma_start(out=u["K"][:, j * 2048:(j + 1) * 2048],
                                                           in_=KT_d[slot][kchunk * 128:(kchunk + 1) * 128, j * 2048:(j + 1) * 2048]),
                         reads=[KT_b[slot]], writes=[u["Kb"][j]], dsem=u["Kd"][j])
                    P.do("sync", lambda e, j=j: e.dma_start(
                        out=u["V"][:, j * 16:(j + 1) * 16, :],
                        in_=VS_d[slot].rearrange("(kc p) f -> p kc f", p=128)[:, j * 16:(j + 1) * 16, f0:f0 + 128]),
                        reads=[VS_b[slot]], writes=[u["Vb"][j]], dsem=u["Vd"][j])

            steps = []
            for n in range(len(units)):
                for qb in range(4):
                    for kc in range(NT_K):
                        steps.append((n, qb, kc))
            nsteps = len(steps)
            cnt = {"ost": 0, "A": 0}
            pending = []
            wthunks = []

            def emit_qk(i):
                n, qb, kc = steps[i]
                u = ub[n % 2]
                Sp, Sb = Sps[i % 2]
                for r in range(2):
                    P.do("tensor", lambda e, r=r: e.matmul(
                        Sp[:, r * 512:(r + 1) * 512], lhsT=u["K"][r * 64:(r + 1) * 64, kc * 128:(kc + 1) * 128],
                        rhs=u["Q"][r * 64:(r + 1) * 64, qb * 512:(qb + 1) * 512], start=True, stop=True),
                        reads=[u["Kb"][kc // 16], u["Qb"]], writes=[Sb])

            def emit_exp(i):
                Sp, Sb = Sps[i % 2]
                pt, pb = Pt[i % 4]
                P.do("scalar", lambda e: e.activation(out=pt[:], in_=Sp[:], func=AF.Exp, scale=0.125),
                     reads=[Sb], writes=[pb])

            def emit_pv(i):
                n, qb, kc = steps[i]
                qi, slot, kind, idx = units[n]
                u = ub[n % 2]
                pt, pb = Pt[i % 4]
                first, last = (kc == 0), (kc == NT_K - 1)
                vb = u["Vb"][kc // 16]
                if kind == "A":
                    a0 = 2 * (qb % 2)
                    (ao, aob), (asm, asb) = acc[a0], acc[a0 + 1]
                    for r in range(2):
                        P.do("tensor", lambda e, r=r: e.matmul(
                            ao[r * 64:(r + 1) * 64, :], lhsT=u["V"][:, kc, r * 64:(r + 1) * 64], rhs=pt[:, r * 512:(r + 1) * 512],
                            start=first, stop=last, tile_position=(0, r * 64)), reads=[vb, pb], writes=[aob])
                    if kc % 2 == 1:
                        ptp, pbp = Pt[(i - 1) % 4]
                        pfirst, plast = (kc == 1), (kc == NT_K - 1)
                        for jq in range(4):
                            src, srcb = (ptp, pbp) if jq < 2 else (pt, pb)
                            mq = jq % 2
                            P.do("tensor", lambda e, jq=jq, src=src, mq=mq: e.matmul(
                                asm[32 * jq:32 * jq + 32, :], lhsT=ones_b[:, 0:32], rhs=src[:, mq * 512:(mq + 1) * 512],
                                start=pfirst, stop=plast, tile_position=(0, 32 * jq)), reads=[cB, srcb], writes=[asb])
                    if last:
                        smc_, smcb = smcA
                        P.do("vector", lambda e: e.tensor_copy(out=smc_[:], in_=asm[:]), reads=[asb], writes=[smcb])
                        split_hl(smc_, smcb)
                        r0 = idx * 128
                        t0 = qi * NBQ + qb * 512

                        def finA():
                            mm_hl(asm[:], asb, sel_b[:, 256:384])
                            ri, rib = rinvA
                            P.do("vector", lambda e: e.reciprocal(out=ri[:], in_=asm[:]), reads=[asb], writes=[rib])
                            os_, osb, osd = ost[cnt["ost"] % 2]
                            cnt["ost"] += 1
                            P.do("vector", lambda e: e.tensor_tensor(out=os_[:], in0=ao[:], in1=ri[:], op=ALU.mult),
                                 reads=[aob, rib], writes=[osb])
                            P.do("sync", lambda e: e.dma_start(out=OT_d[r0:r0 + 128, t0:t0 + 512], in_=os_[:]),
                                 reads=[osb], writes=[OT_b], dsem=osd)

                        nxt_is_b = (qb == 3 and n + 1 < len(units) and units[n + 1][2] == "B")
                        if nxt_is_b:
                            finA()
                        else:
                            pending.append((i + 5, finA))
                else:
                    for m in range(2):
                        ao, aob = acc[m]
                        P.do("tensor", lambda e, m=m, ao=ao: e.matmul(ao[:], lhsT=u["V"][:, kc, :], rhs=pt[:, m * 512:(m + 1) * 512],
                                                                    start=first, stop=last), reads=[vb, pb], writes=[aob])
                    asm, asb = acc[2]
                    if kc % 2 == 1:
                        ptp, pbp = Pt[(i - 1) % 4]
                        pfirst, plast = (kc == 1), (kc == NT_K - 1)
                        for jq in range(4):
                            src, srcb = (ptp, pbp) if jq < 2 else (pt, pb)
                            mq = jq % 2
                            P.do("tensor", lambda e, jq=jq, src=src, mq=mq: e.matmul(
                                asm[32 * jq:32 * jq + 32, :], lhsT=ones_b[:, 0:32], rhs=src[:, mq * 512:(mq + 1) * 512],
                                start=pfirst, stop=plast, tile_position=(0, 32 * jq)), reads=[cB, srcb], writes=[asb])
                    if last:
                        smc_, smcb = smc
                        for m in range(2):
                            ao, aob = acc[m]
                            oc, ocb = ocp[m]
                            P.do("vector", lambda e, oc=oc, ao=ao: e.tensor_copy(out=oc[:], in_=ao[:]), reads=[aob], writes=[ocb])
                        P.do("vector", lambda e: e.tensor_copy(out=smc_[:], in_=asm[:]), reads=[asb], writes=[smcb])
                        split_hl(smc_, smcb)
                        r0 = 512 + idx * 128
                        t0 = qi * NBQ + qb * 512

                        def fin2m(m):
                            rp, rpb = acc[3]
                            mm_hl(rp[:], rpb, sel_b[:, m * 128:(m + 1) * 128])
                            ri, rib = rinv[m]
                            P.do("vector", lambda e: e.reciprocal(out=ri[:], in_=rp[:]), reads=[rpb], writes=[rib])
                            oc, ocb = ocp[m]
                            t_, tb_ = tt[m]
                            P.do("vector", lambda e: e.tensor_tensor(out=t_[:], in0=oc[:], in1=ri[:], op=ALU.mult),
                                 reads=[ocb, rib], writes=[tb_])

                        def fin2b():
                            o_, o_b = ob_
                            P.do("vector", lambda e: e.scalar_tensor_tensor(out=o_[:], in0=tt[1][0][:], scalar=neglam[:, 0:1], in1=tt[0][0][:],
                                                                            op0=ALU.mult, op1=ALU.add),
                                 reads=[tt[0][1], tt[1][1], cB], writes=[o_b])
                            sq_, sq_b = sqo
                            P.do("gpsimd", lambda e: e.tensor_tensor(out=sq_[:], in0=o_[:], in1=o_[:], op=ALU.mult),
                                 reads=[o_b], writes=[sq_b])
                            split_hl(sq_, sq_b)

                        def fin2c():
                            rp, rpb = acc[3]
                            mm_hl(rp[:], rpb, ones_b[:])
                            r_, r_b = rs
                            P.do("vector", lambda e: e.tensor_scalar(out=r_[:], in0=rp[:], scalar1=1.0 / 128.0, scalar2=EPS,
                                                                     op0=ALU.mult, op1=ALU.add), reads=[rpb], writes=[r_b])

                        def fin3a():
                            r_, r_b = rs
                            P.do("scalar", lambda e: e.activation(out=r_[:], in_=r_[:], func=AF.Ln), reads=[r_b], writes=[r_b])

                        def fin3():
                            o_, o_b = ob_
                            r_, r_b = rs
                            P.do("scalar", lambda e: e.activation(out=r_[:], in_=r_[:], func=AF.Exp, scale=-0.5), reads=[r_b], writes=[r_b])
                            os_, osb, osd = ost[cnt["ost"] % 2]
                            cnt["ost"] += 1
                            P.do("vector", lambda e: e.scalar_tensor_tensor(out=os_[:], in0=o_[:], scalar=sg8[:, 0:1], in1=r_[:],
                                                                            op0=ALU.mult, op1=ALU.mult),
                                 reads=[o_b, r_b, cB], writes=[osb])
                            P.do("sync", lambda e: e.dma_start(out=OT_d[r0:r0 + 128, t0:t0 + 512], in_=os_[:]),
                                 reads=[osb], writes=[OT_b], dsem=osd)

                        pending.append((i + 4, lambda: fin2m(0)))
                        pending.append((i + 7, lambda: fin2m(1)))
                        pending.append((i + 10, fin2b))
                        pending.append((i + 15, fin2c))
                        pending.append((i + 19, fin3a))
                        pending.append((i + 24, fin3))

            load_unit(0)
            for (w_, wB_, wd_, nch_, ncol_) in ((wz, wzB, wz_d, 8, 1024), (wg, wgB, wg_d, 8, 2048), (wpa, wpaB, wpa_d, 4, 1024),
                                                (wpb, wpbB, wpb_d, 4, 1024), (wo, woB, wout_d, 8, 1024)):
                load_weight(ctx3, w_, wB_, wd_, nch_, ncol_, engines=("gpsimd",), defer=wthunks)
            emit_qk(0)
            emit_qk(1)
            for i in range(nsteps):
                n, qb, kc = steps[i]
                if qb == 0 and kc == 0 and n + 1 < len(units):
                    load_unit(n + 1)
                emit_exp(i)
                if i + 2 < nsteps:
                    emit_qk(i + 2)
                emit_pv(i)
                if wthunks and i >= 4 and i % 3 == 0:
                    wthunks.pop(0)()
                while pending and pending[0][0] <= i:
                    pending.pop(0)[1]()
            while pending:
                pending.pop(0)[1]()
            while wthunks:
                wthunks.pop(0)()
            P.barrier()
        if stop <= 2:
            P.emit()
            return nc

        with ExitStack() as s3:
            NXT = 8
            xts = [(sbt(s3, "x3t%d" % i, [128, D], F32), Buf(), P.dsem("dx3t%d" % i)) for i in range(NXT)]
            sss = [(sbt(s3, "s3s%d" % i, [128, 1], F32), Buf()) for i in range(4)]
            junk3 = sbt(s3, "junk3", [128, D], BF16)
            junk3b = Buf()
            xns = [(sbt(s3, "x3n%d" % i, [128, D], BF16), Buf()) for i in range(4)]
            psTs = [(pst(s3, "ps3T%d" % i, [128, 8, 128], BF16), Buf()) for i in range(2)]
            hT4s = [(sbt(s3, "hT4_%d" % i, [128, 8, 512], BF16), [Buf(), Buf()]) for i in range(2)]
            On = sbt(s3, "On", [128, 8, 512], BF16)
            OnB, Ond = Buf(), P.dsem("dOn")
            og = sbt(s3, "og", [128, 8, 512], BF16)
            ogB = Buf()
            szt = [(sbt(s3, "szt%d" % i, [128, 512], BF16), Buf()) for i in range(2)]
            sgt_ = [(sbt(s3, "sgs%d" % i, [128, 512], F32), Buf()) for i in range(4)]
            m12 = [(sbt(s3, "m12_%d" % i, [128, 512], F32), Buf()) for i in range(2)]
            mg = sbt(s3, "mg", [128, 8, 512], BF16)
            mgB = Buf()
            ysb = [(sbt(s3, "ysb%d" % i, [128, D], F32), Buf(), P.dsem("dysb%d" % i)) for i in range(2)]
            pz = [(pst(s3, "pz%d" % i, [128, 512], F32), Buf()) for i in range(2)]
            pab = [(pst(s3, "pab%d" % i, [128, 512], F32), Buf()) for i in range(2)]
            pout = [(pst(s3, "pout%d" % i, [128, 512], F32), Buf()) for i in range(2)]
            NG = NQ // 512
            if os.environ.get("PH3_GROUPS"):
                NG = int(os.environ["PH3_GROUPS"])
            pzn = [0]

            def g_mslot(g):
                return 0 if g < 4 else 1

            def p3_load_x(g):
                for tl in range(4):
                    xt, xb_, xd = xts[(g * 4 + tl) % NXT]
                    r0 = g * 512 + tl * 128
                    P.do("sync", lambda e, xt=xt, r0=r0: e.dma_start(out=xt[:], in_=xq_d[r0:r0 + 128, :]), writes=[xb_], dsem=xd)

            def p3_load_on(g):
                t0 = g * 512
                P.do("sync", lambda e: e.dma_start(out=On[:], in_=OT_d.rearrange("(c p) t -> p c t", p=128)[:, :, t0:t0 + 512]),
                     reads=[OT_b], writes=[OnB], dsem=Ond)

            def p3_stats(g):
                for tl in range(4):
                    xt, xb_, xd = xts[(g * 4 + tl) % NXT]
                    col = g * 4 + tl
                    xn, xnb = xns[tl]
                    P.do("vector", lambda e, xn=xn, xt=xt, col=col: e.scalar_tensor_tensor(
                        out=xn[:], in0=xt[:], scalar=rstd_own[:, col:col + 1], in1=gbt[:], op0=ALU.mult, op1=ALU.mult),
                        reads=[xb_, cB], writes=[xnb])

            def p3_trans(g, tiles=(0, 1, 2, 3)):
                mslot = g_mslot(g)
                hT4, hT4B = hT4s[g % 2]
                for tl in tiles:
                    xn, xnb = xns[tl]
                    pT, pTb = psTs[tl % 2]
                    for c in range(8):
                        P.do("tensor", lambda e, c=c, pT=pT, xn=xn: e.transpose(out=pT[:, c, :], in_=xn[:, c * 128:(c + 1) * 128], identity=ident[:]),
                             reads=[xnb, cB], writes=[pTb])
                    for c in range(8):
                        dst = hT4[:, c, tl * 128:(tl + 1) * 128]
                        if tl % 2 == 0:
                            P.do("scalar", lambda e, c=c, pT=pT, dst=dst: e.activation(
                                out=dst, in_=pT[:, c, :], func=AF.Identity, scale=sc1[:, c, mslot:mslot + 1], bias=modT[:, c, mslot:mslot + 1]),
                                reads=[pTb, cB], writes=[hT4B[0]])
                        else:
                            P.do("vector", lambda e, c=c, pT=pT, dst=dst: e.tensor_scalar(
                                out=dst, in0=pT[:, c, :], scalar1=sc1[:, c, mslot:mslot + 1], scalar2=modT[:, c, mslot:mslot + 1],
                                op0=ALU.mult, op1=ALU.add), reads=[pTb, cB], writes=[hT4B[1]])

            def p3_body_a(g):
                hT4, hT4B = hT4s[g % 2]
                for j in range(8):
                    p_, pb_ = pz[pzn[0] % 2]
                    pzn[0] += 1
                    for c in range(8):
                        P.do("tensor", lambda e, p_=p_, j=j, c=c: e.matmul(p_[:], lhsT=wz[:, c, j * 128:(j + 1) * 128], rhs=hT4[:, c, :],
                                                                         start=(c == 0), stop=(c == 7)),
                             reads=[wzB] + hT4B, writes=[pb_])
                    sz, szb = szt[j % 2]
                    P.do("scalar", lambda e, p_=p_, sz=sz: e.activation(out=sz[:], in_=p_[:], func=AF.Silu), reads=[pb_], writes=[szb])
                    P.do("vector", lambda e, sz=sz, j=j: e.tensor_tensor(out=og[:, j, :], in0=sz[:], in1=On[:, j, :], op=ALU.mult),
                         reads=[szb, OnB], writes=[ogB])

            def p3_body_b(g, j):
                hT4, hT4B = hT4s[g % 2]
                sgl = []
                for br in range(2):
                    p_, pb_ = pz[pzn[0] % 2]
                    pzn[0] += 1
                    c0 = br * 1024 + j * 128
                    for c in range(8):
                        P.do("tensor", lambda e, p_=p_, c0=c0, c=c: e.matmul(p_[:], lhsT=wg[:, c, c0:c0 + 128], rhs=hT4[:, c, :],
                                                                           start=(c == 0), stop=(c == 7)),
                             reads=[wgB] + hT4B, writes=[pb_])
                    sg_, sgb_ = sgt_[(2 * j + br) % 4]
                    P.do("scalar", lambda e, p_=p_, sg_=sg_: e.activation(out=sg_[:], in_=p_[:], func=AF.Sigmoid),
                         reads=[pb_], writes=[sgb_])
                    sgl.append((sg_, sgb_))
                for br in range(2):
                    p_, pb_ = pab[br]
                    w_, wB_ = (wpa, wpaB) if br == 0 else (wpb, wpbB)
                    for kk in range(4):
                        P.do("tensor", lambda e, p_=p_, w_=w_, kk=kk, br=br: e.matmul(
                            p_[:], lhsT=w_[:, kk, j * 128:(j + 1) * 128], rhs=og[:, br * 4 + kk, :], start=(kk == 0), stop=(kk == 3)),
                            reads=[wB_, ogB], writes=[pb_])
                    m_, mb_ = m12[br]
                    sg_, sgb_ = sgl[br]
                    P.do("vector", lambda e, m_=m_, p_=p_, sg_=sg_: e.tensor_tensor(out=m_[:], in0=p_[:], in1=sg_[:], op=ALU.mult),
                         reads=[pb_, sgb_], writes=[mb_])
                P.do("gpsimd", lambda e: e.tensor_tensor(out=mg[:, j, :], in0=m12[0][0][:], in1=m12[1][0][:], op=ALU.add),
                     reads=[m12[0][1], m12[1][1]], writes=[mgB])

            def p3_body_c(g):
                mslot = g_mslot(g)
                for tl in range(4):
                    ys, ysB, ysd = ysb[tl % 2]
                    xt, xtb, _ = xts[(g * 4 + tl) % NXT]
                    for hb in range(2):
                        p_, pb_ = pout[hb]
                        for j in range(8):
                            P.do("tensor", lambda e, p_=p_, j=j, tl=tl, hb=hb: e.matmul(
                                p_[:], lhsT=mg[:, j, tl * 128:(tl + 1) * 128], rhs=wo[:, j, hb * 512:(hb + 1) * 512],
                                start=(j == 0), stop=(j == 7)), reads=[mgB, woB], writes=[pb_])
                        P.do("vector", lambda e, p_=p_, ys=ys, hb=hb: e.tensor_tensor(
                            out=ys[:, hb * 512:(hb + 1) * 512], in0=p_[:], in1=gate_bc[:, mslot, hb * 512:(hb + 1) * 512], op=ALU.mult),
                            reads=[pb_, cB], writes=[ysB])
                    P.do("gpsimd", lambda e, ys=ys, xt=xt: e.tensor_tensor(out=ys[:], in0=ys[:], in1=xt[:], op=ALU.add),
                         reads=[ysB, xtb], writes=[ysB])
                    r0 = g * 512 + tl * 128
                    P.do("sync", lambda e, ys=ys, r0=r0: e.dma_start(out=y_d[r0:r0 + 128, :], in_=ys[:]),
                         reads=[ysB], writes=[Y_b], dsem=ysd)

            p3_load_x(0)
            p3_load_on(0)
            p3_stats(0)
            p3_trans(0)
            for g in range(NG):
                if g + 1 < NG:
                    p3_load_x(g + 1)
                p3_body_a(g)
                if g + 1 < NG:
                    p3_load_on(g + 1)
                    p3_stats(g + 1)
                for j in range(8):
                    p3_body_b(g, j)
                    if 2 <= j <= 5 and g + 1 < NG:
                        p3_trans(g + 1, tiles=(j - 2,))
                p3_body_c(g)
            P.barrier()
        es3.close()
        P.emit()
    return nc


def _core_plan():
    plans = []
    for c in range(NCORES):
        us = [(g // 4, g % 4) for g in (3 * c, 3 * c + 1, 3 * c + 2)]
        seqs = [u[0] for u in us]
        if seqs[0] == seqs[1] == seqs[2]:
            order = [0, 1, 2]
        elif seqs[1] == seqs[2]:
            order = [0, 1, 2]
        else:
            order = [2, 0, 1]
        plans.append([us[i] for i in order])
    return plans


def _sel_const():
    m = np.zeros((128, 384), np.float32)
    m[0, 0:128] = 1.0
    m[64, 0:128] = 1.0
    m[32, 128:256] = 1.0
    m[96, 128:256] = 1.0
    m[0, 256:320] = 1.0
    m[64, 256:320] = 1.0
    m[32, 320:384] = 1.0
    m[96, 320:384] = 1.0
    return m


def _host_inputs(inputs):
    f = lambda a: np.ascontiguousarray(np.asarray(a), dtype=np.float32)
    xs = [f(inputs["x_prompt"])[b] for b in range(4)] + [f(inputs["x_sample"])[b] for b in range(2)]
    cs = [f(inputs["c_prompt"])[b] for b in range(4)] + [f(inputs["c_sample"])[b] for b in range(2)]
    w_in = f(inputs["w_in"])[0]
    w_ada = f(inputs["w_ada"])[0]
    b_ada = f(inputs["b_ada"])[0]
    o = np.cumsum([0, 512, 128, 128, 512, 512, 512, 512, 512, 1024, 1024])
    qa, ka, va, za, qb, kb, vb, zb, ga, gbm = [w_in[:, o[i]:o[i + 1]] for i in range(10)]
    perm = np.concatenate([np.concatenate([np.arange(u * 64, (u + 1) * 64), np.arange((4 + u) * 64, (5 + u) * 64)])
                           for u in range(4)])
    rep = lambda v, n=128: np.ascontiguousarray(np.broadcast_to(np.asarray(v, np.float32), (n,) + tuple(np.shape(v))))
    qn_a, kn_a, qn_b, kn_b = [f(inputs[k])[0] for k in ("qn_a", "kn_a", "qn_b", "kn_b")]
    shared = {
        "w_ada": w_ada,
        "b_adaT": np.ascontiguousarray(b_ada.reshape(24, 128).T),
        "b_row": b_ada.reshape(1, -1).copy(),
        "gb": rep(f(inputs["norm_g"])[0]),
        "gq": rep(np.stack([qn_a] * 8 + [qn_b] * 8)),
        "gk": rep(np.stack([kn_b] * 8 + [kn_a] * 2)),
        "lamv": rep(np.stack([f(inputs[k])[0] for k in ("lam_q1", "lam_k1", "lam_q2", "lam_k2")])),
        "sg": f(inputs["subln_g"])[0].reshape(128, 1).copy(),
        "jidx": rep(np.concatenate([np.float32(10000.0) ** (-np.arange(0, 32, 2, dtype=np.float32) / np.float32(32)),
                                    np.float32(500000.0) ** (-np.arange(0, 16, 2, dtype=np.float32) / np.float32(16))]).astype(np.float32)),
        "ident": np.eye(128, dtype=np.float32),
        "selc": _sel_const(),
        "w_kv": np.ascontiguousarray(np.concatenate([kb, ka, va, vb], axis=1)),
        "w_q": np.ascontiguousarray(np.concatenate([qa[:, perm], qb], axis=1)),
        "w_z": np.ascontiguousarray(np.concatenate([za[:, perm], zb], axis=1)),
        "w_g": np.ascontiguousarray(np.concatenate([ga, gbm], axis=1)),
        "w_pa": np.ascontiguousarray(f(inputs["w_proj_a"])[0][perm, :]),
        "w_pb": f(inputs["w_proj_b"])[0],
        "w_out": f(inputs["w_out"])[0],
    }
    plans = _core_plan()
    in_maps = []
    for c in range(NCORES):
        pl = plans[c]
        sa, sb = pl[0][0], pl[1][0]
        m = dict(shared)
        m["xa"] = xs[sa]
        m["xb"] = xs[sb]
        m["xq"] = np.ascontiguousarray(np.concatenate([xs[s][q * 2048:(q + 1) * 2048] for (s, q) in pl], axis=0))
        cT = np.stack([cs[sa], cs[sb]], axis=-1)
        m["cT"] = np.ascontiguousarray(cT.reshape(8, 128, 2).transpose(1, 0, 2))
        pos = np.zeros((128, NTAB), np.float32)
        pos[:, :NT_K] = (np.arange(NT_K)[None, :] * 128 + np.arange(128)[:, None])
        ownpos = np.concatenate([q * 2048 + np.arange(2048) for (s, q) in pl])
        pos[:, NT_K:] = ownpos.reshape(NT_Q, 128).T
        m["pos"] = pos
        in_maps.append(m)
    return in_maps, plans


_NC_CACHE = {}


def kernel(**inputs):
    in_maps, plans = _host_inputs(inputs)
    if "nc" not in _NC_CACHE:
        _NC_CACHE["nc"] = build_program()
    nc = _NC_CACHE["nc"]
    res = run_bass_kernel_spmd(nc, in_maps, core_ids=list(range(NCORES)))
    ys = [np.empty((S, D), np.float32) for _ in range(6)]
    for c in range(NCORES):
        y = np.asarray(res.results[c]["y"])
        for i, (s, q) in enumerate(plans[c]):
            ys[s][q * 2048:(q + 1) * 2048] = y[i * 2048:(i + 1) * 2048]
    return (np.stack(ys[:4]), np.stack(ys[4:]))
```

```python
import math
import os
from contextlib import ExitStack

import numpy as np
import concourse.bass as bass
import concourse.mybir as mybir
from concourse.bass_utils import run_bass_kernel_spmd

F32 = mybir.dt.float32
BF16 = mybir.dt.bfloat16
I32 = mybir.dt.int32
AF = mybir.ActivationFunctionType
ALU = mybir.AluOpType
AX = mybir.AxisListType

D = 1024
S = 8192
NQ = 6144
NCORES = 8
EPS = 1e-6
LAM_INIT = 0.8 - 0.6 * math.exp(-0.3 * 0)
NT_K = S // 128
NT_Q = NQ // 128
NTAB = NT_K + NT_Q
TWO_PI = 2.0 * math.pi
CW1 = 6.28125
CW2 = TWO_PI - CW1


class Buf:
    __slots__ = ("w", "r", "const", "multi", "ws")

    def __init__(self, const=False, multi=False):
        self.w = None
        self.r = []
        self.const = const
        self.multi = multi
        self.ws = {}


class Prog:
    ENGS = ("sync", "scalar", "vector", "gpsimd", "tensor")

    def __init__(self, nc, es):
        self.nc = nc
        self.es = es
        self.ops = []
        self.sem = {e: es.enter_context(nc.semaphore("s_" + e)) for e in self.ENGS}
        self.dsems = []
        self.last = {e: None for e in self.ENGS}

    def dsem(self, name):
        s = self.es.enter_context(self.nc.semaphore(name))
        d = {"sem": s, "count": 0, "last": None}
        self.dsems.append(d)
        return d

    def raw(self, eng, fn, deps, dsem=None):
        o = {"eng": eng, "fn": fn, "deps": [d for d in deps if d is not None], "sig": False,
             "dma": None, "val": None}
        if dsem is not None:
            dsem["count"] += 16
            o["dma"] = (dsem, dsem["count"])
            dsem["last"] = o
        self.ops.append(o)
        self.last[eng] = o
        return o

    def do(self, eng, fn, reads=(), writes=(), dsem=None, extra=()):
        deps = list(extra)
        for b in reads:
            if b.multi:
                deps.extend(b.ws.values())
            elif b.w is not None:
                deps.append(b.w)
        for b in writes:
            if b.multi:
                pass
            else:
                if b.w is not None and not (b.w["dma"] is None and b.w["eng"] == eng and dsem is None):
                    deps.append(b.w)
                deps.extend(b.r)
        o = self.raw(eng, fn, deps, dsem)
        for b in reads:
            if not b.const:
                b.r.append(o)
        for b in writes:
            if b.multi:
                b.ws[id(dsem)] = o
            else:
                b.w = o
                b.r = []
        return o

    def barrier(self):
        deps = [o for o in self.last.values() if o is not None]
        deps += [d["last"] for d in self.dsems if d["last"] is not None]
        for e in self.ENGS:
            self.raw(e, lambda eng: eng.nop(), deps)

    def emit(self):
        nc = self.nc
        for o in self.ops:
            for d in o["deps"]:
                if d["dma"] is None and not (d["eng"] == "tensor" and o["eng"] == "tensor"):
                    d["sig"] = True
        cnt = {e: 0 for e in self.ENGS}
        for o in self.ops:
            if o["dma"] is None and o["sig"]:
                cnt[o["eng"]] += 1
                o["val"] = cnt[o["eng"]]
        by_eng = {e: [o for o in self.ops if o["eng"] == e] for e in self.ENGS}
        sem = self.sem
        with nc.Block() as block:
            def mk(e):
                def body(eng):
                    waited = {}
                    for o in by_eng[e]:
                        need = {}
                        for d in o["deps"]:
                            if d["dma"] is not None:
                                key, s, v = id(d["dma"][0]), d["dma"][0]["sem"], d["dma"][1]
                            else:
                                if d["eng"] == "tensor" and e == "tensor":
                                    continue
                                key, s, v = d["eng"], sem[d["eng"]], d["val"]
                            if v > need.get(key, (None, 0))[1]:
                                need[key] = (s, v)
                        for key, (s, v) in need.items():
                            if waited.get(key, 0) >= v:
                                continue
                            eng.wait_ge(s, v)
                            waited[key] = v
                        ins = o["fn"](eng)
                        if o["dma"] is not None:
                            ins.then_inc(o["dma"][0]["sem"], 16)
                        elif o["sig"]:
                            ins.then_inc(sem[e], 1)
                return body
            for e in self.ENGS:
                if by_eng[e]:
                    getattr(block, e)(mk(e))


def build_program(debug=False, stop=99):
    nc = bass.Bass("TRN2", target_bir_lowering=False)

    def din(name, shape, dt=F32):
        return nc.dram_tensor(name, list(shape), dt, kind="ExternalInput").ap()

    xa_d = din("xa", [S, D])
    xb_d = din("xb", [S, D])
    xq_d = din("xq", [NQ, D])
    cT_d = din("cT", [128, 8, 2])
    wada_d = din("w_ada", [D, 3 * D])
    badaT_d = din("b_adaT", [128, 24])
    brow_d = din("b_row", [1, 3 * D])
    gb_d = din("gb", [128, D])
    gq_d = din("gq", [128, 16, 64])
    gk_d = din("gk", [128, 10, 64])
    lamv_d = din("lamv", [128, 4, 64])
    sg_d = din("sg", [128, 1])
    pos_d = din("pos", [128, NTAB])
    jidx_d = din("jidx", [128, 24])
    ident_d = din("ident", [128, 128])
    sel_d = din("selc", [128, 384])
    wkv_d = din("w_kv", [D, 1280])
    wq_d = din("w_q", [D, 1024])
    wz_d = din("w_z", [D, 1024])
    wg_d = din("w_g", [D, 2048])
    wpa_d = din("w_pa", [512, D])
    wpb_d = din("w_pb", [512, D])
    wout_d = din("w_out", [D, D])
    y_d = nc.dram_tensor("y", [NQ, D], F32, kind="ExternalOutput").ap()
    skind = "ExternalOutput" if debug else "Internal"
    KT_d = [nc.dram_tensor("KT%d" % s, [640, S], BF16, kind=skind).ap() for s in range(2)]
    VS_d = [nc.dram_tensor("VS%d" % s, [S, 640], BF16, kind=skind).ap() for s in range(2)]
    QT_d = nc.dram_tensor("QT", [1024, NQ], BF16, kind=skind).ap()
    OT_d = nc.dram_tensor("OT", [1024, NQ], BF16, kind=skind).ap()
    x_srcs = [xa_d, xb_d]

    with ExitStack() as es:
        P = Prog(nc, es)

        def sbt(scope, name, shape, dt):
            return scope.enter_context(nc.sbuf_tensor("sb_" + name, list(shape), dt))

        def pst(scope, name, shape, dt):
            return scope.enter_context(nc.psum_tensor("ps_" + name, list(shape), dt))

        dbg_sem = P.dsem("dbg") if debug else None

        def dbg(name, ap, shape, bufs, dt=F32):
            if not debug:
                return
            o = nc.dram_tensor("dbg_" + name, list(shape), dt, kind="ExternalOutput").ap()
            P.do("sync", lambda e: e.dma_start(out=o, in_=ap), reads=bufs, dsem=dbg_sem)

        KT_b = [Buf(multi=True) for _ in range(2)]
        VS_b = [Buf(multi=True) for _ in range(2)]
        QT_b = Buf(multi=True)
        OT_b = Buf(multi=True)
        Y_b = Buf(multi=True)

        ident = sbt(es, "ident", [128, 128], BF16)
        ones_b = sbt(es, "ones_b", [128, 128], BF16)
        ones_f = sbt(es, "ones_f", [128, 128], F32)
        gbt = sbt(es, "gbt", [128, D], F32)
        modT = sbt(es, "modT", [128, 24, 2], F32)
        sc1 = sbt(es, "sc1", [128, 8, 2], F32)
        gate_bc = sbt(es, "gate_bc", [128, 2, D], F32)
        neglam = sbt(es, "neglam", [128, 1], F32)
        sg8 = sbt(es, "sg8", [128, 1], F32)
        epsc = sbt(es, "epsc", [128, 1], F32)
        sel = sbt(es, "sel", [128, 384], F32)
        sel_b = sbt(es, "sel_b", [128, 384], BF16)
        rstd_own = sbt(es, "rstd_own", [128, NT_Q], F32)
        rownB = Buf()
        cB = Buf(const=True)

        def load_weight(ctx, dst, dst_buf, w_ap, nchunk, ncols, engines=("vector", "gpsimd", "scalar"), defer=None):
            for c in range(nchunk):
                for c0 in range(0, ncols, 1024):
                    if defer is not None:
                        defer.append(lambda c=c, c0=c0: load_weight_piece(ctx, dst, dst_buf, w_ap, ncols, engines, c, c0))
                    else:
                        load_weight_piece(ctx, dst, dst_buf, w_ap, ncols, engines, c, c0)

        def load_weight_piece(ctx, dst, dst_buf, w_ap, ncols, engines, c, c0):
            if True:
                if True:
                    cw = min(1024, ncols - c0)
                    i = ctx["wn"]
                    ctx["wn"] += 1
                    st, stb, std = ctx["wst"][i % len(ctx["wst"])]
                    P.do("sync", lambda e, st=st, c=c, c0=c0, cw=cw: e.dma_start(out=st[:, 0:cw], in_=w_ap[c * 128:(c + 1) * 128, c0:c0 + cw]),
                         writes=[stb], dsem=std)
                    eng = engines[i % len(engines)]
                    if eng == "scalar":
                        P.do(eng, lambda e, st=st, c=c, c0=c0, cw=cw: e.activation(out=dst[:, c, c0:c0 + cw], in_=st[:, 0:cw], func=AF.Copy),
                             reads=[stb], writes=[dst_buf])
                    else:
                        P.do(eng, lambda e, st=st, c=c, c0=c0, cw=cw: e.tensor_copy(out=dst[:, c, c0:c0 + cw], in_=st[:, 0:cw]),
                             reads=[stb], writes=[dst_buf])

        es1 = ExitStack()
        wkv = sbt(es1, "wkv", [128, 8, 1280], BF16)
        wq = sbt(es1, "wq", [128, 8, 1024], BF16)
        wkvB, wqB = Buf(), Buf()
        ctx1 = {"n": 0, "wn": 0}
        ctx1["wst"] = [(sbt(es1, "wst%d" % i, [128, 1024], F32), Buf(), P.dsem("dwst%d" % i)) for i in range(3)]

        with ExitStack() as s0:
            wa = sbt(s0, "wa", [128, 8, 3 * D], F32)
            idf = sbt(s0, "idf", [128, 128], F32)
            cT = sbt(s0, "cT", [128, 8, 2], F32)
            ec = sbt(s0, "ec", [128, 8, 2], F32)
            scT = sbt(s0, "scT", [128, 8, 2], F32)
            grep = [(sbt(s0, "grep%d" % i, [128, 128], F32), Buf()) for i in range(2)]
            badaT = sbt(s0, "badaT", [128, 24], F32)
            brow = sbt(s0, "brow", [1, 3 * D], F32)
            lamv = sbt(s0, "lamv", [128, 4, 64], F32)
            lprod = sbt(s0, "lprod", [128, 2, 64], F32)
            lsum = sbt(s0, "lsum", [128, 2], F32)
            lexp = sbt(s0, "lexp", [128, 2], F32)
            sgt = sbt(s0, "sgt", [128, 1], F32)
            mps = pst(s0, "mps", [128, 24, 2], F32)
            gps = pst(s0, "gps", [128, 2, 512], F32)
            b0 = {k: Buf() for k in ("wa", "idf", "cT", "ec", "scT", "screp", "badaT", "brow", "lamv",
                                     "lprod", "lsum", "lexp", "sgt", "mps", "gps0", "gps1",
                                     "ident", "ones", "gbt", "modT", "sc1", "gate", "neglam", "sg8", "epsc")}
            ds0 = [P.dsem("ds0_%d" % i) for i in range(4)]
            for c in range(8):
                P.do("sync", lambda e, c=c: e.dma_start(out=wa[:, c, :], in_=wada_d[c * 128:(c + 1) * 128, :]),
                     writes=[], dsem=ds0[0])
            b0["wa"].w = ds0[0]["last"]
            P.do("sync", lambda e: e.dma_start(out=idf[:], in_=ident_d), writes=[b0["idf"]], dsem=ds0[1])
            P.do("sync", lambda e: e.dma_start(out=cT[:], in_=cT_d), writes=[b0["cT"]], dsem=ds0[1])
            P.do("sync", lambda e: e.dma_start(out=badaT[:], in_=badaT_d), writes=[b0["badaT"]], dsem=ds0[1])
            P.do("sync", lambda e: e.dma_start(out=brow[:], in_=brow_d), writes=[b0["brow"]], dsem=ds0[1])
            P.do("sync", lambda e: e.dma_start(out=lamv[:], in_=lamv_d), writes=[b0["lamv"]], dsem=ds0[1])
            P.do("sync", lambda e: e.dma_start(out=sgt[:], in_=sg_d), writes=[b0["sgt"]], dsem=ds0[1])
            P.do("sync", lambda e: e.dma_start(out=gbt[:], in_=gb_d), writes=[b0["gbt"]], dsem=ds0[1])
            P.do("sync", lambda e: e.dma_start(out=sel[:], in_=sel_d), writes=[b0["gbt"]], dsem=ds0[1])
            small_last = ds0[1]["last"]
            for k in ("idf", "cT", "badaT", "brow", "lamv", "sgt", "gbt"):
                b0[k].w = small_last
            load_weight(ctx1, wkv, wkvB, wkv_d, 8, 1280, engines=("gpsimd",))
            load_weight(ctx1, wq, wqB, wq_d, 8, 1024, engines=("gpsimd",))
            P.do("vector", lambda e: e.tensor_copy(out=ident[:], in_=idf[:]), reads=[b0["idf"]], writes=[b0["ident"]])
            P.do("vector", lambda e: e.tensor_copy(out=sel_b[:], in_=sel[:]), reads=[b0["gbt"]], writes=[b0["ident"]])
            P.do("gpsimd", lambda e: e.memset(ones_b[:], 1.0), writes=[b0["ones"]])
            P.do("gpsimd", lambda e: e.memset(ones_f[:], 1.0), writes=[b0["ones"]])
            P.do("gpsimd", lambda e: e.memset(epsc[:], EPS), writes=[b0["epsc"]])
            if stop <= -3:
                P.barrier(); P.emit(); return nc
            P.do("scalar", lambda e: e.activation(out=ec[:], in_=cT[:], func=AF.Exp, scale=-1.0),
                 reads=[b0["cT"]], writes=[b0["ec"]])
            P.do("vector", lambda e: e.tensor_scalar(out=ec[:], in0=ec[:], scalar1=1.0, scalar2=None, op0=ALU.add),
                 reads=[b0["ec"]], writes=[b0["ec"]])
            P.do("vector", lambda e: e.reciprocal(out=ec[:], in_=ec[:]), reads=[b0["ec"]], writes=[b0["ec"]])
            P.do("vector", lambda e: e.tensor_tensor(out=scT[:], in0=cT[:], in1=ec[:], op=ALU.mult),
                 reads=[b0["ec"], b0["cT"]], writes=[b0["scT"]])
            if stop <= -2:
                P.barrier(); P.emit(); return nc
            for j in range(24):
                for c in range(8):
                    P.do("tensor", lambda e, j=j, c=c: e.matmul(mps[:, j, :], lhsT=wa[:, c, j * 128:(j + 1) * 128],
                                                              rhs=scT[:, c, :], start=(c == 0), stop=(c == 7)),
                         reads=[b0["wa"], b0["scT"]], writes=[b0["mps"]])
            P.do("vector", lambda e: e.tensor_tensor(out=modT[:], in0=mps[:],
                                                     in1=badaT[:].unsqueeze(2).to_broadcast([128, 24, 2]), op=ALU.add),
                 reads=[b0["mps"], b0["badaT"]], writes=[b0["modT"]])
            P.do("vector", lambda e: e.tensor_scalar(out=sc1[:], in0=modT[:, 8:16, :], scalar1=1.0, scalar2=None, op0=ALU.add),
                 reads=[b0["modT"]], writes=[b0["sc1"]])
            if stop <= -1:
                P.barrier(); P.emit(); return nc
            for s in range(2):
                for hb in range(2):
                    gk_ = "gps%d" % hb
                    for jj in range(4):
                        j = hb * 4 + jj
                        rp_, rpb_ = grep[j % 2]
                        P.do("vector", lambda e, rp_=rp_, j=j, s=s: e.tensor_copy(out=rp_[:], in_=modT[:, 16 + j, s:s + 1].to_broadcast([128, 128])),
                             reads=[b0["modT"]], writes=[rpb_])
                        P.do("tensor", lambda e, rp_=rp_, hb=hb, jj=jj: e.transpose(out=gps[:, hb, jj * 128:(jj + 1) * 128], in_=rp_[:], identity=idf[:]),
                             reads=[rpb_, b0["idf"]], writes=[b0[gk_]])
                    P.do("vector", lambda e, s=s, hb=hb: e.tensor_copy(out=gate_bc[:, s, hb * 512:(hb + 1) * 512], in_=gps[:, hb, :]),
                         reads=[b0[gk_]], writes=[b0["gate"]])
            if stop <= -0.5:
                P.barrier(); P.emit(); return nc
            P.do("vector", lambda e: e.tensor_tensor(out=lprod[:, 0, :], in0=lamv[:, 0, :], in1=lamv[:, 1, :], op=ALU.mult),
                 reads=[b0["lamv"]], writes=[b0["lprod"]])
            P.do("vector", lambda e: e.tensor_tensor(out=lprod[:, 1, :], in0=lamv[:, 2, :], in1=lamv[:, 3, :], op=ALU.mult),
                 reads=[b0["lamv"], b0["lprod"]], writes=[b0["lprod"]])
            P.do("vector", lambda e: e.tensor_reduce(out=lsum[:], in_=lprod[:], axis=AX.X, op=ALU.add),
                 reads=[b0["lprod"]], writes=[b0["lsum"]])
            P.do("scalar", lambda e: e.activation(out=lexp[:], in_=lsum[:], func=AF.Exp),
                 reads=[b0["lsum"]], writes=[b0["lexp"]])
            P.do("vector", lambda e: e.scalar_tensor_tensor(out=neglam[:], in0=lexp[:, 1:2], scalar=-LAM_INIT, in1=lexp[:, 0:1],
                                                            op0=ALU.add, op1=ALU.subtract),
                 reads=[b0["lexp"]], writes=[b0["neglam"]])
            P.do("vector", lambda e: e.tensor_scalar(out=sg8[:], in0=sgt[:], scalar1=1.0 - LAM_INIT, scalar2=None, op0=ALU.mult),
                 reads=[b0["sgt"]], writes=[b0["sg8"]])
            dbg("modT", modT[:], [128, 24, 2], [b0["modT"]])
            dbg("sc1", sc1[:], [128, 8, 2], [b0["sc1"]])
            dbg("gate", gate_bc[:], [128, 2, D], [b0["gate"]])
            pass
            P.barrier()
        if stop <= 0:
            P.emit()
            return nc

        def front_end(sc, ctx, x_ap, slot_idx, mslot, hT_dst, hT_buf, xt_keep=None):
            i = ctx["n"]
            ctx["n"] += 1
            xs = ctx["xt"][i % len(ctx["xt"])] if xt_keep is None else xt_keep
            xt, xb_, xd = xs
            P.do("sync", lambda e: e.dma_start(out=xt[:], in_=x_ap), writes=[xb_], dsem=xd)
            k = i % 2
            FE = int(os.environ.get("PH1_FE", "99"))
            ss, ssb = ctx["ss"][k]
            if FE < 1:
                return
            P.do("scalar", lambda e: e.activation(out=ctx["junk"][:], in_=xt[:], func=AF.Square, accum_out=ss[:]),
                 reads=[xb_], writes=[ctx["junkb"], ssb])
            if FE < 2:
                return
            P.do("scalar", lambda e: e.activation(out=ss[:], in_=ss[:], func=AF.Ln, scale=1.0 / D, bias=epsc[:, 0:1]),
                 reads=[ssb, cB], writes=[ssb])
            P.do("scalar", lambda e: e.activation(out=ss[:], in_=ss[:], func=AF.Exp, scale=-0.5), reads=[ssb], writes=[ssb])
            if FE < 3:
                return
            xn, xnb = ctx["xn"][k]
            P.do("vector", lambda e: e.scalar_tensor_tensor(out=xn[:], in0=xt[:], scalar=ss[:, 0:1], in1=gbt[:],
                                                            op0=ALU.mult, op1=ALU.mult),
                 reads=[xb_, ssb, cB], writes=[xnb])
            if FE < 4:
                return
            pT, pTb = ctx["psT"][k]
            for c in range(8):
                P.do("tensor", lambda e, c=c: e.transpose(out=pT[:, c, :], in_=xn[:, c * 128:(c + 1) * 128], identity=ident[:]),
                     reads=[xnb, cB], writes=[pTb])
            if FE < 5:
                return
            EV = os.environ.get("PH1_EV", "act")
            for c in range(8):
                if (c % 2 == 0 and EV == "both") or EV == "act":
                    P.do("scalar", lambda e, c=c: e.activation(out=hT_dst(c), in_=pT[:, c, :], func=AF.Identity,
                                                               scale=sc1[:, c, mslot:mslot + 1], bias=modT[:, c, mslot:mslot + 1]),
                         reads=[pTb, cB], writes=[hT_buf[0]])
                else:
                    P.do("vector", lambda e, c=c: e.tensor_scalar(out=hT_dst(c), in0=pT[:, c, :],
                                                                  scalar1=sc1[:, c, mslot:mslot + 1], scalar2=modT[:, c, mslot:mslot + 1],
                                                                  op0=ALU.mult, op1=ALU.add),
                         reads=[pTb, cB], writes=[hT_buf[1]])

        with ExitStack() as s1:
            sinT = sbt(s1, "sinT", [128, NTAB, 40], F32)
            cosT = sbt(s1, "cosT", [128, NTAB, 40], F32)
            gq = sbt(s1, "gq", [128, 16, 64], F32)
            gk = sbt(s1, "gk", [128, 10, 64], F32)
            tabB = Buf()
            with ExitStack() as sr:
                pos = sbt(sr, "pos", [128, NTAB], F32)
                jidx = sbt(sr, "jidx", [128, 24], F32)
                inv = sbt(sr, "inv", [128, 24], F32)
                posi = sbt(sr, "posi", [128, NTAB], I32)
                rowf = sbt(sr, "rowf", [128, NTAB], F32)
                colf = sbt(sr, "colf", [128, NTAB], F32)
                ang = sbt(sr, "ang", [128, NTAB, 40], F32)
                tq = sbt(sr, "tq", [128, NTAB, 40], F32)
                ki = sbt(sr, "ki", [128, NTAB, 40], I32)
                kf = sbt(sr, "kf", [128, NTAB, 40], F32)
                yy = sbt(sr, "yy", [128, NTAB, 40], F32)
                cc = tq
                rb = Buf()
                dsr = P.dsem("dsr")
                P.do("sync", lambda e: e.dma_start(out=pos[:], in_=pos_d), writes=[rb], dsem=dsr)
                P.do("sync", lambda e: e.dma_start(out=inv[:], in_=jidx_d), writes=[rb], dsem=dsr)
                P.do("sync", lambda e: e.dma_start(out=gq[:], in_=gq_d), writes=[rb], dsem=dsr)
                P.do("sync", lambda e: e.dma_start(out=gk[:], in_=gk_d), writes=[rb], dsem=dsr)
                V = lambda fn: P.do("vector", fn, reads=[rb], writes=[rb])
                A = lambda fn: P.do("scalar", fn, reads=[rb], writes=[rb])
                V(lambda e: e.tensor_scalar(out=posi[:], in0=pos[:], scalar1=1.0 / 64.0, scalar2=None, op0=ALU.mult))
                V(lambda e: e.tensor_copy(out=rowf[:], in_=posi[:]))
                V(lambda e: e.scalar_tensor_tensor(out=colf[:], in0=rowf[:], scalar=-64.0, in1=pos[:], op0=ALU.mult, op1=ALU.add))
                V(lambda e: e.tensor_scalar(out=tq[:, :, 0], in0=colf[:], scalar1=0.0, scalar2=None, op0=ALU.is_lt))
                V(lambda e: e.tensor_tensor(out=rowf[:], in0=rowf[:], in1=tq[:, :, 0], op=ALU.subtract))
                V(lambda e: e.scalar_tensor_tensor(out=colf[:], in0=tq[:, :, 0], scalar=64.0, in1=colf[:], op0=ALU.mult, op1=ALU.add))
                V(lambda e: e.tensor_tensor(out=ang[:, :, 0:16], in0=rowf[:].unsqueeze(2).to_broadcast([128, NTAB, 16]),
                                            in1=inv[:, 0:16].unsqueeze(1).to_broadcast([128, NTAB, 16]), op=ALU.mult))
                V(lambda e: e.tensor_tensor(out=ang[:, :, 16:32], in0=colf[:].unsqueeze(2).to_broadcast([128, NTAB, 16]),
                                            in1=inv[:, 0:16].unsqueeze(1).to_broadcast([128, NTAB, 16]), op=ALU.mult))
                V(lambda e: e.tensor_tensor(out=ang[:, :, 32:40], in0=pos[:].unsqueeze(2).to_broadcast([128, NTAB, 8]),
                                            in1=inv[:, 16:24].unsqueeze(1).to_broadcast([128, NTAB, 8]), op=ALU.mult))
                for (dst, shift) in ((sinT, 0.0), (cosT, 0.25)):
                    V(lambda e, shift=shift: e.tensor_scalar(out=tq[:], in0=ang[:], scalar1=1.0 / TWO_PI, scalar2=shift,
                                                             op0=ALU.mult, op1=ALU.add))
                    V(lambda e: e.tensor_copy(out=ki[:], in_=tq[:]))
                    V(lambda e: e.tensor_copy(out=kf[:], in_=ki[:]))
                    V(lambda e: e.scalar_tensor_tensor(out=yy[:], in0=kf[:], scalar=-CW1, in1=ang[:], op0=ALU.mult, op1=ALU.add))
                    V(lambda e: e.scalar_tensor_tensor(out=yy[:], in0=kf[:], scalar=-CW2, in1=yy[:], op0=ALU.mult, op1=ALU.add))
                    if shift != 0.0:
                        V(lambda e: e.tensor_scalar(out=yy[:], in0=yy[:], scalar1=0.5 * math.pi, scalar2=None, op0=ALU.add))
                    V(lambda e: e.tensor_scalar(out=cc[:], in0=yy[:], scalar1=math.pi, scalar2=-TWO_PI, op0=ALU.is_gt, op1=ALU.mult))
                    V(lambda e: e.tensor_tensor(out=yy[:], in0=yy[:], in1=cc[:], op=ALU.add))
                    V(lambda e: e.tensor_scalar(out=cc[:], in0=yy[:], scalar1=-math.pi, scalar2=TWO_PI, op0=ALU.is_lt, op1=ALU.mult))
                    V(lambda e: e.tensor_tensor(out=yy[:], in0=yy[:], in1=cc[:], op=ALU.add))
                    V(lambda e: e.tensor_scalar(out=yy[:], in0=yy[:], scalar1=3.1415925, scalar2=-3.1415925, op0=ALU.min, op1=ALU.max))
                    A(lambda e, dst=dst: e.activation(out=dst[:], in_=yy[:], func=AF.Sin))
                dbg("sinT", sinT[:], [128, NTAB, 40], [rb])
                dbg("cosT", cosT[:], [128, NTAB, 40], [rb])
                P.barrier()
            if stop <= 0.5:
                P.emit()
                return nc

            NF = 4
            NX = 6
            xts = [(sbt(s1, "xt%d" % i, [128, D], F32), Buf(), P.dsem("dxt%d" % i)) for i in range(NX)]
            sss = [(sbt(s1, "ss%d" % i, [128, 1], F32), Buf()) for i in range(NF)]
            junk = sbt(s1, "junk", [128, D], BF16)
            junkb = Buf()
            xns = [(sbt(s1, "xn%d" % i, [128, D], BF16), Buf()) for i in range(NF)]
            psT = (pst(s1, "psT", [128, 8, 128], BF16), Buf())
            hTs = [(sbt(s1, "hT%d" % i, [128, 8, 128], BF16), Buf()) for i in range(NF)]
            pps = [(pst(s1, "pp%d" % i, [128, 1536], F32), Buf()) for i in range(2)]
            pkT = (pst(s1, "pkT", [128, 8, 128], BF16), Buf())
            sqs = [(sbt(s1, "sq%d" % i, [128, 1024], F32), Buf()) for i in range(2)]
            ssk = [(sbt(s1, "ssk%d" % i, [128, 16], F32), Buf()) for i in range(NF)]
            xg = [(sbt(s1, "xg%d" % i, [128, 1024], F32), Buf()) for i in range(NF)]
            rts = [[(sbt(s1, "rt%d_%d" % (k, i), [128, 256], F32), Buf()) for i in range(4)] for k in range(2)]
            ybf = [(sbt(s1, "ybf%d" % i, [128, 1024], BF16), Buf(), Buf()) for i in range(NF)]
            vst = [(sbt(s1, "vst%d" % i, [128, 640], BF16), Buf(), P.dsem("dvst%d" % i)) for i in range(2)]
            kst = [(sbt(s1, "kst%d" % i, [128, 8, 512], BF16), Buf(), P.dsem("dkst%d" % i)) for i in range(2)]

            jobs = []
            for s in range(2):
                for t in range(NT_K):
                    jobs.append(("kv", x_srcs[s][t * 128:(t + 1) * 128, :], t, s, s, t))
            for t in range(NT_Q):
                jobs.append(("q", xq_d[t * 128:(t + 1) * 128, :], NT_K + t, 0 if t < 16 else 1, None, t))
            if os.environ.get("PH1_JOBS"):
                jobs = jobs[:int(os.environ["PH1_JOBS"])]
            n_jobs = len(jobs)

            def st_load(ji):
                xt, xb_, xd = xts[ji % NX]
                x_ap = jobs[ji][1]
                P.do("sync", lambda e: e.dma_start(out=xt[:], in_=x_ap), writes=[xb_], dsem=xd)

            def st_stats(ji):
                kind, x_ap, tab, mslot, sslot, t = jobs[ji]
                xt, xb_, xd = xts[ji % NX]
                ss, ssb = sss[ji % NF]
                P.do("scalar", lambda e: e.activation(out=junk[:], in_=xt[:], func=AF.Square, accum_out=ss[:]),
                     reads=[xb_], writes=[junkb, ssb])
                P.do("scalar", lambda e: e.activation(out=ss[:], in_=ss[:], func=AF.Ln, scale=1.0 / D, bias=epsc[:, 0:1]),
                     reads=[ssb, cB], writes=[ssb])
                P.do("scalar", lambda e: e.activation(out=ss[:], in_=ss[:], func=AF.Exp, scale=-0.5), reads=[ssb], writes=[ssb])
                if kind == "q":
                    P.do("scalar", lambda e: e.activation(out=rstd_own[:, t:t + 1], in_=ss[:], func=AF.Copy), reads=[ssb], writes=[rownB])
                xn, xnb = xns[ji % NF]
                P.do("vector", lambda e: e.scalar_tensor_tensor(out=xn[:], in0=xt[:], scalar=ss[:, 0:1], in1=gbt[:],
                                                                op0=ALU.mult, op1=ALU.mult),
                     reads=[xb_, ssb, cB], writes=[xnb])

            def st_trans(ji):
                kind, x_ap, tab, mslot, sslot, t = jobs[ji]
                xn, xnb = xns[ji % NF]
                pT, pTb = psT
                for c in range(8):
                    P.do("tensor", lambda e, c=c: e.transpose(out=pT[:, c, :], in_=xn[:, c * 128:(c + 1) * 128], identity=ident[:]),
                         reads=[xnb, cB], writes=[pTb])
                hT, hTb = hTs[ji % NF]
                for c in range(8):
                    if ji % 2 == 0:
                        P.do("scalar", lambda e, c=c: e.activation(out=hT[:, c, :], in_=pT[:, c, :], func=AF.Identity,
                                                                   scale=sc1[:, c, mslot:mslot + 1], bias=modT[:, c, mslot:mslot + 1]),
                             reads=[pTb, cB], writes=[hTb])
                    else:
                        P.do("vector", lambda e, c=c: e.tensor_scalar(out=hT[:, c, :], in0=pT[:, c, :],
                                                                      scalar1=sc1[:, c, mslot:mslot + 1], scalar2=modT[:, c, mslot:mslot + 1],
                                                                      op0=ALU.mult, op1=ALU.add),
                             reads=[pTb, cB], writes=[hTb])

            def st_proj(ji):
                kind = jobs[ji][0]
                hT, hTb = hTs[ji % NF]
                pp, ppB = pps[ji % 2]
                w, wB, ncol = (wkv, wkvB, 1280) if kind == "kv" else (wq, wqB, 1024)
                for c0 in range(0, ncol, 512):
                    cw = min(512, ncol - c0)
                    for c in range(8):
                        P.do("tensor", lambda e, c=c, c0=c0, cw=cw: e.matmul(
                            pp[:, c0:c0 + cw], lhsT=hT[:, c, :], rhs=w[:, c, c0:c0 + cw], start=(c == 0), stop=(c == 7)),
                            reads=[hTb, wB], writes=[ppB])

            def st_b1(ji, part):
                kind, x_ap, tab, mslot, sslot, t = jobs[ji]
                pp, ppB = pps[ji % 2]
                if kind == "kv":
                    nh, nf, gain, Arange, Brange = 10, 640, gk, (8, 2), (0, 8)
                else:
                    nh, nf, gain, Arange, Brange = 16, 1024, gq, (0, 8), (8, 8)
                sq, sqB = sqs[ji % 2]
                sk, skb = ssk[ji % NF]
                xgt, xgb = xg[ji % NF]
                X = xgt[:, 0:nf].rearrange("p (h d) -> p h d", d=64)
                yt, yb, yb2 = ybf[ji % NF]
                Y = yt[:, 0:nf].rearrange("p (h d) -> p h d", d=64)
                if part == "act1":
                    if kind == "kv":
                        vs, vsb, vsd = vst[t % 2]
                        P.do("scalar", lambda e: e.activation(out=vs[:], in_=pp[:, 640:1280], func=AF.Copy),
                             reads=[ppB], writes=[vsb])
                        P.do("sync", lambda e: e.dma_start(out=VS_d[sslot][t * 128:(t + 1) * 128, :], in_=vs[:]),
                             reads=[vsb], writes=[VS_b[sslot]], dsem=vsd)
                    P.do("scalar", lambda e: e.activation(out=sq[:, 0:nf], in_=pp[:, 0:nf], func=AF.Square),
                         reads=[ppB], writes=[sqB])
                    return
                if part == "dve1":
                    P.do("vector", lambda e: e.tensor_reduce(out=sk[:, 0:nh], in_=sq[:, 0:nf].rearrange("p (h d) -> p h d", d=64),
                                                             axis=AX.X, op=ALU.add),
                         reads=[sqB], writes=[skb])
                    return
                if part == "act2":
                    P.do("scalar", lambda e: e.activation(out=sk[:, 0:nh], in_=sk[:, 0:nh], func=AF.Ln, scale=1.0 / 64.0, bias=epsc[:, 0:1]),
                         reads=[skb, cB], writes=[skb])
                    P.do("scalar", lambda e: e.activation(out=sk[:, 0:nh], in_=sk[:, 0:nh], func=AF.Exp, scale=-0.5),
                         reads=[skb], writes=[skb])
                    return
                if part == "dve2":
                    P.do("vector", lambda e: e.tensor_tensor(out=X, in0=pp[:, 0:nf].rearrange("p (h d) -> p h d", d=64),
                                                             in1=sk[:, 0:nh].unsqueeze(2).to_broadcast([128, nh, 64]), op=ALU.mult),
                         reads=[ppB, skb], writes=[xgb])
                    P.do("vector", lambda e: e.tensor_tensor(out=X, in0=X, in1=gain[:], op=ALU.mult),
                         reads=[xgb, tabB], writes=[xgb])
                    return
                if part == "cast":
                    P.do("scalar", lambda e: e.activation(out=Y[:, Brange[0]:Brange[0] + 8, 16:64], in_=X[:, Brange[0]:Brange[0] + 8, 16:64],
                                                          func=AF.Copy),
                         reads=[xgb], writes=[yb2])
                    return
                assert part == "rope"
                rt = rts[ji % 2]
                for (a_mode, (H0, H), ti) in ((True, Arange, 0), (False, Brange, 2)):
                    (t0, tb0), (t1, tb1) = rt[ti], rt[ti + 1]
                    if a_mode:
                        Xv = X[:, H0:H0 + H, :].rearrange("p h (a f j) -> p h a f j", a=2, f=2)
                        Yv = Y[:, H0:H0 + H, :].rearrange("p h (a f j) -> p h a f j", a=2, f=2)
                        X1, X2 = Xv[:, :, :, 0, :], Xv[:, :, :, 1, :]
                        Y1, Y2 = Yv[:, :, :, 0, :], Yv[:, :, :, 1, :]
                        shp = [128, H, 2, 16]
                        cs = cosT[:, tab, 0:32].rearrange("p (a j) -> p a j", a=2).unsqueeze(1).to_broadcast(shp)
                        sn = sinT[:, tab, 0:32].rearrange("p (a j) -> p a j", a=2).unsqueeze(1).to_broadcast(shp)
                        n = H * 32
                        T0 = t0[:, 0:n].rearrange("p (h a j) -> p h a j", h=H, a=2)
                        T1 = t1[:, 0:n].rearrange("p (h a j) -> p h a j", h=H, a=2)
                    else:
                        X1, X2 = X[:, H0:H0 + H, 0:8], X[:, H0:H0 + H, 8:16]
                        Y1, Y2 = Y[:, H0:H0 + H, 0:8], Y[:, H0:H0 + H, 8:16]
                        shp = [128, H, 8]
                        cs = cosT[:, tab, 32:40].unsqueeze(1).to_broadcast(shp)
                        sn = sinT[:, tab, 32:40].unsqueeze(1).to_broadcast(shp)
                        n = H * 8
                        T0 = t0[:, 0:n].rearrange("p (h j) -> p h j", h=H)
                        T1 = t1[:, 0:n].rearrange("p (h j) -> p h j", h=H)
                    G = lambda fn, rd, wr: P.do("gpsimd", fn, reads=rd, writes=wr)
                    G(lambda e, T0=T0, X1=X1, cs=cs: e.tensor_tensor(out=T0, in0=X1, in1=cs, op=ALU.mult), [xgb, tabB], [tb0])
                    G(lambda e, T1=T1, X2=X2, sn=sn: e.tensor_tensor(out=T1, in0=X2, in1=sn, op=ALU.mult), [xgb, tabB], [tb1])
                    G(lambda e, T0=T0, T1=T1, Y1=Y1: e.tensor_tensor(out=Y1, in0=T0, in1=T1, op=ALU.subtract), [tb0, tb1], [yb])
                    G(lambda e, T0=T0, X2=X2, cs=cs: e.tensor_tensor(out=T0, in0=X2, in1=cs, op=ALU.mult), [xgb, tabB, tb0], [tb0])
                    G(lambda e, T1=T1, X1=X1, sn=sn: e.tensor_tensor(out=T1, in0=X1, in1=sn, op=ALU.mult), [xgb, tabB, tb1], [tb1])
                    G(lambda e, T0=T0, T1=T1, Y2=Y2: e.tensor_tensor(out=Y2, in0=T0, in1=T1, op=ALU.add), [tb0, tb1], [yb])

            def st_b2(ji):
                kind, x_ap, tab, mslot, sslot, t = jobs[ji]
                nf = 640 if kind == "kv" else 1024
                nch = nf // 128
                yt, yb, yb2 = ybf[ji % NF]
                pk, pkb = pkT
                for c in range(nch):
                    P.do("tensor", lambda e, c=c: e.transpose(out=pk[:, c, :], in_=yt[:, c * 128:(c + 1) * 128], identity=ident[:]),
                         reads=[yb, yb2, cB], writes=[pkb])
                g = t // 4
                ks, ksb, ksd = kst[g % 2]
                P.do("vector", lambda e: e.tensor_copy(out=ks[:, 0:nch, (t % 4) * 128:(t % 4 + 1) * 128], in_=pk[:, 0:nch, :]),
                     reads=[pkb], writes=[ksb])
                if t % 4 == 3:
                    if kind == "kv":
                        dst, dB = KT_d[sslot].rearrange("(c p) t -> p c t", p=128)[:, :, g * 512:(g + 1) * 512], KT_b[sslot]
                    else:
                        dst, dB = QT_d.rearrange("(c p) t -> p c t", p=128)[:, :, g * 512:(g + 1) * 512], QT_b
                    P.do("sync", lambda e: e.dma_start(out=dst, in_=ks[:, 0:nch, :]), reads=[ksb], writes=[dB], dsem=ksd)

            LA = 5
            ok = lambda j: 0 <= j < n_jobs
            for ji in range(min(LA, n_jobs)):
                st_load(ji)
            for it in range(-3, n_jobs + 3):
                if ok(it + LA) and it + LA >= LA:
                    st_load(it + LA)
                if ok(it - 1):
                    st_b1(it - 1, "cast")
                if ok(it):
                    st_b1(it, "act1")
                    st_b1(it, "dve1")
                if ok(it + 3):
                    st_stats(it + 3)
                if ok(it):
                    st_b1(it, "act2")
                    st_b1(it, "dve2")
                if ok(it + 2):
                    st_trans(it + 2)
                if ok(it + 1):
                    st_proj(it + 1)
                if ok(it):
                    st_b1(it, "rope")
                if ok(it - 2):
                    st_b2(it - 2)
            P.barrier()
        es1.close()
        if stop <= 1:
            P.emit()
            return nc

        es3 = ExitStack()
        wz = sbt(es3, "wz", [128, 8, 1024], BF16)
        wg = sbt(es3, "wg", [128, 8, 2048], BF16)
        wpa = sbt(es3, "wpa", [128, 4, 1024], BF16)
        wpb = sbt(es3, "wpb", [128, 4, 1024], BF16)
        wo = sbt(es3, "wo", [128, 8, 1024], BF16)
        wzB, wgB, wpaB, wpbB, woB = Buf(), Buf(), Buf(), Buf(), Buf()
        with ExitStack() as s2:
            NBQ = 2048
            ctx3 = {"n": 0, "wn": 0}
            ctx3["wst"] = [(sbt(s2, "w3st%d" % i, [128, 1024], F32), Buf(), P.dsem("dw3st%d" % i)) for i in range(2)]
            ub = []
            for i in range(2):
                u = {"K": sbt(s2, "uK%d" % i, [128, S], BF16), "V": sbt(s2, "uV%d" % i, [128, NT_K, 128], BF16),
                     "Q": sbt(s2, "uQ%d" % i, [128, NBQ], BF16),
                     "Kb": [Buf() for _ in range(4)], "Vb": [Buf() for _ in range(4)], "Qb": Buf(),
                     "Kd": [P.dsem("dK%d_%d" % (i, j)) for j in range(4)],
                     "Vd": [P.dsem("dV%d_%d" % (i, j)) for j in range(4)], "Qd": P.dsem("dQ%d" % i)}
                ub.append(u)
            Sps = [(pst(s2, "Sps%d" % i, [128, 1024], F32), Buf()) for i in range(2)]
            acc = [(pst(s2, "acc%d" % i, [128, 512], F32), Buf()) for i in range(4)]
            Pt = [(sbt(s2, "Pt%d" % i, [128, 1024], BF16), Buf()) for i in range(4)]
            rinv = [(sbt(s2, "rinv%d" % i, [128, 512], F32), Buf()) for i in range(2)]
            tt = [(sbt(s2, "tt%d" % i, [128, 512], F32), Buf()) for i in range(2)]
            ob_ = (sbt(s2, "ob", [128, 512], F32), Buf())
            ocp = [(sbt(s2, "ocp%d" % i, [128, 512], F32), Buf()) for i in range(2)]
            smc = (sbt(s2, "smc", [128, 512], F32), Buf())
            smcA = smc
            rinvA = rinv[1]
            hl = [(sbt(s2, "hl%d" % i, [128, 512], BF16), Buf()) for i in range(2)]

            def split_hl(src, srcb):
                (h_, hb_), (l_, lb_) = hl
                P.do("vector", lambda e: e.tensor_copy(out=h_[:], in_=src[:]), reads=[srcb], writes=[hb_])
                P.do("vector", lambda e: e.tensor_tensor(out=l_[:], in0=src[:], in1=h_[:], op=ALU.subtract),
                     reads=[srcb, hb_], writes=[lb_])

            def mm_hl(out_ap, out_b, lhsT_ap):
                (h_, hb_), (l_, lb_) = hl
                P.do("tensor", lambda e: e.matmul(out_ap, lhsT=lhsT_ap, rhs=h_[:], start=True, stop=False),
                     reads=[hb_, cB], writes=[out_b])
                P.do("tensor", lambda e: e.matmul(out_ap, lhsT=lhsT_ap, rhs=l_[:], start=False, stop=True),
                     reads=[lb_, cB], writes=[out_b])
            sqo = (sbt(s2, "sqo", [128, 512], F32), Buf())
            rs = (sbt(s2, "rs", [128, 512], F32), Buf())
            ost = [(sbt(s2, "ost%d" % i, [128, 512], BF16), Buf(), P.dsem("dost%d" % i)) for i in range(2)]

            units = []
            for qi in range(3):
                slot = 0 if qi == 0 else 1
                for u in range(4):
                    units.append((qi, slot, "A", u))
                for h in range(4):
                    units.append((qi, slot, "B", h))

            def load_unit(n):
                qi, slot, kind, idx = units[n]
                u = ub[n % 2]
                kchunk = 4 if kind == "A" else idx
                qchunk = idx if kind == "A" else 4 + idx
                f0 = 0 if kind == "A" else 128 + idx * 128
                P.do("sync", lambda e: e.dma_start(out=u["Q"][:], in_=QT_d[qchunk * 128:(qchunk + 1) * 128, qi * NBQ:(qi + 1) * NBQ]),
                     reads=[QT_b], writes=[u["Qb"]], dsem=u["Qd"])
                for j in range(4):
                    P.do("sync", lambda e, j=j: e.dma_start(out=u["K"][:, j * 2048:(j + 1) * 2048],
                                                           in_=KT_d[slot][kchunk * 128:(kchunk + 1) * 128, j * 2048:(j + 1) * 2048]),
                         reads=[KT_b[slot]], writes=[u["Kb"][j]], dsem=u["Kd"][j])
                    P.do("sync", lambda e, j=j: e.dma_start(
                        out=u["V"][:, j * 16:(j + 1) * 16, :],
                        in_=VS_d[slot].rearrange("(kc p) f -> p kc f", p=128)[:, j * 16:(j + 1) * 16, f0:f0 + 128]),
                        reads=[VS_b[slot]], writes=[u["Vb"][j]], dsem=u["Vd"][j])

            steps = []
            for n in range(len(units)):
                for qb in range(4):
                    for kc in range(NT_K):
                        steps.append((n, qb, kc))
            nsteps = len(steps)
            cnt = {"ost": 0, "A": 0}
            pending = []
            wthunks = []

            def emit_qk(i):
                n, qb, kc = steps[i]
                u = ub[n % 2]
                Sp, Sb = Sps[i % 2]
                for r in range(2):
                    P.do("tensor", lambda e, r=r: e.matmul(
                        Sp[:, r * 512:(r + 1) * 512], lhsT=u["K"][r * 64:(r + 1) * 64, kc * 128:(kc + 1) * 128],
                        rhs=u["Q"][r * 64:(r + 1) * 64, qb * 512:(qb + 1) * 512], start=True, stop=True),
                        reads=[u["Kb"][kc // 16], u["Qb"]], writes=[Sb])

            def emit_exp(i):
                Sp, Sb = Sps[i % 2]
                pt, pb = Pt[i % 4]
                P.do("scalar", lambda e: e.activation(out=pt[:], in_=Sp[:], func=AF.Exp, scale=0.125),
                     reads=[Sb], writes=[pb])

            def emit_pv(i):
                n, qb, kc = steps[i]
                qi, slot, kind, idx = units[n]
                u = ub[n % 2]
                pt, pb = Pt[i % 4]
                first, last = (kc == 0), (kc == NT_K - 1)
                vb = u["Vb"][kc // 16]
                if kind == "A":
                    a0 = 2 * (qb % 2)
                    (ao, aob), (asm, asb) = acc[a0], acc[a0 + 1]
                    for r in range(2):
                        P.do("tensor", lambda e, r=r: e.matmul(
                            ao[r * 64:(r + 1) * 64, :], lhsT=u["V"][:, kc, r * 64:(r + 1) * 64], rhs=pt[:, r * 512:(r + 1) * 512],
                            start=first, stop=last, tile_position=(0, r * 64)), reads=[vb, pb], writes=[aob])
                    if kc % 2 == 1:
                        ptp, pbp = Pt[(i - 1) % 4]
                        pfirst, plast = (kc == 1), (kc == NT_K - 1)
                        for jq in range(4):
                            src, srcb = (ptp, pbp) if jq < 2 else (pt, pb)
                            mq = jq % 2
                            P.do("tensor", lambda e, jq=jq, src=src, mq=mq: e.matmul(
                                asm[32 * jq:32 * jq + 32, :], lhsT=ones_b[:, 0:32], rhs=src[:, mq * 512:(mq + 1) * 512],
                                start=pfirst, stop=plast, tile_position=(0, 32 * jq)), reads=[cB, srcb], writes=[asb])
                    if last:
                        smc_, smcb = smcA
                        P.do("vector", lambda e: e.tensor_copy(out=smc_[:], in_=asm[:]), reads=[asb], writes=[smcb])
                        split_hl(smc_, smcb)
                        r0 = idx * 128
                        t0 = qi * NBQ + qb * 512

                        def finA():
                            mm_hl(asm[:], asb, sel_b[:, 256:384])
                            ri, rib = rinvA
                            P.do("vector", lambda e: e.reciprocal(out=ri[:], in_=asm[:]), reads=[asb], writes=[rib])
                            os_, osb, osd = ost[cnt["ost"] % 2]
                            cnt["ost"] += 1
                            P.do("vector", lambda e: e.tensor_tensor(out=os_[:], in0=ao[:], in1=ri[:], op=ALU.mult),
                                 reads=[aob, rib], writes=[osb])
                            P.do("sync", lambda e: e.dma_start(out=OT_d[r0:r0 + 128, t0:t0 + 512], in_=os_[:]),
                                 reads=[osb], writes=[OT_b], dsem=osd)

                        nxt_is_b = (qb == 3 and n + 1 < len(units) and units[n + 1][2] == "B")
                        if nxt_is_b:
                            finA()
                        else:
                            pending.append((i + 5, finA))
                else:
                    for m in range(2):
                        ao, aob = acc[m]
                        P.do("tensor", lambda e, m=m, ao=ao: e.matmul(ao[:], lhsT=u["V"][:, kc, :], rhs=pt[:, m * 512:(m + 1) * 512],
                                                                    start=first, stop=last), reads=[vb, pb], writes=[aob])
                    asm, asb = acc[2]
                    if kc % 2 == 1:
                        ptp, pbp = Pt[(i - 1) % 4]
                        pfirst, plast = (kc == 1), (kc == NT_K - 1)
                        for jq in range(4):
                            src, srcb = (ptp, pbp) if jq < 2 else (pt, pb)
                            mq = jq % 2
                            P.do("tensor", lambda e, jq=jq, src=src, mq=mq: e.matmul(
                                asm[32 * jq:32 * jq + 32, :], lhsT=ones_b[:, 0:32], rhs=src[:, mq * 512:(mq + 1) * 512],
                                start=pfirst, stop=plast, tile_position=(0, 32 * jq)), reads=[cB, srcb], writes=[asb])
                    if last:
                        smc_, smcb = smc
                        for m in range(2):
                            ao, aob = acc[m]
                            oc, ocb = ocp[m]
                            P.do("vector", lambda e, oc=oc, ao=ao: e.tensor_copy(out=oc[:], in_=ao[:]), reads=[aob], writes=[ocb])
                        P.do("vector", lambda e: e.tensor_copy(out=smc_[:], in_=asm[:]), reads=[asb], writes=[smcb])
                        split_hl(smc_, smcb)
                        r0 = 512 + idx * 128
                        t0 = qi * NBQ + qb * 512

                        def fin2m(m):
                            rp, rpb = acc[3]
                            mm_hl(rp[:], rpb, sel_b[:, m * 128:(m + 1) * 128])
                            ri, rib = rinv[m]
                            P.do("vector", lambda e: e.reciprocal(out=ri[:], in_=rp[:]), reads=[rpb], writes=[rib])
                            oc, ocb = ocp[m]
                            t_, tb_ = tt[m]
                            P.do("vector", lambda e: e.tensor_tensor(out=t_[:], in0=oc[:], in1=ri[:], op=ALU.mult),
                                 reads=[ocb, rib], writes=[tb_])

                        def fin2b():
                            o_, o_b = ob_
                            P.do("vector", lambda e: e.scalar_tensor_tensor(out=o_[:], in0=tt[1][0][:], scalar=neglam[:, 0:1], in1=tt[0][0][:],
                                                                            op0=ALU.mult, op1=ALU.add),
                                 reads=[tt[0][1], tt[1][1], cB], writes=[o_b])
                            sq_, sq_b = sqo
                            P.do("gpsimd", lambda e: e.tensor_tensor(out=sq_[:], in0=o_[:], in1=o_[:], op=ALU.mult),
                                 reads=[o_b], writes=[sq_b])
                            split_hl(sq_, sq_b)

                        def fin2c():
                            rp, rpb = acc[3]
                            mm_hl(rp[:], rpb, ones_b[:])
                            r_, r_b = rs
                            P.do("vector", lambda e: e.tensor_scalar(out=r_[:], in0=rp[:], scalar1=1.0 / 128.0, scalar2=EPS,
                                                                     op0=ALU.mult, op1=ALU.add), reads=[rpb], writes=[r_b])

                        def fin3a():
                            r_, r_b = rs
                            P.do("scalar", lambda e: e.activation(out=r_[:], in_=r_[:], func=AF.Ln), reads=[r_b], writes=[r_b])

                        def fin3():
                            o_, o_b = ob_
                            r_, r_b = rs
                            P.do("scalar", lambda e: e.activation(out=r_[:], in_=r_[:], func=AF.Exp, scale=-0.5), reads=[r_b], writes=[r_b])
                            os_, osb, osd = ost[cnt["ost"] % 2]
                            cnt["ost"] += 1
                            P.do("vector", lambda e: e.scalar_tensor_tensor(out=os_[:], in0=o_[:], scalar=sg8[:, 0:1], in1=r_[:],
                                                                            op0=ALU.mult, op1=ALU.mult),
                                 reads=[o_b, r_b, cB], writes=[osb])
                            P.do("sync", lambda e: e.dma_start(out=OT_d[r0:r0 + 128, t0:t0 + 512], in_=os_[:]),
                                 reads=[osb], writes=[OT_b], dsem=osd)

                        pending.append((i + 4, lambda: fin2m(0)))
                        pending.append((i + 7, lambda: fin2m(1)))
                        pending.append((i + 10, fin2b))
                        pending.append((i + 15, fin2c))
                        pending.append((i + 19, fin3a))
                        pending.append((i + 24, fin3))

            load_unit(0)
            for (w_, wB_, wd_, nch_, ncol_) in ((wz, wzB, wz_d, 8, 1024), (wg, wgB, wg_d, 8, 2048), (wpa, wpaB, wpa_d, 4, 1024),
                                                (wpb, wpbB, wpb_d, 4, 1024), (wo, woB, wout_d, 8, 1024)):
                load_weight(ctx3, w_, wB_, wd_, nch_, ncol_, engines=("gpsimd",), defer=wthunks)
            emit_qk(0)
            emit_qk(1)
            for i in range(nsteps):
                n, qb, kc = steps[i]
                if qb == 0 and kc == 0 and n + 1 < len(units):
                    load_unit(n + 1)
                emit_exp(i)
                if i + 2 < nsteps:
                    emit_qk(i + 2)
                emit_pv(i)
                if wthunks and i >= 4 and i % 3 == 0:
                    wthunks.pop(0)()
                while pending and pending[0][0] <= i:
                    pending.pop(0)[1]()
            while pending:
                pending.pop(0)[1]()
            while wthunks:
                wthunks.pop(0)()
            P.barrier()
        if stop <= 2:
            P.emit()
            return nc

        with ExitStack() as s3:
            NXT = 8
            xts = [(sbt(s3, "x3t%d" % i, [128, D], F32), Buf(), P.dsem("dx3t%d" % i)) for i in range(NXT)]
            sss = [(sbt(s3, "s3s%d" % i, [128, 1], F32), Buf()) for i in range(4)]
            junk3 = sbt(s3, "junk3", [128, D], BF16)
            junk3b = Buf()
            xns = [(sbt(s3, "x3n%d" % i, [128, D], BF16), Buf()) for i in range(4)]
            psTs = [(pst(s3, "ps3T%d" % i, [128, 8, 128], BF16), Buf()) for i in range(2)]
            hT4s = [(sbt(s3, "hT4_%d" % i, [128, 8, 512], BF16), [Buf(), Buf()]) for i in range(2)]
            On = sbt(s3, "On", [128, 8, 512], BF16)
            OnB, Ond = Buf(), P.dsem("dOn")
            og = sbt(s3, "og", [128, 8, 512], BF16)
            ogB = Buf()
            szt = [(sbt(s3, "szt%d" % i, [128, 512], BF16), Buf()) for i in range(2)]
            sgt_ = [(sbt(s3, "sgs%d" % i, [128, 512], F32), Buf()) for i in range(4)]
            m12 = [(sbt(s3, "m12_%d" % i, [128, 512], F32), Buf()) for i in range(2)]
            mg = sbt(s3, "mg", [128, 8, 512], BF16)
            mgB = Buf()
            ysb = [(sbt(s3, "ysb%d" % i, [128, D], F32), Buf(), P.dsem("dysb%d" % i)) for i in range(3)]
            pz = [(pst(s3, "pz%d" % i, [128, 512], F32), Buf()) for i in range(2)]
            pab = [(pst(s3, "pab%d" % i, [128, 512], F32), Buf()) for i in range(2)]
            pout = [(pst(s3, "pout%d" % i, [128, 512], F32), Buf()) for i in range(2)]
            NG = NQ // 512
            if os.environ.get("PH3_GROUPS"):
                NG = int(os.environ["PH3_GROUPS"])
            pzn = [0]

            def g_mslot(g):
                return 0 if g < 4 else 1

            def p3_load_x(g):
                for tl in range(4):
                    xt, xb_, xd = xts[(g * 4 + tl) % NXT]
                    r0 = g * 512 + tl * 128
                    P.do("sync", lambda e, xt=xt, r0=r0: e.dma_start(out=xt[:], in_=xq_d[r0:r0 + 128, :]), writes=[xb_], dsem=xd)

            def p3_load_on(g):
                t0 = g * 512
                P.do("sync", lambda e: e.dma_start(out=On[:], in_=OT_d.rearrange("(c p) t -> p c t", p=128)[:, :, t0:t0 + 512]),
                     reads=[OT_b], writes=[OnB], dsem=Ond)

            def p3_stats(g):
                for tl in range(4):
                    xt, xb_, xd = xts[(g * 4 + tl) % NXT]
                    col = g * 4 + tl
                    xn, xnb = xns[tl]
                    P.do("vector", lambda e, xn=xn, xt=xt, col=col: e.scalar_tensor_tensor(
                        out=xn[:], in0=xt[:], scalar=rstd_own[:, col:col + 1], in1=gbt[:], op0=ALU.mult, op1=ALU.mult),
                        reads=[xb_, cB], writes=[xnb])

            def p3_trans(g, tiles=(0, 1, 2, 3)):
                mslot = g_mslot(g)
                hT4, hT4B = hT4s[g % 2]
                for tl in tiles:
                    xn, xnb = xns[tl]
                    pT, pTb = psTs[tl % 2]
                    for c in range(8):
                        P.do("tensor", lambda e, c=c, pT=pT, xn=xn: e.transpose(out=pT[:, c, :], in_=xn[:, c * 128:(c + 1) * 128], identity=ident[:]),
                             reads=[xnb, cB], writes=[pTb])
                    for c in range(8):
                        dst = hT4[:, c, tl * 128:(tl + 1) * 128]
                        if tl % 2 == 0:
                            P.do("scalar", lambda e, c=c, pT=pT, dst=dst: e.activation(
                                out=dst, in_=pT[:, c, :], func=AF.Identity, scale=sc1[:, c, mslot:mslot + 1], bias=modT[:, c, mslot:mslot + 1]),
                                reads=[pTb, cB], writes=[hT4B[0]])
                        else:
                            P.do("vector", lambda e, c=c, pT=pT, dst=dst: e.tensor_scalar(
                                out=dst, in0=pT[:, c, :], scalar1=sc1[:, c, mslot:mslot + 1], scalar2=modT[:, c, mslot:mslot + 1],
                                op0=ALU.mult, op1=ALU.add), reads=[pTb, cB], writes=[hT4B[1]])

            def p3_body_a(g):
                hT4, hT4B = hT4s[g % 2]
                for j in range(8):
                    p_, pb_ = pz[pzn[0] % 2]
                    pzn[0] += 1
                    for c in range(8):
                        P.do("tensor", lambda e, p_=p_, j=j, c=c: e.matmul(p_[:], lhsT=wz[:, c, j * 128:(j + 1) * 128], rhs=hT4[:, c, :],
                                                                         start=(c == 0), stop=(c == 7)),
                             reads=[wzB] + hT4B, writes=[pb_])
                    sz, szb = szt[j % 2]
                    P.do("scalar", lambda e, p_=p_, sz=sz: e.activation(out=sz[:], in_=p_[:], func=AF.Silu), reads=[pb_], writes=[szb])
                    P.do("vector", lambda e, sz=sz, j=j: e.tensor_tensor(out=og[:, j, :], in0=sz[:], in1=On[:, j, :], op=ALU.mult),
                         reads=[szb, OnB], writes=[ogB])

            def p3_body_b(g, j):
                hT4, hT4B = hT4s[g % 2]
                sgl = []
                for br in range(2):
                    p_, pb_ = pz[pzn[0] % 2]
                    pzn[0] += 1
                    c0 = br * 1024 + j * 128
                    for c in range(8):
                        P.do("tensor", lambda e, p_=p_, c0=c0, c=c: e.matmul(p_[:], lhsT=wg[:, c, c0:c0 + 128], rhs=hT4[:, c, :],
                                                                           start=(c == 0), stop=(c == 7)),
                             reads=[wgB] + hT4B, writes=[pb_])
                    sg_, sgb_ = sgt_[(2 * j + br) % 4]
                    P.do("scalar", lambda e, p_=p_, sg_=sg_: e.activation(out=sg_[:], in_=p_[:], func=AF.Sigmoid),
                         reads=[pb_], writes=[sgb_])
                    sgl.append((sg_, sgb_))
                for br in range(2):
                    p_, pb_ = pab[br]
                    w_, wB_ = (wpa, wpaB) if br == 0 else (wpb, wpbB)
                    for kk in range(4):
                        P.do("tensor", lambda e, p_=p_, w_=w_, kk=kk, br=br: e.matmul(
                            p_[:], lhsT=w_[:, kk, j * 128:(j + 1) * 128], rhs=og[:, br * 4 + kk, :], start=(kk == 0), stop=(kk == 3)),
                            reads=[wB_, ogB], writes=[pb_])
                    m_, mb_ = m12[br]
                    sg_, sgb_ = sgl[br]
                    P.do("vector", lambda e, m_=m_, p_=p_, sg_=sg_: e.tensor_tensor(out=m_[:], in0=p_[:], in1=sg_[:], op=ALU.mult),
                         reads=[pb_, sgb_], writes=[mb_])
                P.do("gpsimd", lambda e: e.tensor_tensor(out=mg[:, j, :], in0=m12[0][0][:], in1=m12[1][0][:], op=ALU.add),
                     reads=[m12[0][1], m12[1][1]], writes=[mgB])

            def p3_body_c(g):
                mslot = g_mslot(g)
                for tl in range(4):
                    ys, ysB, ysd = ysb[(g * 4 + tl) % 3]
                    xt, xtb, _ = xts[(g * 4 + tl) % NXT]
                    for hb in range(2):
                        p_, pb_ = pout[hb]
                        for j in range(8):
                            P.do("tensor", lambda e, p_=p_, j=j, tl=tl, hb=hb: e.matmul(
                                p_[:], lhsT=mg[:, j, tl * 128:(tl + 1) * 128], rhs=wo[:, j, hb * 512:(hb + 1) * 512],
                                start=(j == 0), stop=(j == 7)), reads=[mgB, woB], writes=[pb_])
                        P.do("vector", lambda e, p_=p_, ys=ys, hb=hb: e.tensor_tensor(
                            out=ys[:, hb * 512:(hb + 1) * 512], in0=p_[:], in1=gate_bc[:, mslot, hb * 512:(hb + 1) * 512], op=ALU.mult),
                            reads=[pb_, cB], writes=[ysB])
                    P.do("gpsimd", lambda e, ys=ys, xt=xt: e.tensor_tensor(out=ys[:], in0=ys[:], in1=xt[:], op=ALU.add),
                         reads=[ysB, xtb], writes=[ysB])
                    r0 = g * 512 + tl * 128
                    P.do("sync", lambda e, ys=ys, r0=r0: e.dma_start(out=y_d[r0:r0 + 128, :], in_=ys[:]),
                         reads=[ysB], writes=[Y_b], dsem=ysd)

            p3_load_x(0)
            p3_load_on(0)
            p3_stats(0)
            p3_trans(0)
            for g in range(NG):
                if g + 1 < NG:
                    p3_load_x(g + 1)
                p3_body_a(g)
                if g + 1 < NG:
                    p3_load_on(g + 1)
                    p3_stats(g + 1)
                for j in range(8):
                    p3_body_b(g, j)
                    if 2 <= j <= 5 and g + 1 < NG:
                        p3_trans(g + 1, tiles=(j - 2,))
                p3_body_c(g)
            P.barrier()
        es3.close()
        P.emit()
    return nc


def _core_plan():
    plans = []
    for c in range(NCORES):
        us = [(g // 4, g % 4) for g in (3 * c, 3 * c + 1, 3 * c + 2)]
        seqs = [u[0] for u in us]
        if seqs[0] == seqs[1] == seqs[2]:
            order = [0, 1, 2]
        elif seqs[1] == seqs[2]:
            order = [0, 1, 2]
        else:
            order = [2, 0, 1]
        plans.append([us[i] for i in order])
    return plans


def _sel_const():
    m = np.zeros((128, 384), np.float32)
    m[0, 0:128] = 1.0
    m[64, 0:128] = 1.0
    m[32, 128:256] = 1.0
    m[96, 128:256] = 1.0
    m[0, 256:320] = 1.0
    m[64, 256:320] = 1.0
    m[32, 320:384] = 1.0
    m[96, 320:384] = 1.0
    return m


def _host_inputs(inputs):
    f = lambda a: np.ascontiguousarray(np.asarray(a), dtype=np.float32)
    xs = [f(inputs["x_prompt"])[b] for b in range(4)] + [f(inputs["x_sample"])[b] for b in range(2)]
    cs = [f(inputs["c_prompt"])[b] for b in range(4)] + [f(inputs["c_sample"])[b] for b in range(2)]
    w_in = f(inputs["w_in"])[0]
    w_ada = f(inputs["w_ada"])[0]
    b_ada = f(inputs["b_ada"])[0]
    o = np.cumsum([0, 512, 128, 128, 512, 512, 512, 512, 512, 1024, 1024])
    qa, ka, va, za, qb, kb, vb, zb, ga, gbm = [w_in[:, o[i]:o[i + 1]] for i in range(10)]
    perm = np.concatenate([np.concatenate([np.arange(u * 64, (u + 1) * 64), np.arange((4 + u) * 64, (5 + u) * 64)])
                           for u in range(4)])
    rep = lambda v, n=128: np.ascontiguousarray(np.broadcast_to(np.asarray(v, np.float32), (n,) + tuple(np.shape(v))))
    qn_a, kn_a, qn_b, kn_b = [f(inputs[k])[0] for k in ("qn_a", "kn_a", "qn_b", "kn_b")]
    shared = {
        "w_ada": w_ada,
        "b_adaT": np.ascontiguousarray(b_ada.reshape(24, 128).T),
        "b_row": b_ada.reshape(1, -1).copy(),
        "gb": rep(f(inputs["norm_g"])[0]),
        "gq": rep(np.stack([qn_a] * 8 + [qn_b] * 8)),
        "gk": rep(np.stack([kn_b] * 8 + [kn_a] * 2)),
        "lamv": rep(np.stack([f(inputs[k])[0] for k in ("lam_q1", "lam_k1", "lam_q2", "lam_k2")])),
        "sg": f(inputs["subln_g"])[0].reshape(128, 1).copy(),
        "jidx": rep(np.concatenate([np.float32(10000.0) ** (-np.arange(0, 32, 2, dtype=np.float32) / np.float32(32)),
                                    np.float32(500000.0) ** (-np.arange(0, 16, 2, dtype=np.float32) / np.float32(16))]).astype(np.float32)),
        "ident": np.eye(128, dtype=np.float32),
        "selc": _sel_const(),
        "w_kv": np.ascontiguousarray(np.concatenate([kb, ka, va, vb], axis=1)),
        "w_q": np.ascontiguousarray(np.concatenate([qa[:, perm], qb], axis=1)),
        "w_z": np.ascontiguousarray(np.concatenate([za[:, perm], zb], axis=1)),
        "w_g": np.ascontiguousarray(np.concatenate([ga, gbm], axis=1)),
        "w_pa": np.ascontiguousarray(f(inputs["w_proj_a"])[0][perm, :]),
        "w_pb": f(inputs["w_proj_b"])[0],
        "w_out": f(inputs["w_out"])[0],
    }
    plans = _core_plan()
    in_maps = []
    for c in range(NCORES):
        pl = plans[c]
        sa, sb = pl[0][0], pl[1][0]
        m = dict(shared)
        m["xa"] = xs[sa]
        m["xb"] = xs[sb]
        m["xq"] = np.ascontiguousarray(np.concatenate([xs[s][q * 2048:(q + 1) * 2048] for (s, q) in pl], axis=0))
        cT = np.stack([cs[sa], cs[sb]], axis=-1)
        m["cT"] = np.ascontiguousarray(cT.reshape(8, 128, 2).transpose(1, 0, 2))
        pos = np.zeros((128, NTAB), np.float32)
        pos[:, :NT_K] = (np.arange(NT_K)[None, :] * 128 + np.arange(128)[:, None])
        ownpos = np.concatenate([q * 2048 + np.arange(2048) for (s, q) in pl])
        pos[:, NT_K:] = ownpos.reshape(NT_Q, 128).T
        m["pos"] = pos
        in_maps.append(m)
    return in_maps, plans


_NC_CACHE = {}


def kernel(**inputs):
    in_maps, plans = _host_inputs(inputs)
    if "nc" not in _NC_CACHE:
        _NC_CACHE["nc"] = build_program()
    nc = _NC_CACHE["nc"]
    res = run_bass_kernel_spmd(nc, in_maps, core_ids=list(range(NCORES)))
    ys = [np.empty((S, D), np.float32) for _ in range(6)]
    for c in range(NCORES):
        y = np.asarray(res.results[c]["y"])
        for i, (s, q) in enumerate(plans[c]):
            ys[s][q * 2048:(q + 1) * 2048] = y[i * 2048:(i + 1) * 2048]
    return (np.stack(ys[:4]), np.stack(ys[4:]))
```

```python
import math
import os
from contextlib import ExitStack

import numpy as np
import concourse.bass as bass
import concourse.mybir as mybir
from concourse.bass_utils import run_bass_kernel_spmd

F32 = mybir.dt.float32
BF16 = mybir.dt.bfloat16
I32 = mybir.dt.int32
AF = mybir.ActivationFunctionType
ALU = mybir.AluOpType
AX = mybir.AxisListType

D = 1024
S = 8192
NQ = 6144
NCORES = 8
EPS = 1e-6
LAM_INIT = 0.8 - 0.6 * math.exp(-0.3 * 0)
NT_K = S // 128
NT_Q = NQ // 128
NTAB = NT_K + NT_Q
TWO_PI = 2.0 * math.pi
CW1 = 6.28125
CW2 = TWO_PI - CW1


class Buf:
    __slots__ = ("w", "r", "const", "multi", "ws")

    def __init__(self, const=False, multi=False):
        self.w = None
        self.r = []
        self.const = const
        self.multi = multi
        self.ws = {}


class Prog:
    ENGS = ("sync", "scalar", "vector", "gpsimd", "tensor")

    def __init__(self, nc, es):
        self.nc = nc
        self.es = es
        self.ops = []
        self.sem = {e: es.enter_context(nc.semaphore("s_" + e)) for e in self.ENGS}
        self.dsems = []
        self.last = {e: None for e in self.ENGS}

    def dsem(self, name):
        s = self.es.enter_context(self.nc.semaphore(name))
        d = {"sem": s, "count": 0, "last": None}
        self.dsems.append(d)
        return d

    def raw(self, eng, fn, deps, dsem=None):
        o = {"eng": eng, "fn": fn, "deps": [d for d in deps if d is not None], "sig": False,
             "dma": None, "val": None}
        if dsem is not None:
            dsem["count"] += 16
            o["dma"] = (dsem, dsem["count"])
            dsem["last"] = o
        self.ops.append(o)
        self.last[eng] = o
        return o

    def do(self, eng, fn, reads=(), writes=(), dsem=None, extra=()):
        deps = list(extra)
        for b in reads:
            if b.multi:
                deps.extend(b.ws.values())
            elif b.w is not None:
                deps.append(b.w)
        for b in writes:
            if b.multi:
                pass
            else:
                if b.w is not None and not (b.w["dma"] is None and b.w["eng"] == eng and dsem is None):
                    deps.append(b.w)
                deps.extend(b.r)
        o = self.raw(eng, fn, deps, dsem)
        for b in reads:
            if not b.const:
                b.r.append(o)
        for b in writes:
            if b.multi:
                b.ws[id(dsem)] = o
            else:
                b.w = o
                b.r = []
        return o

    def barrier(self):
        deps = [o for o in self.last.values() if o is not None]
        deps += [d["last"] for d in self.dsems if d["last"] is not None]
        for e in self.ENGS:
            self.raw(e, lambda eng: eng.nop(), deps)

    def emit(self):
        nc = self.nc
        for o in self.ops:
            for d in o["deps"]:
                if d["dma"] is None and not (d["eng"] == "tensor" and o["eng"] == "tensor"):
                    d["sig"] = True
        cnt = {e: 0 for e in self.ENGS}
        for o in self.ops:
            if o["dma"] is None and o["sig"]:
                cnt[o["eng"]] += 1
                o["val"] = cnt[o["eng"]]
        by_eng = {e: [o for o in self.ops if o["eng"] == e] for e in self.ENGS}
        sem = self.sem
        with nc.Block() as block:
            def mk(e):
                def body(eng):
                    waited = {}
                    for o in by_eng[e]:
                        need = {}
                        for d in o["deps"]:
                            if d["dma"] is not None:
                                key, s, v = id(d["dma"][0]), d["dma"][0]["sem"], d["dma"][1]
                            else:
                                if d["eng"] == "tensor" and e == "tensor":
                                    continue
                                key, s, v = d["eng"], sem[d["eng"]], d["val"]
                            if v > need.get(key, (None, 0))[1]:
                                need[key] = (s, v)
                        for key, (s, v) in need.items():
                            if waited.get(key, 0) >= v:
                                continue
                            eng.wait_ge(s, v)
                            waited[key] = v
                        ins = o["fn"](eng)
                        if o["dma"] is not None:
                            ins.then_inc(o["dma"][0]["sem"], 16)
                        elif o["sig"]:
                            ins.then_inc(sem[e], 1)
                return body
            for e in self.ENGS:
                if by_eng[e]:
                    getattr(block, e)(mk(e))


def build_program(debug=False, stop=99):
    nc = bass.Bass("TRN2", target_bir_lowering=False)

    def din(name, shape, dt=F32):
        return nc.dram_tensor(name, list(shape), dt, kind="ExternalInput").ap()

    xa_d = din("xa", [S, D])
    xb_d = din("xb", [S, D])
    xq_d = din("xq", [NQ, D])
    cT_d = din("cT", [128, 8, 2])
    wada_d = din("w_ada", [D, 3 * D])
    badaT_d = din("b_adaT", [128, 24])
    brow_d = din("b_row", [1, 3 * D])
    gb_d = din("gb", [128, D])
    gq_d = din("gq", [128, 16, 64])
    gk_d = din("gk", [128, 10, 64])
    lamv_d = din("lamv", [128, 4, 64])
    sg_d = din("sg", [128, 1])
    pos_d = din("pos", [128, NTAB])
    jidx_d = din("jidx", [128, 24])
    ident_d = din("ident", [128, 128])
    sel_d = din("selc", [128, 384])
    wkv_d = din("w_kv", [D, 1280])
    wq_d = din("w_q", [D, 1024])
    wz_d = din("w_z", [D, 1024])
    wg_d = din("w_g", [D, 2048])
    wpa_d = din("w_pa", [512, D])
    wpb_d = din("w_pb", [512, D])
    wout_d = din("w_out", [D, D])
    y_d = nc.dram_tensor("y", [NQ, D], F32, kind="ExternalOutput").ap()
    skind = "ExternalOutput" if debug else "Internal"
    KT_d = [nc.dram_tensor("KT%d" % s, [640, S], BF16, kind=skind).ap() for s in range(2)]
    VS_d = [nc.dram_tensor("VS%d" % s, [S, 640], BF16, kind=skind).ap() for s in range(2)]
    QT_d = nc.dram_tensor("QT", [1024, NQ], BF16, kind=skind).ap()
    OT_d = nc.dram_tensor("OT", [1024, NQ], BF16, kind=skind).ap()
    x_srcs = [xa_d, xb_d]

    with ExitStack() as es:
        P = Prog(nc, es)

        def sbt(scope, name, shape, dt):
            return scope.enter_context(nc.sbuf_tensor("sb_" + name, list(shape), dt))

        def pst(scope, name, shape, dt):
            return scope.enter_context(nc.psum_tensor("ps_" + name, list(shape), dt))

        dbg_sem = P.dsem("dbg") if debug else None

        def dbg(name, ap, shape, bufs, dt=F32):
            if not debug:
                return
            o = nc.dram_tensor("dbg_" + name, list(shape), dt, kind="ExternalOutput").ap()
            P.do("sync", lambda e: e.dma_start(out=o, in_=ap), reads=bufs, dsem=dbg_sem)

        KT_b = [Buf(multi=True) for _ in range(2)]
        VS_b = [Buf(multi=True) for _ in range(2)]
        QT_b = Buf(multi=True)
        OT_b = Buf(multi=True)
        Y_b = Buf(multi=True)

        ident = sbt(es, "ident", [128, 128], BF16)
        ones_b = sbt(es, "ones_b", [128, 128], BF16)
        ones_f = sbt(es, "ones_f", [128, 128], F32)
        gbt = sbt(es, "gbt", [128, D], F32)
        modT = sbt(es, "modT", [128, 24, 2], F32)
        sc1 = sbt(es, "sc1", [128, 8, 2], F32)
        gate_bc = sbt(es, "gate_bc", [128, 2, D], F32)
        neglam = sbt(es, "neglam", [128, 1], F32)
        sg8 = sbt(es, "sg8", [128, 1], F32)
        epsc = sbt(es, "epsc", [128, 1], F32)
        sel = sbt(es, "sel", [128, 384], F32)
        sel_b = sbt(es, "sel_b", [128, 384], BF16)
        rstd_own = sbt(es, "rstd_own", [128, NT_Q], F32)
        rownB = Buf()
        cB = Buf(const=True)

        def load_weight(ctx, dst, dst_buf, w_ap, nchunk, ncols, engines=("vector", "gpsimd", "scalar"), defer=None):
            for c in range(nchunk):
                for c0 in range(0, ncols, 1024):
                    if defer is not None:
                        defer.append(lambda c=c, c0=c0: load_weight_piece(ctx, dst, dst_buf, w_ap, ncols, engines, c, c0))
                    else:
                        load_weight_piece(ctx, dst, dst_buf, w_ap, ncols, engines, c, c0)

        def load_weight_piece(ctx, dst, dst_buf, w_ap, ncols, engines, c, c0):
            if True:
                if True:
                    cw = min(1024, ncols - c0)
                    i = ctx["wn"]
                    ctx["wn"] += 1
                    st, stb, std = ctx["wst"][i % len(ctx["wst"])]
                    P.do("sync", lambda e, st=st, c=c, c0=c0, cw=cw: e.dma_start(out=st[:, 0:cw], in_=w_ap[c * 128:(c + 1) * 128, c0:c0 + cw]),
                         writes=[stb], dsem=std)
                    eng = engines[i % len(engines)]
                    if eng == "scalar":
                        P.do(eng, lambda e, st=st, c=c, c0=c0, cw=cw: e.activation(out=dst[:, c, c0:c0 + cw], in_=st[:, 0:cw], func=AF.Copy),
                             reads=[stb], writes=[dst_buf])
                    else:
                        P.do(eng, lambda e, st=st, c=c, c0=c0, cw=cw: e.tensor_copy(out=dst[:, c, c0:c0 + cw], in_=st[:, 0:cw]),
                             reads=[stb], writes=[dst_buf])

        es1 = ExitStack()
        wkv = sbt(es1, "wkv", [128, 8, 1280], BF16)
        wq = sbt(es1, "wq", [128, 8, 1024], BF16)
        wkvB, wqB = Buf(), Buf()
        ctx1 = {"n": 0, "wn": 0}
        ctx1["wst"] = [(sbt(es1, "wst%d" % i, [128, 1024], F32), Buf(), P.dsem("dwst%d" % i)) for i in range(3)]

        with ExitStack() as s0:
            wa = sbt(s0, "wa", [128, 8, 3 * D], F32)
            idf = sbt(s0, "idf", [128, 128], F32)
            cT = sbt(s0, "cT", [128, 8, 2], F32)
            ec = sbt(s0, "ec", [128, 8, 2], F32)
            scT = sbt(s0, "scT", [128, 8, 2], F32)
            grep = [(sbt(s0, "grep%d" % i, [128, 128], F32), Buf()) for i in range(2)]
            badaT = sbt(s0, "badaT", [128, 24], F32)
            brow = sbt(s0, "brow", [1, 3 * D], F32)
            lamv = sbt(s0, "lamv", [128, 4, 64], F32)
            lprod = sbt(s0, "lprod", [128, 2, 64], F32)
            lsum = sbt(s0, "lsum", [128, 2], F32)
            lexp = sbt(s0, "lexp", [128, 2], F32)
            sgt = sbt(s0, "sgt", [128, 1], F32)
            mps = pst(s0, "mps", [128, 24, 2], F32)
            gps = pst(s0, "gps", [128, 2, 512], F32)
            b0 = {k: Buf() for k in ("wa", "idf", "cT", "ec", "scT", "screp", "badaT", "brow", "lamv",
                                     "lprod", "lsum", "lexp", "sgt", "mps", "gps0", "gps1",
                                     "ident", "ones", "gbt", "modT", "sc1", "gate", "neglam", "sg8", "epsc")}
            ds0 = [P.dsem("ds0_%d" % i) for i in range(4)]
            for c in range(8):
                P.do("sync", lambda e, c=c: e.dma_start(out=wa[:, c, :], in_=wada_d[c * 128:(c + 1) * 128, :]),
                     writes=[], dsem=ds0[0])
            b0["wa"].w = ds0[0]["last"]
            P.do("sync", lambda e: e.dma_start(out=idf[:], in_=ident_d), writes=[b0["idf"]], dsem=ds0[1])
            P.do("sync", lambda e: e.dma_start(out=cT[:], in_=cT_d), writes=[b0["cT"]], dsem=ds0[1])
            P.do("sync", lambda e: e.dma_start(out=badaT[:], in_=badaT_d), writes=[b0["badaT"]], dsem=ds0[1])
            P.do("sync", lambda e: e.dma_start(out=brow[:], in_=brow_d), writes=[b0["brow"]], dsem=ds0[1])
            P.do("sync", lambda e: e.dma_start(out=lamv[:], in_=lamv_d), writes=[b0["lamv"]], dsem=ds0[1])
            P.do("sync", lambda e: e.dma_start(out=sgt[:], in_=sg_d), writes=[b0["sgt"]], dsem=ds0[1])
            P.do("sync", lambda e: e.dma_start(out=gbt[:], in_=gb_d), writes=[b0["gbt"]], dsem=ds0[1])
            P.do("sync", lambda e: e.dma_start(out=sel[:], in_=sel_d), writes=[b0["gbt"]], dsem=ds0[1])
            small_last = ds0[1]["last"]
            for k in ("idf", "cT", "badaT", "brow", "lamv", "sgt", "gbt"):
                b0[k].w = small_last
            load_weight(ctx1, wkv, wkvB, wkv_d, 8, 1280, engines=("gpsimd",))
            load_weight(ctx1, wq, wqB, wq_d, 8, 1024, engines=("gpsimd",))
            P.do("vector", lambda e: e.tensor_copy(out=ident[:], in_=idf[:]), reads=[b0["idf"]], writes=[b0["ident"]])
            P.do("vector", lambda e: e.tensor_copy(out=sel_b[:], in_=sel[:]), reads=[b0["gbt"]], writes=[b0["ident"]])
            P.do("gpsimd", lambda e: e.memset(ones_b[:], 1.0), writes=[b0["ones"]])
            P.do("gpsimd", lambda e: e.memset(ones_f[:], 1.0), writes=[b0["ones"]])
            P.do("gpsimd", lambda e: e.memset(epsc[:], EPS), writes=[b0["epsc"]])
            if stop <= -3:
                P.barrier(); P.emit(); return nc
            P.do("scalar", lambda e: e.activation(out=ec[:], in_=cT[:], func=AF.Exp, scale=-1.0),
                 reads=[b0["cT"]], writes=[b0["ec"]])
            P.do("vector", lambda e: e.tensor_scalar(out=ec[:], in0=ec[:], scalar1=1.0, scalar2=None, op0=ALU.add),
                 reads=[b0["ec"]], writes=[b0["ec"]])
            P.do("vector", lambda e: e.reciprocal(out=ec[:], in_=ec[:]), reads=[b0["ec"]], writes=[b0["ec"]])
            P.do("vector", lambda e: e.tensor_tensor(out=scT[:], in0=cT[:], in1=ec[:], op=ALU.mult),
                 reads=[b0["ec"], b0["cT"]], writes=[b0["scT"]])
            if stop <= -2:
                P.barrier(); P.emit(); return nc
            for j in range(24):
                for c in range(8):
                    P.do("tensor", lambda e, j=j, c=c: e.matmul(mps[:, j, :], lhsT=wa[:, c, j * 128:(j + 1) * 128],
                                                              rhs=scT[:, c, :], start=(c == 0), stop=(c == 7)),
                         reads=[b0["wa"], b0["scT"]], writes=[b0["mps"]])
            P.do("vector", lambda e: e.tensor_tensor(out=modT[:], in0=mps[:],
                                                     in1=badaT[:].unsqueeze(2).to_broadcast([128, 24, 2]), op=ALU.add),
                 reads=[b0["mps"], b0["badaT"]], writes=[b0["modT"]])
            P.do("vector", lambda e: e.tensor_scalar(out=sc1[:], in0=modT[:, 8:16, :], scalar1=1.0, scalar2=None, op0=ALU.add),
                 reads=[b0["modT"]], writes=[b0["sc1"]])
            if stop <= -1:
                P.barrier(); P.emit(); return nc
            for s in range(2):
                for hb in range(2):
                    gk_ = "gps%d" % hb
                    for jj in range(4):
                        j = hb * 4 + jj
                        rp_, rpb_ = grep[j % 2]
                        P.do("vector", lambda e, rp_=rp_, j=j, s=s: e.tensor_copy(out=rp_[:], in_=modT[:, 16 + j, s:s + 1].to_broadcast([128, 128])),
                             reads=[b0["modT"]], writes=[rpb_])
                        P.do("tensor", lambda e, rp_=rp_, hb=hb, jj=jj: e.transpose(out=gps[:, hb, jj * 128:(jj + 1) * 128], in_=rp_[:], identity=idf[:]),
                             reads=[rpb_, b0["idf"]], writes=[b0[gk_]])
                    P.do("vector", lambda e, s=s, hb=hb: e.tensor_copy(out=gate_bc[:, s, hb * 512:(hb + 1) * 512], in_=gps[:, hb, :]),
                         reads=[b0[gk_]], writes=[b0["gate"]])
            if stop <= -0.5:
                P.barrier(); P.emit(); return nc
            P.do("vector", lambda e: e.tensor_tensor(out=lprod[:, 0, :], in0=lamv[:, 0, :], in1=lamv[:, 1, :], op=ALU.mult),
                 reads=[b0["lamv"]], writes=[b0["lprod"]])
            P.do("vector", lambda e: e.tensor_tensor(out=lprod[:, 1, :], in0=lamv[:, 2, :], in1=lamv[:, 3, :], op=ALU.mult),
                 reads=[b0["lamv"], b0["lprod"]], writes=[b0["lprod"]])
            P.do("vector", lambda e: e.tensor_reduce(out=lsum[:], in_=lprod[:], axis=AX.X, op=ALU.add),
                 reads=[b0["lprod"]], writes=[b0["lsum"]])
            P.do("scalar", lambda e: e.activation(out=lexp[:], in_=lsum[:], func=AF.Exp),
                 reads=[b0["lsum"]], writes=[b0["lexp"]])
            P.do("vector", lambda e: e.scalar_tensor_tensor(out=neglam[:], in0=lexp[:, 1:2], scalar=-LAM_INIT, in1=lexp[:, 0:1],
                                                            op0=ALU.add, op1=ALU.subtract),
                 reads=[b0["lexp"]], writes=[b0["neglam"]])
            P.do("vector", lambda e: e.tensor_scalar(out=sg8[:], in0=sgt[:], scalar1=1.0 - LAM_INIT, scalar2=None, op0=ALU.mult),
                 reads=[b0["sgt"]], writes=[b0["sg8"]])
            dbg("modT", modT[:], [128, 24, 2], [b0["modT"]])
            dbg("sc1", sc1[:], [128, 8, 2], [b0["sc1"]])
            dbg("gate", gate_bc[:], [128, 2, D], [b0["gate"]])
            pass
            P.barrier()
        if stop <= 0:
            P.emit()
            return nc

        def front_end(sc, ctx, x_ap, slot_idx, mslot, hT_dst, hT_buf, xt_keep=None):
            i = ctx["n"]
            ctx["n"] += 1
            xs = ctx["xt"][i % len(ctx["xt"])] if xt_keep is None else xt_keep
            xt, xb_, xd = xs
            P.do("sync", lambda e: e.dma_start(out=xt[:], in_=x_ap), writes=[xb_], dsem=xd)
            k = i % 2
            FE = int(os.environ.get("PH1_FE", "99"))
            ss, ssb = ctx["ss"][k]
            if FE < 1:
                return
            P.do("scalar", lambda e: e.activation(out=ctx["junk"][:], in_=xt[:], func=AF.Square, accum_out=ss[:]),
                 reads=[xb_], writes=[ctx["junkb"], ssb])
            if FE < 2:
                return
            P.do("scalar", lambda e: e.activation(out=ss[:], in_=ss[:], func=AF.Ln, scale=1.0 / D, bias=epsc[:, 0:1]),
                 reads=[ssb, cB], writes=[ssb])
            P.do("scalar", lambda e: e.activation(out=ss[:], in_=ss[:], func=AF.Exp, scale=-0.5), reads=[ssb], writes=[ssb])
            if FE < 3:
                return
            xn, xnb = ctx["xn"][k]
            P.do("vector", lambda e: e.scalar_tensor_tensor(out=xn[:], in0=xt[:], scalar=ss[:, 0:1], in1=gbt[:],
                                                            op0=ALU.mult, op1=ALU.mult),
                 reads=[xb_, ssb, cB], writes=[xnb])
            if FE < 4:
                return
            pT, pTb = ctx["psT"][k]
            for c in range(8):
                P.do("tensor", lambda e, c=c: e.transpose(out=pT[:, c, :], in_=xn[:, c * 128:(c + 1) * 128], identity=ident[:]),
                     reads=[xnb, cB], writes=[pTb])
            if FE < 5:
                return
            EV = os.environ.get("PH1_EV", "act")
            for c in range(8):
                if (c % 2 == 0 and EV == "both") or EV == "act":
                    P.do("scalar", lambda e, c=c: e.activation(out=hT_dst(c), in_=pT[:, c, :], func=AF.Identity,
                                                               scale=sc1[:, c, mslot:mslot + 1], bias=modT[:, c, mslot:mslot + 1]),
                         reads=[pTb, cB], writes=[hT_buf[0]])
                else:
                    P.do("vector", lambda e, c=c: e.tensor_scalar(out=hT_dst(c), in0=pT[:, c, :],
                                                                  scalar1=sc1[:, c, mslot:mslot + 1], scalar2=modT[:, c, mslot:mslot + 1],
                                                                  op0=ALU.mult, op1=ALU.add),
                         reads=[pTb, cB], writes=[hT_buf[1]])

        with ExitStack() as s1:
            sinT = sbt(s1, "sinT", [128, NTAB, 40], F32)
            cosT = sbt(s1, "cosT", [128, NTAB, 40], F32)
            gq = sbt(s1, "gq", [128, 16, 64], F32)
            gk = sbt(s1, "gk", [128, 10, 64], F32)
            tabB = Buf()
            with ExitStack() as sr:
                pos = sbt(sr, "pos", [128, NTAB], F32)
                jidx = sbt(sr, "jidx", [128, 24], F32)
                inv = sbt(sr, "inv", [128, 24], F32)
                posi = sbt(sr, "posi", [128, NTAB], I32)
                rowf = sbt(sr, "rowf", [128, NTAB], F32)
                colf = sbt(sr, "colf", [128, NTAB], F32)
                ang = sbt(sr, "ang", [128, NTAB, 40], F32)
                tq = sbt(sr, "tq", [128, NTAB, 40], F32)
                ki = sbt(sr, "ki", [128, NTAB, 40], I32)
                kf = sbt(sr, "kf", [128, NTAB, 40], F32)
                yy = sbt(sr, "yy", [128, NTAB, 40], F32)
                cc = tq
                rb = Buf()
                dsr = P.dsem("dsr")
                P.do("sync", lambda e: e.dma_start(out=pos[:], in_=pos_d), writes=[rb], dsem=dsr)
                P.do("sync", lambda e: e.dma_start(out=inv[:], in_=jidx_d), writes=[rb], dsem=dsr)
                P.do("sync", lambda e: e.dma_start(out=gq[:], in_=gq_d), writes=[rb], dsem=dsr)
                P.do("sync", lambda e: e.dma_start(out=gk[:], in_=gk_d), writes=[rb], dsem=dsr)
                V = lambda fn: P.do("vector", fn, reads=[rb], writes=[rb])
                A = lambda fn: P.do("scalar", fn, reads=[rb], writes=[rb])
                V(lambda e: e.tensor_scalar(out=posi[:], in0=pos[:], scalar1=1.0 / 64.0, scalar2=None, op0=ALU.mult))
                V(lambda e: e.tensor_copy(out=rowf[:], in_=posi[:]))
                V(lambda e: e.scalar_tensor_tensor(out=colf[:], in0=rowf[:], scalar=-64.0, in1=pos[:], op0=ALU.mult, op1=ALU.add))
                V(lambda e: e.tensor_scalar(out=tq[:, :, 0], in0=colf[:], scalar1=0.0, scalar2=None, op0=ALU.is_lt))
                V(lambda e: e.tensor_tensor(out=rowf[:], in0=rowf[:], in1=tq[:, :, 0], op=ALU.subtract))
                V(lambda e: e.scalar_tensor_tensor(out=colf[:], in0=tq[:, :, 0], scalar=64.0, in1=colf[:], op0=ALU.mult, op1=ALU.add))
                V(lambda e: e.tensor_tensor(out=ang[:, :, 0:16], in0=rowf[:].unsqueeze(2).to_broadcast([128, NTAB, 16]),
                                            in1=inv[:, 0:16].unsqueeze(1).to_broadcast([128, NTAB, 16]), op=ALU.mult))
                V(lambda e: e.tensor_tensor(out=ang[:, :, 16:32], in0=colf[:].unsqueeze(2).to_broadcast([128, NTAB, 16]),
                                            in1=inv[:, 0:16].unsqueeze(1).to_broadcast([128, NTAB, 16]), op=ALU.mult))
                V(lambda e: e.tensor_tensor(out=ang[:, :, 32:40], in0=pos[:].unsqueeze(2).to_broadcast([128, NTAB, 8]),
                                            in1=inv[:, 16:24].unsqueeze(1).to_broadcast([128, NTAB, 8]), op=ALU.mult))
                for (dst, shift) in ((sinT, 0.0), (cosT, 0.25)):
                    V(lambda e, shift=shift: e.tensor_scalar(out=tq[:], in0=ang[:], scalar1=1.0 / TWO_PI, scalar2=shift,
                                                             op0=ALU.mult, op1=ALU.add))
                    V(lambda e: e.tensor_copy(out=ki[:], in_=tq[:]))
                    V(lambda e: e.tensor_copy(out=kf[:], in_=ki[:]))
                    V(lambda e: e.scalar_tensor_tensor(out=yy[:], in0=kf[:], scalar=-CW1, in1=ang[:], op0=ALU.mult, op1=ALU.add))
                    V(lambda e: e.scalar_tensor_tensor(out=yy[:], in0=kf[:], scalar=-CW2, in1=yy[:], op0=ALU.mult, op1=ALU.add))
                    if shift != 0.0:
                        V(lambda e: e.tensor_scalar(out=yy[:], in0=yy[:], scalar1=0.5 * math.pi, scalar2=None, op0=ALU.add))
                    V(lambda e: e.tensor_scalar(out=cc[:], in0=yy[:], scalar1=math.pi, scalar2=-TWO_PI, op0=ALU.is_gt, op1=ALU.mult))
                    V(lambda e: e.tensor_tensor(out=yy[:], in0=yy[:], in1=cc[:], op=ALU.add))
                    V(lambda e: e.tensor_scalar(out=cc[:], in0=yy[:], scalar1=-math.pi, scalar2=TWO_PI, op0=ALU.is_lt, op1=ALU.mult))
                    V(lambda e: e.tensor_tensor(out=yy[:], in0=yy[:], in1=cc[:], op=ALU.add))
                    V(lambda e: e.tensor_scalar(out=yy[:], in0=yy[:], scalar1=3.1415925, scalar2=-3.1415925, op0=ALU.min, op1=ALU.max))
                    A(lambda e, dst=dst: e.activation(out=dst[:], in_=yy[:], func=AF.Sin))
                dbg("sinT", sinT[:], [128, NTAB, 40], [rb])
                dbg("cosT", cosT[:], [128, NTAB, 40], [rb])
                P.barrier()
            if stop <= 0.5:
                P.emit()
                return nc

            NF = 4
            NX = 6
            xts = [(sbt(s1, "xt%d" % i, [128, D], F32), Buf(), P.dsem("dxt%d" % i)) for i in range(NX)]
            sss = [(sbt(s1, "ss%d" % i, [128, 1], F32), Buf()) for i in range(NF)]
            junk = sbt(s1, "junk", [128, D], BF16)
            junkb = Buf()
            xns = [(sbt(s1, "xn%d" % i, [128, D], BF16), Buf()) for i in range(NF)]
            psT = (pst(s1, "psT", [128, 8, 128], BF16), Buf())
            hTs = [(sbt(s1, "hT%d" % i, [128, 8, 128], BF16), Buf()) for i in range(NF)]
            pps = [(pst(s1, "pp%d" % i, [128, 1536], F32), Buf()) for i in range(2)]
            pkT = (pst(s1, "pkT", [128, 8, 128], BF16), Buf())
            sqs = [(sbt(s1, "sq%d" % i, [128, 1024], F32), Buf()) for i in range(2)]
            ssk = [(sbt(s1, "ssk%d" % i, [128, 16], F32), Buf()) for i in range(NF)]
            xg = [(sbt(s1, "xg%d" % i, [128, 1024], F32), Buf()) for i in range(NF)]
            rts = [[(sbt(s1, "rt%d_%d" % (k, i), [128, 256], F32), Buf()) for i in range(4)] for k in range(2)]
            ybf = [(sbt(s1, "ybf%d" % i, [128, 1024], BF16), Buf(), Buf()) for i in range(NF)]
            vst = [(sbt(s1, "vst%d" % i, [128, 640], BF16), Buf(), P.dsem("dvst%d" % i)) for i in range(2)]
            kst = [(sbt(s1, "kst%d" % i, [128, 8, 512], BF16), Buf(), P.dsem("dkst%d" % i)) for i in range(2)]

            jobs = []
            for s in range(2):
                for t in range(NT_K):
                    jobs.append(("kv", x_srcs[s][t * 128:(t + 1) * 128, :], t, s, s, t))
            for t in range(NT_Q):
                jobs.append(("q", xq_d[t * 128:(t + 1) * 128, :], NT_K + t, 0 if t < 16 else 1, None, t))
            if os.environ.get("PH1_JOBS"):
                jobs = jobs[:int(os.environ["PH1_JOBS"])]
            n_jobs = len(jobs)

            def st_load(ji):
                xt, xb_, xd = xts[ji % NX]
                x_ap = jobs[ji][1]
                P.do("sync", lambda e: e.dma_start(out=xt[:], in_=x_ap), writes=[xb_], dsem=xd)

            def st_stats(ji):
                kind, x_ap, tab, mslot, sslot, t = jobs[ji]
                xt, xb_, xd = xts[ji % NX]
                ss, ssb = sss[ji % NF]
                P.do("scalar", lambda e: e.activation(out=junk[:], in_=xt[:], func=AF.Square, accum_out=ss[:]),
                     reads=[xb_], writes=[junkb, ssb])
                P.do("scalar", lambda e: e.activation(out=ss[:], in_=ss[:], func=AF.Ln, scale=1.0 / D, bias=epsc[:, 0:1]),
                     reads=[ssb, cB], writes=[ssb])
                P.do("scalar", lambda e: e.activation(out=ss[:], in_=ss[:], func=AF.Exp, scale=-0.5), reads=[ssb], writes=[ssb])
                if kind == "q":
                    P.do("scalar", lambda e: e.activation(out=rstd_own[:, t:t + 1], in_=ss[:], func=AF.Copy), reads=[ssb], writes=[rownB])
                xn, xnb = xns[ji % NF]
                P.do("vector", lambda e: e.scalar_tensor_tensor(out=xn[:], in0=xt[:], scalar=ss[:, 0:1], in1=gbt[:],
                                                                op0=ALU.mult, op1=ALU.mult),
                     reads=[xb_, ssb, cB], writes=[xnb])

            def st_trans(ji):
                kind, x_ap, tab, mslot, sslot, t = jobs[ji]
                xn, xnb = xns[ji % NF]
                pT, pTb = psT
                for c in range(8):
                    P.do("tensor", lambda e, c=c: e.transpose(out=pT[:, c, :], in_=xn[:, c * 128:(c + 1) * 128], identity=ident[:]),
                         reads=[xnb, cB], writes=[pTb])
                hT, hTb = hTs[ji % NF]
                for c in range(8):
                    if ji % 2 == 0:
                        P.do("scalar", lambda e, c=c: e.activation(out=hT[:, c, :], in_=pT[:, c, :], func=AF.Identity,
                                                                   scale=sc1[:, c, mslot:mslot + 1], bias=modT[:, c, mslot:mslot + 1]),
                             reads=[pTb, cB], writes=[hTb])
                    else:
                        P.do("vector", lambda e, c=c: e.tensor_scalar(out=hT[:, c, :], in0=pT[:, c, :],
                                                                      scalar1=sc1[:, c, mslot:mslot + 1], scalar2=modT[:, c, mslot:mslot + 1],
                                                                      op0=ALU.mult, op1=ALU.add),
                             reads=[pTb, cB], writes=[hTb])

            def st_proj(ji):
                kind = jobs[ji][0]
                hT, hTb = hTs[ji % NF]
                pp, ppB = pps[ji % 2]
                w, wB, ncol = (wkv, wkvB, 1280) if kind == "kv" else (wq, wqB, 1024)
                for c0 in range(0, ncol, 512):
                    cw = min(512, ncol - c0)
                    for c in range(8):
                        P.do("tensor", lambda e, c=c, c0=c0, cw=cw: e.matmul(
                            pp[:, c0:c0 + cw], lhsT=hT[:, c, :], rhs=w[:, c, c0:c0 + cw], start=(c == 0), stop=(c == 7)),
                            reads=[hTb, wB], writes=[ppB])

            def st_b1(ji, part):
                kind, x_ap, tab, mslot, sslot, t = jobs[ji]
                pp, ppB = pps[ji % 2]
                if kind == "kv":
                    nh, nf, gain, Arange, Brange = 10, 640, gk, (8, 2), (0, 8)
                else:
                    nh, nf, gain, Arange, Brange = 16, 1024, gq, (0, 8), (8, 8)
                sq, sqB = sqs[ji % 2]
                sk, skb = ssk[ji % NF]
                xgt, xgb = xg[ji % NF]
                X = xgt[:, 0:nf].rearrange("p (h d) -> p h d", d=64)
                yt, yb, yb2 = ybf[ji % NF]
                Y = yt[:, 0:nf].rearrange("p (h d) -> p h d", d=64)
                if part == "act1":
                    if kind == "kv":
                        vs, vsb, vsd = vst[t % 2]
                        P.do("scalar", lambda e: e.activation(out=vs[:], in_=pp[:, 640:1280], func=AF.Copy),
                             reads=[ppB], writes=[vsb])
                        P.do("sync", lambda e: e.dma_start(out=VS_d[sslot][t * 128:(t + 1) * 128, :], in_=vs[:]),
                             reads=[vsb], writes=[VS_b[sslot]], dsem=vsd)
                    P.do("scalar", lambda e: e.activation(out=sq[:, 0:nf], in_=pp[:, 0:nf], func=AF.Square),
                         reads=[ppB], writes=[sqB])
                    return
                if part == "dve1":
                    P.do("vector", lambda e: e.tensor_reduce(out=sk[:, 0:nh], in_=sq[:, 0:nf].rearrange("p (h d) -> p h d", d=64),
                                                             axis=AX.X, op=ALU.add),
                         reads=[sqB], writes=[skb])
                    return
                if part == "act2":
                    P.do("scalar", lambda e: e.activation(out=sk[:, 0:nh], in_=sk[:, 0:nh], func=AF.Ln, scale=1.0 / 64.0, bias=epsc[:, 0:1]),
                         reads=[skb, cB], writes=[skb])
                    P.do("scalar", lambda e: e.activation(out=sk[:, 0:nh], in_=sk[:, 0:nh], func=AF.Exp, scale=-0.5),
                         reads=[skb], writes=[skb])
                    return
                if part == "dve2":
                    P.do("vector", lambda e: e.tensor_tensor(out=X, in0=pp[:, 0:nf].rearrange("p (h d) -> p h d", d=64),
                                                             in1=sk[:, 0:nh].unsqueeze(2).to_broadcast([128, nh, 64]), op=ALU.mult),
                         reads=[ppB, skb], writes=[xgb])
                    P.do("vector", lambda e: e.tensor_tensor(out=X, in0=X, in1=gain[:], op=ALU.mult),
                         reads=[xgb, tabB], writes=[xgb])
                    return
                if part == "cast":
                    P.do("scalar", lambda e: e.activation(out=Y[:, Brange[0]:Brange[0] + 8, 16:64], in_=X[:, Brange[0]:Brange[0] + 8, 16:64],
                                                          func=AF.Copy),
                         reads=[xgb], writes=[yb2])
                    return
                assert part == "rope"
                rt = rts[ji % 2]
                for (a_mode, (H0, H), ti) in ((True, Arange, 0), (False, Brange, 2)):
                    (t0, tb0), (t1, tb1) = rt[ti], rt[ti + 1]
                    if a_mode:
                        Xv = X[:, H0:H0 + H, :].rearrange("p h (a f j) -> p h a f j", a=2, f=2)
                        Yv = Y[:, H0:H0 + H, :].rearrange("p h (a f j) -> p h a f j", a=2, f=2)
                        X1, X2 = Xv[:, :, :, 0, :], Xv[:, :, :, 1, :]
                        Y1, Y2 = Yv[:, :, :, 0, :], Yv[:, :, :, 1, :]
                        shp = [128, H, 2, 16]
                        cs = cosT[:, tab, 0:32].rearrange("p (a j) -> p a j", a=2).unsqueeze(1).to_broadcast(shp)
                        sn = sinT[:, tab, 0:32].rearrange("p (a j) -> p a j", a=2).unsqueeze(1).to_broadcast(shp)
                        n = H * 32
                        T0 = t0[:, 0:n].rearrange("p (h a j) -> p h a j", h=H, a=2)
                        T1 = t1[:, 0:n].rearrange("p (h a j) -> p h a j", h=H, a=2)
                    else:
                        X1, X2 = X[:, H0:H0 + H, 0:8], X[:, H0:H0 + H, 8:16]
                        Y1, Y2 = Y[:, H0:H0 + H, 0:8], Y[:, H0:H0 + H, 8:16]
                        shp = [128, H, 8]
                        cs = cosT[:, tab, 32:40].unsqueeze(1).to_broadcast(shp)
                        sn = sinT[:, tab, 32:40].unsqueeze(1).to_broadcast(shp)
                        n = H * 8
                        T0 = t0[:, 0:n].rearrange("p (h j) -> p h j", h=H)
                        T1 = t1[:, 0:n].rearrange("p (h j) -> p h j", h=H)
                    G = lambda fn, rd, wr: P.do("gpsimd", fn, reads=rd, writes=wr)
                    G(lambda e, T0=T0, X1=X1, cs=cs: e.tensor_tensor(out=T0, in0=X1, in1=cs, op=ALU.mult), [xgb, tabB], [tb0])
                    G(lambda e, T1=T1, X2=X2, sn=sn: e.tensor_tensor(out=T1, in0=X2, in1=sn, op=ALU.mult), [xgb, tabB], [tb1])
                    G(lambda e, T0=T0, T1=T1, Y1=Y1: e.tensor_tensor(out=Y1, in0=T0, in1=T1, op=ALU.subtract), [tb0, tb1], [yb])
                    G(lambda e, T0=T0, X2=X2, cs=cs: e.tensor_tensor(out=T0, in0=X2, in1=cs, op=ALU.mult), [xgb, tabB, tb0], [tb0])
                    G(lambda e, T1=T1, X1=X1, sn=sn: e.tensor_tensor(out=T1, in0=X1, in1=sn, op=ALU.mult), [xgb, tabB, tb1], [tb1])
                    G(lambda e, T0=T0, T1=T1, Y2=Y2: e.tensor_tensor(out=Y2, in0=T0, in1=T1, op=ALU.add), [tb0, tb1], [yb])

            def st_b2(ji):
                kind, x_ap, tab, mslot, sslot, t = jobs[ji]
                nf = 640 if kind == "kv" else 1024
                nch = nf // 128
                yt, yb, yb2 = ybf[ji % NF]
                pk, pkb = pkT
                for c in range(nch):
                    P.do("tensor", lambda e, c=c: e.transpose(out=pk[:, c, :], in_=yt[:, c * 128:(c + 1) * 128], identity=ident[:]),
                         reads=[yb, yb2, cB], writes=[pkb])
                g = t // 4
                ks, ksb, ksd = kst[g % 2]
                P.do("vector", lambda e: e.tensor_copy(out=ks[:, 0:nch, (t % 4) * 128:(t % 4 + 1) * 128], in_=pk[:, 0:nch, :]),
                     reads=[pkb], writes=[ksb])
                if t % 4 == 3:
                    if kind == "kv":
                        dst, dB = KT_d[sslot].rearrange("(c p) t -> p c t", p=128)[:, :, g * 512:(g + 1) * 512], KT_b[sslot]
                    else:
                        dst, dB = QT_d.rearrange("(c p) t -> p c t", p=128)[:, :, g * 512:(g + 1) * 512], QT_b
                    P.do("sync", lambda e: e.dma_start(out=dst, in_=ks[:, 0:nch, :]), reads=[ksb], writes=[dB], dsem=ksd)

            LA = 5
            ok = lambda j: 0 <= j < n_jobs
            for ji in range(min(LA, n_jobs)):
                st_load(ji)
            for it in range(-3, n_jobs + 3):
                if ok(it + LA) and it + LA >= LA:
                    st_load(it + LA)
                if ok(it - 1):
                    st_b1(it - 1, "cast")
                if ok(it):
                    st_b1(it, "act1")
                    st_b1(it, "dve1")
                if ok(it + 3):
                    st_stats(it + 3)
                if ok(it):
                    st_b1(it, "act2")
                    st_b1(it, "dve2")
                if ok(it + 2):
                    st_trans(it + 2)
                if ok(it + 1):
                    st_proj(it + 1)
                if ok(it):
                    st_b1(it, "rope")
                if ok(it - 2):
                    st_b2(it - 2)
            P.barrier()
        es1.close()
        if stop <= 1:
            P.emit()
            return nc

        es3 = ExitStack()
        wz = sbt(es3, "wz", [128, 8, 1024], BF16)
        wg = sbt(es3, "wg", [128, 8, 2048], BF16)
        wpa = sbt(es3, "wpa", [128, 4, 1024], BF16)
        wpb = sbt(es3, "wpb", [128, 4, 1024], BF16)
        wo = sbt(es3, "wo", [128, 8, 1024], BF16)
        wzB, wgB, wpaB, wpbB, woB = Buf(), Buf(), Buf(), Buf(), Buf()
        with ExitStack() as s2:
            NBQ = 2048
            ctx3 = {"n": 0, "wn": 0}
            ctx3["wst"] = [(sbt(s2, "w3st%d" % i, [128, 1024], F32), Buf(), P.dsem("dw3st%d" % i)) for i in range(2)]
            ub = []
            for i in range(2):
                u = {"K": sbt(s2, "uK%d" % i, [128, S], BF16), "V": sbt(s2, "uV%d" % i, [128, NT_K, 128], BF16),
                     "Q": sbt(s2, "uQ%d" % i, [128, NBQ], BF16),
                     "Kb": [Buf() for _ in range(4)], "Vb": [Buf() for _ in range(4)], "Qb": Buf(),
                     "Kd": [P.dsem("dK%d_%d" % (i, j)) for j in range(4)],
                     "Vd": [P.dsem("dV%d_%d" % (i, j)) for j in range(4)], "Qd": P.dsem("dQ%d" % i)}
                ub.append(u)
            Sps = [(pst(s2, "Sps%d" % i, [128, 1024], F32), Buf()) for i in range(2)]
            acc = [(pst(s2, "acc%d" % i, [128, 512], F32), Buf()) for i in range(4)]
            Pt = [(sbt(s2, "Pt%d" % i, [128, 1024], BF16), Buf()) for i in range(4)]
            rinv = [(sbt(s2, "rinv%d" % i, [128, 512], F32), Buf()) for i in range(2)]
            tt = [(sbt(s2, "tt%d" % i, [128, 512], F32), Buf()) for i in range(2)]
            ob_ = (sbt(s2, "ob", [128, 512], F32), Buf())
            ocp = [(sbt(s2, "ocp%d" % i, [128, 512], F32), Buf()) for i in range(2)]
            smc = (sbt(s2, "smc", [128, 512], F32), Buf())
            smcA = smc
            rinvA = rinv[1]
            hl = [(sbt(s2, "hl%d" % i, [128, 512], BF16), Buf()) for i in range(2)]

            def split_hl(src, srcb):
                (h_, hb_), (l_, lb_) = hl
                P.do("vector", lambda e: e.tensor_copy(out=h_[:], in_=src[:]), reads=[srcb], writes=[hb_])
                P.do("vector", lambda e: e.tensor_tensor(out=l_[:], in0=src[:], in1=h_[:], op=ALU.subtract),
                     reads=[srcb, hb_], writes=[lb_])

            def mm_hl(out_ap, out_b, lhsT_ap):
                (h_, hb_), (l_, lb_) = hl
                P.do("tensor", lambda e: e.matmul(out_ap, lhsT=lhsT_ap, rhs=h_[:], start=True, stop=False),
                     reads=[hb_, cB], writes=[out_b])
                P.do("tensor", lambda e: e.matmul(out_ap, lhsT=lhsT_ap, rhs=l_[:], start=False, stop=True),
                     reads=[lb_, cB], writes=[out_b])
            sqo = (sbt(s2, "sqo", [128, 512], F32), Buf())
            rs = (sbt(s2, "rs", [128, 512], F32), Buf())
            ost = [(sbt(s2, "ost%d" % i, [128, 512], BF16), Buf(), P.dsem("dost%d" % i)) for i in range(2)]

            units = []
            for qi in range(3):
                slot = 0 if qi == 0 else 1
                for u in range(4):
                    units.append((qi, slot, "A", u))
                for h in range(4):
                    units.append((qi, slot, "B", h))

            def load_unit(n):
                qi, slot, kind, idx = units[n]
                u = ub[n % 2]
                kchunk = 4 if kind == "A" else idx
                qchunk = idx if kind == "A" else 4 + idx
                f0 = 0 if kind == "A" else 128 + idx * 128
                P.do("sync", lambda e: e.dma_start(out=u["Q"][:], in_=QT_d[qchunk * 128:(qchunk + 1) * 128, qi * NBQ:(qi + 1) * NBQ]),
                     reads=[QT_b], writes=[u["Qb"]], dsem=u["Qd"])
                for j in range(4):
                    P.do("sync", lambda e, j=j: e.dma_start(out=u["K"][:, j * 2048:(j + 1) * 2048],
                                                           in_=KT_d[slot][kchunk * 128:(kchunk + 1) * 128, j * 2048:(j + 1) * 2048]),
                         reads=[KT_b[slot]], writes=[u["Kb"][j]], dsem=u["Kd"][j])
                    P.do("sync", lambda e, j=j: e.dma_start(
                        out=u["V"][:, j * 16:(j + 1) * 16, :],
                        in_=VS_d[slot].rearrange("(kc p) f -> p kc f", p=128)[:, j * 16:(j + 1) * 16, f0:f0 + 128]),
                        reads=[VS_b[slot]], writes=[u["Vb"][j]], dsem=u["Vd"][j])

            steps = []
            for n in range(len(units)):
                for qb in range(4):
                    for kc in range(NT_K):
                        steps.append((n, qb, kc))
            nsteps = len(steps)
            cnt = {"ost": 0, "A": 0}
            pending = []
            wthunks = []

            def emit_qk(i):
                n, qb, kc = steps[i]
                u = ub[n % 2]
                Sp, Sb = Sps[i % 2]
                for r in range(2):
                    P.do("tensor", lambda e, r=r: e.matmul(
                        Sp[:, r * 512:(r + 1) * 512], lhsT=u["K"][r * 64:(r + 1) * 64, kc * 128:(kc + 1) * 128],
                        rhs=u["Q"][r * 64:(r + 1) * 64, qb * 512:(qb + 1) * 512], start=True, stop=True),
                        reads=[u["Kb"][kc // 16], u["Qb"]], writes=[Sb])

            def emit_exp(i):
                Sp, Sb = Sps[i % 2]
                pt, pb = Pt[i % 4]
                P.do("scalar", lambda e: e.activation(out=pt[:], in_=Sp[:], func=AF.Exp, scale=0.125),
                     reads=[Sb], writes=[pb])

            def emit_pv(i):
                n, qb, kc = steps[i]
                qi, slot, kind, idx = units[n]
                u = ub[n % 2]
                pt, pb = Pt[i % 4]
                first, last = (kc == 0), (kc == NT_K - 1)
                vb = u["Vb"][kc // 16]
                if kind == "A":
                    a0 = 2 * (qb % 2)
                    (ao, aob), (asm, asb) = acc[a0], acc[a0 + 1]
                    for r in range(2):
                        P.do("tensor", lambda e, r=r: e.matmul(
                            ao[r * 64:(r + 1) * 64, :], lhsT=u["V"][:, kc, r * 64:(r + 1) * 64], rhs=pt[:, r * 512:(r + 1) * 512],
                            start=first, stop=last, tile_position=(0, r * 64)), reads=[vb, pb], writes=[aob])
                    if kc % 2 == 1:
                        ptp, pbp = Pt[(i - 1) % 4]
                        pfirst, plast = (kc == 1), (kc == NT_K - 1)
                        for jq in range(4):
                            src, srcb = (ptp, pbp) if jq < 2 else (pt, pb)
                            mq = jq % 2
                            P.do("tensor", lambda e, jq=jq, src=src, mq=mq: e.matmul(
                                asm[32 * jq:32 * jq + 32, :], lhsT=ones_b[:, 0:32], rhs=src[:, mq * 512:(mq + 1) * 512],
                                start=pfirst, stop=plast, tile_position=(0, 32 * jq)), reads=[cB, srcb], writes=[asb])
                    if last:
                        smc_, smcb = smcA
                        P.do("vector", lambda e: e.tensor_copy(out=smc_[:], in_=asm[:]), reads=[asb], writes=[smcb])
                        split_hl(smc_, smcb)
                        r0 = idx * 128
                        t0 = qi * NBQ + qb * 512

                        def finA():
                            mm_hl(asm[:], asb, sel_b[:, 256:384])
                            ri, rib = rinvA
                            P.do("vector", lambda e: e.reciprocal(out=ri[:], in_=asm[:]), reads=[asb], writes=[rib])
                            os_, osb, osd = ost[cnt["ost"] % 2]
                            cnt["ost"] += 1
                            P.do("vector", lambda e: e.tensor_tensor(out=os_[:], in0=ao[:], in1=ri[:], op=ALU.mult),
                                 reads=[aob, rib], writes=[osb])
                            P.do("sync", lambda e: e.dma_start(out=OT_d[r0:r0 + 128, t0:t0 + 512], in_=os_[:]),
                                 reads=[osb], writes=[OT_b], dsem=osd)

                        nxt_is_b = (qb == 3 and n + 1 < len(units) and units[n + 1][2] == "B")
                        if nxt_is_b:
                            oc_, ocb_ = ocp[0]
                            P.do("vector", lambda e: e.tensor_copy(out=oc_[:], in_=ao[:]), reads=[aob], writes=[ocb_])

                            def finA_b():
                                mm_hl(asm[:], asb, sel_b[:, 256:384])
                                ri, rib = rinvA
                                P.do("vector", lambda e: e.reciprocal(out=ri[:], in_=asm[:]), reads=[asb], writes=[rib])
                                os_, osb, osd = ost[cnt["ost"] % 2]
                                cnt["ost"] += 1
                                P.do("vector", lambda e: e.tensor_tensor(out=os_[:], in0=oc_[:], in1=ri[:], op=ALU.mult),
                                     reads=[ocb_, rib], writes=[osb])
                                P.do("sync", lambda e: e.dma_start(out=OT_d[r0:r0 + 128, t0:t0 + 512], in_=os_[:]),
                                     reads=[osb], writes=[OT_b], dsem=osd)

                            pending.append((i + 5, finA_b))
                        else:
                            pending.append((i + 5, finA))
                else:
                    for m in range(2):
                        ao, aob = acc[m]
                        P.do("tensor", lambda e, m=m, ao=ao: e.matmul(ao[:], lhsT=u["V"][:, kc, :], rhs=pt[:, m * 512:(m + 1) * 512],
                                                                    start=first, stop=last), reads=[vb, pb], writes=[aob])
                    asm, asb = acc[2]
                    if kc % 2 == 1:
                        ptp, pbp = Pt[(i - 1) % 4]
                        pfirst, plast = (kc == 1), (kc == NT_K - 1)
                        for jq in range(4):
                            src, srcb = (ptp, pbp) if jq < 2 else (pt, pb)
                            mq = jq % 2
                            P.do("tensor", lambda e, jq=jq, src=src, mq=mq: e.matmul(
                                asm[32 * jq:32 * jq + 32, :], lhsT=ones_b[:, 0:32], rhs=src[:, mq * 512:(mq + 1) * 512],
                                start=pfirst, stop=plast, tile_position=(0, 32 * jq)), reads=[cB, srcb], writes=[asb])
                    if last:
                        smc_, smcb = smc
                        for m in range(2):
                            ao, aob = acc[m]
                            oc, ocb = ocp[m]
                            P.do("vector", lambda e, oc=oc, ao=ao: e.tensor_copy(out=oc[:], in_=ao[:]), reads=[aob], writes=[ocb])
                        P.do("vector", lambda e: e.tensor_copy(out=smc_[:], in_=asm[:]), reads=[asb], writes=[smcb])
                        split_hl(smc_, smcb)
                        r0 = 512 + idx * 128
                        t0 = qi * NBQ + qb * 512

                        def fin2m(m):
                            rp, rpb = acc[3]
                            mm_hl(rp[:], rpb, sel_b[:, m * 128:(m + 1) * 128])
                            ri, rib = rinv[m]
                            P.do("vector", lambda e: e.reciprocal(out=ri[:], in_=rp[:]), reads=[rpb], writes=[rib])
                            oc, ocb = ocp[m]
                            t_, tb_ = tt[m]
                            P.do("vector", lambda e: e.tensor_tensor(out=t_[:], in0=oc[:], in1=ri[:], op=ALU.mult),
                                 reads=[ocb, rib], writes=[tb_])

                        def fin2b():
                            o_, o_b = ob_
                            P.do("vector", lambda e: e.scalar_tensor_tensor(out=o_[:], in0=tt[1][0][:], scalar=neglam[:, 0:1], in1=tt[0][0][:],
                                                                            op0=ALU.mult, op1=ALU.add),
                                 reads=[tt[0][1], tt[1][1], cB], writes=[o_b])
                            sq_, sq_b = sqo
                            P.do("gpsimd", lambda e: e.tensor_tensor(out=sq_[:], in0=o_[:], in1=o_[:], op=ALU.mult),
                                 reads=[o_b], writes=[sq_b])
                            split_hl(sq_, sq_b)

                        def fin2c():
                            rp, rpb = acc[3]
                            mm_hl(rp[:], rpb, ones_b[:])
                            r_, r_b = rs
                            P.do("vector", lambda e: e.tensor_scalar(out=r_[:], in0=rp[:], scalar1=1.0 / 128.0, scalar2=EPS,
                                                                     op0=ALU.mult, op1=ALU.add), reads=[rpb], writes=[r_b])

                        def fin3a():
                            r_, r_b = rs
                            P.do("scalar", lambda e: e.activation(out=r_[:], in_=r_[:], func=AF.Ln), reads=[r_b], writes=[r_b])

                        def fin3():
                            o_, o_b = ob_
                            r_, r_b = rs
                            P.do("scalar", lambda e: e.activation(out=r_[:], in_=r_[:], func=AF.Exp, scale=-0.5), reads=[r_b], writes=[r_b])
                            os_, osb, osd = ost[cnt["ost"] % 2]
                            cnt["ost"] += 1
                            P.do("vector", lambda e: e.scalar_tensor_tensor(out=os_[:], in0=o_[:], scalar=sg8[:, 0:1], in1=r_[:],
                                                                            op0=ALU.mult, op1=ALU.mult),
                                 reads=[o_b, r_b, cB], writes=[osb])
                            P.do("sync", lambda e: e.dma_start(out=OT_d[r0:r0 + 128, t0:t0 + 512], in_=os_[:]),
                                 reads=[osb], writes=[OT_b], dsem=osd)

                        pending.append((i + 4, lambda: fin2m(0)))
                        pending.append((i + 7, lambda: fin2m(1)))
                        pending.append((i + 10, fin2b))
                        pending.append((i + 15, fin2c))
                        pending.append((i + 19, fin3a))
                        pending.append((i + 24, fin3))

            load_unit(0)
            for (w_, wB_, wd_, nch_, ncol_) in ((wz, wzB, wz_d, 8, 1024), (wg, wgB, wg_d, 8, 2048), (wpa, wpaB, wpa_d, 4, 1024),
                                                (wpb, wpbB, wpb_d, 4, 1024), (wo, woB, wout_d, 8, 1024)):
                load_weight(ctx3, w_, wB_, wd_, nch_, ncol_, engines=("gpsimd",), defer=wthunks)
            emit_qk(0)
            emit_qk(1)
            for i in range(nsteps):
                n, qb, kc = steps[i]
                if qb == 0 and kc == 0 and n + 1 < len(units):
                    load_unit(n + 1)
                emit_exp(i)
                if i + 2 < nsteps:
                    emit_qk(i + 2)
                emit_pv(i)
                if wthunks and i >= 4 and i % 3 == 0:
                    wthunks.pop(0)()
                while pending and pending[0][0] <= i:
                    pending.pop(0)[1]()
            while pending:
                pending.pop(0)[1]()
            while wthunks:
                wthunks.pop(0)()
            P.barrier()
        if stop <= 2:
            P.emit()
            return nc

        with ExitStack() as s3:
            NXT = 8
            xts = [(sbt(s3, "x3t%d" % i, [128, D], F32), Buf(), P.dsem("dx3t%d" % i)) for i in range(NXT)]
            sss = [(sbt(s3, "s3s%d" % i, [128, 1], F32), Buf()) for i in range(4)]
            junk3 = sbt(s3, "junk3", [128, D], BF16)
            junk3b = Buf()
            xns = [(sbt(s3, "x3n%d" % i, [128, D], BF16), Buf()) for i in range(4)]
            psTs = [(pst(s3, "ps3T%d" % i, [128, 8, 128], BF16), Buf()) for i in range(2)]
            hT4s = [(sbt(s3, "hT4_%d" % i, [128, 8, 512], BF16), [Buf(), Buf()]) for i in range(2)]
            On = sbt(s3, "On", [128, 8, 512], BF16)
            OnB, Ond = Buf(), P.dsem("dOn")
            og = sbt(s3, "og", [128, 8, 512], BF16)
            ogB = Buf()
            szt = [(sbt(s3, "szt%d" % i, [128, 512], BF16), Buf()) for i in range(2)]
            sgt_ = [(sbt(s3, "sgs%d" % i, [128, 512], F32), Buf()) for i in range(4)]
            m12 = [(sbt(s3, "m12_%d" % i, [128, 512], F32), Buf()) for i in range(2)]
            mg = sbt(s3, "mg", [128, 8, 512], BF16)
            mgB = Buf()
            ysb = [(sbt(s3, "ysb%d" % i, [128, D], F32), Buf(), P.dsem("dysb%d" % i)) for i in range(2)]
            pz = [(pst(s3, "pz%d" % i, [128, 512], F32), Buf()) for i in range(2)]
            pab = [(pst(s3, "pab%d" % i, [128, 512], F32), Buf()) for i in range(2)]
            pout = [(pst(s3, "pout%d" % i, [128, 512], F32), Buf()) for i in range(2)]
            NG = NQ // 512
            if os.environ.get("PH3_GROUPS"):
                NG = int(os.environ["PH3_GROUPS"])
            pzn = [0]

            def g_mslot(g):
                return 0 if g < 4 else 1

            def p3_load_x(g):
                for tl in range(4):
                    xt, xb_, xd = xts[(g * 4 + tl) % NXT]
                    r0 = g * 512 + tl * 128
                    P.do("sync", lambda e, xt=xt, r0=r0: e.dma_start(out=xt[:], in_=xq_d[r0:r0 + 128, :]), writes=[xb_], dsem=xd)

            def p3_load_on(g):
                t0 = g * 512
                P.do("sync", lambda e: e.dma_start(out=On[:], in_=OT_d.rearrange("(c p) t -> p c t", p=128)[:, :, t0:t0 + 512]),
                     reads=[OT_b], writes=[OnB], dsem=Ond)

            def p3_stats(g):
                for tl in range(4):
                    xt, xb_, xd = xts[(g * 4 + tl) % NXT]
                    col = g * 4 + tl
                    xn, xnb = xns[tl]
                    P.do("vector", lambda e, xn=xn, xt=xt, col=col: e.scalar_tensor_tensor(
                        out=xn[:], in0=xt[:], scalar=rstd_own[:, col:col + 1], in1=gbt[:], op0=ALU.mult, op1=ALU.mult),
                        reads=[xb_, cB], writes=[xnb])

            def p3_trans(g, tiles=(0, 1, 2, 3)):
                mslot = g_mslot(g)
                hT4, hT4B = hT4s[g % 2]
                for tl in tiles:
                    xn, xnb = xns[tl]
                    pT, pTb = psTs[tl % 2]
                    for c in range(8):
                        P.do("tensor", lambda e, c=c, pT=pT, xn=xn: e.transpose(out=pT[:, c, :], in_=xn[:, c * 128:(c + 1) * 128], identity=ident[:]),
                             reads=[xnb, cB], writes=[pTb])
                    for c in range(8):
                        dst = hT4[:, c, tl * 128:(tl + 1) * 128]
                        if tl % 2 == 0:
                            P.do("scalar", lambda e, c=c, pT=pT, dst=dst: e.activation(
                                out=dst, in_=pT[:, c, :], func=AF.Identity, scale=sc1[:, c, mslot:mslot + 1], bias=modT[:, c, mslot:mslot + 1]),
                                reads=[pTb, cB], writes=[hT4B[0]])
                        else:
                            P.do("vector", lambda e, c=c, pT=pT, dst=dst: e.tensor_scalar(
                                out=dst, in0=pT[:, c, :], scalar1=sc1[:, c, mslot:mslot + 1], scalar2=modT[:, c, mslot:mslot + 1],
                                op0=ALU.mult, op1=ALU.add), reads=[pTb, cB], writes=[hT4B[1]])

            def p3_body_a(g):
                hT4, hT4B = hT4s[g % 2]
                for j in range(8):
                    p_, pb_ = pz[pzn[0] % 2]
                    pzn[0] += 1
                    for c in range(8):
                        P.do("tensor", lambda e, p_=p_, j=j, c=c: e.matmul(p_[:], lhsT=wz[:, c, j * 128:(j + 1) * 128], rhs=hT4[:, c, :],
                                                                         start=(c == 0), stop=(c == 7)),
                             reads=[wzB] + hT4B, writes=[pb_])
                    sz, szb = szt[j % 2]
                    P.do("scalar", lambda e, p_=p_, sz=sz: e.activation(out=sz[:], in_=p_[:], func=AF.Silu), reads=[pb_], writes=[szb])
                    P.do("vector", lambda e, sz=sz, j=j: e.tensor_tensor(out=og[:, j, :], in0=sz[:], in1=On[:, j, :], op=ALU.mult),
                         reads=[szb, OnB], writes=[ogB])

            def p3_body_b(g, j):
                hT4, hT4B = hT4s[g % 2]
                sgl = []
                for br in range(2):
                    p_, pb_ = pz[pzn[0] % 2]
                    pzn[0] += 1
                    c0 = br * 1024 + j * 128
                    for c in range(8):
                        P.do("tensor", lambda e, p_=p_, c0=c0, c=c: e.matmul(p_[:], lhsT=wg[:, c, c0:c0 + 128], rhs=hT4[:, c, :],
                                                                           start=(c == 0), stop=(c == 7)),
                             reads=[wgB] + hT4B, writes=[pb_])
                    sg_, sgb_ = sgt_[(2 * j + br) % 4]
                    P.do("scalar", lambda e, p_=p_, sg_=sg_: e.activation(out=sg_[:], in_=p_[:], func=AF.Sigmoid),
                         reads=[pb_], writes=[sgb_])
                    sgl.append((sg_, sgb_))
                for br in range(2):
                    p_, pb_ = pab[br]
                    w_, wB_ = (wpa, wpaB) if br == 0 else (wpb, wpbB)
                    for kk in range(4):
                        P.do("tensor", lambda e, p_=p_, w_=w_, kk=kk, br=br: e.matmul(
                            p_[:], lhsT=w_[:, kk, j * 128:(j + 1) * 128], rhs=og[:, br * 4 + kk, :], start=(kk == 0), stop=(kk == 3)),
                            reads=[wB_, ogB], writes=[pb_])
                    m_, mb_ = m12[br]
                    sg_, sgb_ = sgl[br]
                    P.do("vector", lambda e, m_=m_, p_=p_, sg_=sg_: e.tensor_tensor(out=m_[:], in0=p_[:], in1=sg_[:], op=ALU.mult),
                         reads=[pb_, sgb_], writes=[mb_])
                P.do("gpsimd", lambda e: e.tensor_tensor(out=mg[:, j, :], in0=m12[0][0][:], in1=m12[1][0][:], op=ALU.add),
                     reads=[m12[0][1], m12[1][1]], writes=[mgB])

            def p3_body_c(g):
                mslot = g_mslot(g)
                for tl in range(4):
                    ys, ysB, ysd = ysb[tl % 2]
                    xt, xtb, _ = xts[(g * 4 + tl) % NXT]
                    for hb in range(2):
                        p_, pb_ = pout[hb]
                        for j in range(8):
                            P.do("tensor", lambda e, p_=p_, j=j, tl=tl, hb=hb: e.matmul(
                                p_[:], lhsT=mg[:, j, tl * 128:(tl + 1) * 128], rhs=wo[:, j, hb * 512:(hb + 1) * 512],
                                start=(j == 0), stop=(j == 7)), reads=[mgB, woB], writes=[pb_])
                        P.do("vector", lambda e, p_=p_, ys=ys, hb=hb: e.tensor_tensor(
                            out=ys[:, hb * 512:(hb + 1) * 512], in0=p_[:], in1=gate_bc[:, mslot, hb * 512:(hb + 1) * 512], op=ALU.mult),
                            reads=[pb_, cB], writes=[ysB])
                    P.do("gpsimd", lambda e, ys=ys, xt=xt: e.tensor_tensor(out=ys[:], in0=ys[:], in1=xt[:], op=ALU.add),
                         reads=[ysB, xtb], writes=[ysB])
                    r0 = g * 512 + tl * 128
                    P.do("sync", lambda e, ys=ys, r0=r0: e.dma_start(out=y_d[r0:r0 + 128, :], in_=ys[:]),
                         reads=[ysB], writes=[Y_b], dsem=ysd)

            p3_load_x(0)
            p3_load_on(0)
            p3_stats(0)
            p3_trans(0)
            for g in range(NG):
                if g + 1 < NG:
                    p3_load_x(g + 1)
                p3_body_a(g)
                if g + 1 < NG:
                    p3_load_on(g + 1)
                    p3_stats(g + 1)
                for j in range(8):
                    p3_body_b(g, j)
                    if 2 <= j <= 5 and g + 1 < NG:
                        p3_trans(g + 1, tiles=(j - 2,))
                p3_body_c(g)
            P.barrier()
        es3.close()
        P.emit()
    return nc


def _core_plan():
    plans = []
    for c in range(NCORES):
        us = [(g // 4, g % 4) for g in (3 * c, 3 * c + 1, 3 * c + 2)]
        seqs = [u[0] for u in us]
        if seqs[0] == seqs[1] == seqs[2]:
            order = [0, 1, 2]
        elif seqs[1] == seqs[2]:
            order = [0, 1, 2]
        else:
            order = [2, 0, 1]
        plans.append([us[i] for i in order])
    return plans


def _sel_const():
    m = np.zeros((128, 384), np.float32)
    m[0, 0:128] = 1.0
    m[64, 0:128] = 1.0
    m[32, 128:256] = 1.0
    m[96, 128:256] = 1.0
    m[0, 256:320] = 1.0
    m[64, 256:320] = 1.0
    m[32, 320:384] = 1.0
    m[96, 320:384] = 1.0
    return m


def _host_inputs(inputs):
    f = lambda a: np.ascontiguousarray(np.asarray(a), dtype=np.float32)
    xs = [f(inputs["x_prompt"])[b] for b in range(4)] + [f(inputs["x_sample"])[b] for b in range(2)]
    cs = [f(inputs["c_prompt"])[b] for b in range(4)] + [f(inputs["c_sample"])[b] for b in range(2)]
    w_in = f(inputs["w_in"])[0]
    w_ada = f(inputs["w_ada"])[0]
    b_ada = f(inputs["b_ada"])[0]
    o = np.cumsum([0, 512, 128, 128, 512, 512, 512, 512, 512, 1024, 1024])
    qa, ka, va, za, qb, kb, vb, zb, ga, gbm = [w_in[:, o[i]:o[i + 1]] for i in range(10)]
    perm = np.concatenate([np.concatenate([np.arange(u * 64, (u + 1) * 64), np.arange((4 + u) * 64, (5 + u) * 64)])
                           for u in range(4)])
    rep = lambda v, n=128: np.ascontiguousarray(np.broadcast_to(np.asarray(v, np.float32), (n,) + tuple(np.shape(v))))
    qn_a, kn_a, qn_b, kn_b = [f(inputs[k])[0] for k in ("qn_a", "kn_a", "qn_b", "kn_b")]
    shared = {
        "w_ada": w_ada,
        "b_adaT": np.ascontiguousarray(b_ada.reshape(24, 128).T),
        "b_row": b_ada.reshape(1, -1).copy(),
        "gb": rep(f(inputs["norm_g"])[0]),
        "gq": rep(np.stack([qn_a] * 8 + [qn_b] * 8)),
        "gk": rep(np.stack([kn_b] * 8 + [kn_a] * 2)),
        "lamv": rep(np.stack([f(inputs[k])[0] for k in ("lam_q1", "lam_k1", "lam_q2", "lam_k2")])),
        "sg": f(inputs["subln_g"])[0].reshape(128, 1).copy(),
        "jidx": rep(np.concatenate([np.float32(10000.0) ** (-np.arange(0, 32, 2, dtype=np.float32) / np.float32(32)),
                                    np.float32(500000.0) ** (-np.arange(0, 16, 2, dtype=np.float32) / np.float32(16))]).astype(np.float32)),
        "ident": np.eye(128, dtype=np.float32),
        "selc": _sel_const(),
        "w_kv": np.ascontiguousarray(np.concatenate([kb, ka, va, vb], axis=1)),
        "w_q": np.ascontiguousarray(np.concatenate([qa[:, perm], qb], axis=1)),
        "w_z": np.ascontiguousarray(np.concatenate([za[:, perm], zb], axis=1)),
        "w_g": np.ascontiguousarray(np.concatenate([ga, gbm], axis=1)),
        "w_pa": np.ascontiguousarray(f(inputs["w_proj_a"])[0][perm, :]),
        "w_pb": f(inputs["w_proj_b"])[0],
        "w_out": f(inputs["w_out"])[0],
    }
    plans = _core_plan()
    in_maps = []
    for c in range(NCORES):
        pl = plans[c]
        sa, sb = pl[0][0], pl[1][0]
        m = dict(shared)
        m["xa"] = xs[sa]
        m["xb"] = xs[sb]
        m["xq"] = np.ascontiguousarray(np.concatenate([xs[s][q * 2048:(q + 1) * 2048] for (s, q) in pl], axis=0))
        cT = np.stack([cs[sa], cs[sb]], axis=-1)
        m["cT"] = np.ascontiguousarray(cT.reshape(8, 128, 2).transpose(1, 0, 2))
        pos = np.zeros((128, NTAB), np.float32)
        pos[:, :NT_K] = (np.arange(NT_K)[None, :] * 128 + np.arange(128)[:, None])
        ownpos = np.concatenate([q * 2048 + np.arange(2048) for (s, q) in pl])
        pos[:, NT_K:] = ownpos.reshape(NT_Q, 128).T
        m["pos"] = pos
        in_maps.append(m)
    return in_maps, plans


_NC_CACHE = {}


def kernel(**inputs):
    in_maps, plans = _host_inputs(inputs)
    if "nc" not in _NC_CACHE:
        _NC_CACHE["nc"] = build_program()
    nc = _NC_CACHE["nc"]
    res = run_bass_kernel_spmd(nc, in_maps, core_ids=list(range(NCORES)))
    ys = [np.empty((S, D), np.float32) for _ in range(6)]
    for c in range(NCORES):
        y = np.asarray(res.results[c]["y"])
        for i, (s, q) in enumerate(plans[c]):
            ys[s][q * 2048:(q + 1) * 2048] = y[i * 2048:(i + 1) * 2048]
    return (np.stack(ys[:4]), np.stack(ys[4:]))
```
